# Optimizing a Trainium2 kernel written in Bass

```python
import jax, jax.numpy as jnp
from jax import lax
import numpy as np

D_MODEL = 1024
BATCH = 16
SEQ = 256
DEPTH = 4
DEC_BATCH = 4
DEC_SEQ = 1024
PAST_LEN = 256

GRID_W = 64
N_MIXERS = 3
N_MLA = (DEPTH + 2) // N_MIXERS
N_CONV = (DEPTH + 1) // N_MIXERS
N_SSD = DEPTH // N_MIXERS

MLA_HEADS = 16
QK_NOPE = 64
QK_ROPE = 32
QK_DIM = QK_NOPE + QK_ROPE
V_HEAD = 64
Q_LORA = 384
KV_LORA = 256
ROPE_AXIS = QK_ROPE // 2
ROPE_BASE = 10000.0
Q_BLOCK = 128

CONV_WIDTH = 31

SSD_INNER = 2 * D_MODEL
SSD_HEADDIM = 64
SSD_HEADS = SSD_INNER // SSD_HEADDIM
SSD_GROUPS = 4
SSD_STATE = 128
SSD_CONV = 5
SSD_CONV_DIM = SSD_INNER + 2 * SSD_GROUPS * SSD_STATE
SSD_IN_DIM = SSD_INNER + SSD_CONV_DIM + 2 * SSD_HEADS
CHUNK = 128

FFN_HIDDEN = ((8 * D_MODEL + 3 * 256 - 1) // (3 * 256)) * 256
EPS = 1e-6

kernel_name = 'hybrid_mla_conformer_ssd_diffusion_step'


def rmsnorm(x, g):
    x32 = x.astype(jnp.float32)
    y = x32 * lax.rsqrt(jnp.mean(x32 * x32, axis=-1, keepdims=True) + EPS)
    return (y * g.astype(jnp.float32)).astype(x.dtype)


def layernorm(x, g, b):
    x32 = x.astype(jnp.float32)
    xc = x32 - jnp.mean(x32, axis=-1, keepdims=True)
    y = xc * lax.rsqrt(jnp.mean(xc * xc, axis=-1, keepdims=True) + EPS)
    return (y * g.astype(jnp.float32) + b.astype(jnp.float32)).astype(x.dtype)


def adaln(cond, w, b):
    return jnp.split(jax.nn.silu(cond) @ w + b, 6, axis=-1)


def modulate(h, shift, scale):
    return h * (1 + scale) + shift


def swiglu(h, w_in, w_out):
    a, u = jnp.split(h @ w_in, 2, axis=-1)
    return (jax.nn.silu(a) * u) @ w_out


def axial_angles(n_tok):
    rows = n_tok // GRID_W
    row = jnp.repeat(jnp.arange(rows, dtype=jnp.float32), GRID_W)
    col = jnp.tile(jnp.arange(GRID_W, dtype=jnp.float32), rows)
    inv = ROPE_BASE ** (-jnp.arange(0, ROPE_AXIS, 2, dtype=jnp.float32) / ROPE_AXIS)
    return row[:, None] * inv, col[:, None] * inv


def rotate_half(x, ang):
    m = ang.shape[-1]
    cos = jnp.cos(ang)[:, None, :].astype(x.dtype)
    sin = jnp.sin(ang)[:, None, :].astype(x.dtype)
    x1, x2 = x[..., :m], x[..., m:]
    return jnp.concatenate([x1 * cos - x2 * sin, x2 * cos + x1 * sin], axis=-1)


def axial_rope(x, ang_r, ang_c):
    pe = x[..., QK_NOPE:]
    return jnp.concatenate([x[..., :QK_NOPE], rotate_half(pe[..., :ROPE_AXIS], ang_r),
                            rotate_half(pe[..., ROPE_AXIS:], ang_c)], axis=-1)


def attention(q, k, v):
    b, tq, h, dk = q.shape
    scale = dk ** -0.5
    qb = q.reshape(b, tq // Q_BLOCK, Q_BLOCK, h, dk).transpose(1, 0, 2, 3, 4)

    def block(qi):
        s = jnp.einsum('bqhd,bkhd->bhqk', qi, k).astype(jnp.float32) * scale
        p = jax.nn.softmax(s, axis=-1).astype(v.dtype)
        return jnp.einsum('bhqk,bkhd->bqhd', p, v)

    o = lax.map(block, qb)
    return o.transpose(1, 0, 2, 3, 4).reshape(b, tq, h, v.shape[-1])


def mla_queries(h, w_dq, g_q, w_uq, g_qn):
    b, t, _ = h.shape
    q = (rmsnorm(h @ w_dq, g_q) @ w_uq).reshape(b, t, MLA_HEADS, QK_DIM)
    return rmsnorm(q, g_qn)


def mla_kv_latent(h, w_dkv, g_kv):
    kv = h @ w_dkv
    return rmsnorm(kv[..., :KV_LORA], g_kv), kv[..., KV_LORA:]


def mla_keys_values(ckv, kpe, w_ukv, g_kn):
    b, s, _ = ckv.shape
    kv = (ckv @ w_ukv).reshape(b, s, MLA_HEADS, QK_NOPE + V_HEAD)
    k_pe = jnp.broadcast_to(kpe[:, :, None, :], (b, s, MLA_HEADS, QK_ROPE))
    k = jnp.concatenate([kv[..., :QK_NOPE], k_pe], axis=-1)
    return rmsnorm(k, g_kn), kv[..., QK_NOPE:]


def mla_output(q, k, v, w_o):
    b, t = q.shape[:2]
    return attention(q, k, v).reshape(b, t, MLA_HEADS * V_HEAD) @ w_o


def mla_context(h, w_dq, g_q, w_uq, w_dkv, g_kv, w_ukv, g_qn, g_kn, w_o):
    q = mla_queries(h, w_dq, g_q, w_uq, g_qn)
    ckv, kpe = mla_kv_latent(h, w_dkv, g_kv)
    k, v = mla_keys_values(ckv, kpe, w_ukv, g_kn)
    return mla_output(q, k, v, w_o), ckv, kpe


def mla_latent(h, ckv_ctx, kpe_ctx, ang_r, ang_c, w_dq, g_q, w_uq, w_dkv, g_kv, w_ukv, g_qn, g_kn, w_o):
    q = axial_rope(mla_queries(h, w_dq, g_q, w_uq, g_qn), ang_r, ang_c)
    ckv, kpe = mla_kv_latent(h, w_dkv, g_kv)
    k_l, v_l = mla_keys_values(ckv, kpe, w_ukv, g_kn)
    k_l = axial_rope(k_l, ang_r, ang_c)
    k_c, v_c = mla_keys_values(ckv_ctx, kpe_ctx, w_ukv, g_kn)
    k = jnp.concatenate([k_c, k_l], axis=1)
    v = jnp.concatenate([v_c, v_l], axis=1)
    return mla_output(q, k, v, w_o)


def depthwise_conv(x, w, b):
    pad = (w.shape[0] - 1) // 2
    y = lax.conv_general_dilated(x, w[:, None, :], window_strides=(1,), padding=[(pad, pad)],
                                 dimension_numbers=('NWC', 'WIO', 'NWC'),
                                 feature_group_count=x.shape[-1])
    return y + b


def conv_module(h, w_pw1, b_pw1, w_dw, b_dw, g_ln, b_ln, w_pw2, b_pw2):
    a, g = jnp.split(h @ w_pw1 + b_pw1, 2, axis=-1)
    u = a * jax.nn.sigmoid(g)
    u = jax.nn.silu(layernorm(depthwise_conv(u, w_dw, b_dw), g_ln, b_ln))
    return u @ w_pw2 + b_pw2


def ssd_scan(x, dt, a, bm, cm, d_skip, h0):
    f32 = jnp.float32
    b, L, H, P = x.shape
    G, N = bm.shape[-2:]
    R = H // G
    nc = L // CHUNK
    xr = x.astype(f32).reshape(b, nc, CHUNK, G, R, P)
    dtr = dt.reshape(b, nc, CHUNK, G, R)
    br = bm.astype(f32).reshape(b, nc, CHUNK, G, N)
    cr = cm.astype(f32).reshape(b, nc, CHUNK, G, N)
    a_cum = jnp.cumsum(dtr * a.reshape(G, R), axis=2)
    xdt = xr * dtr[..., None]
    ac = jnp.moveaxis(a_cum, 2, -1)
    seg = ac[..., :, None] - ac[..., None, :]
    lower = jnp.tril(jnp.ones((CHUNK, CHUNK), dtype=bool))
    lmat = jnp.exp(jnp.where(lower, seg, -jnp.inf))
    cb = jnp.einsum('bcign,bcjgn->bcgij', cr, br)
    y_diag = jnp.einsum('bcgrij,bcjgrp->bcigrp', cb[:, :, :, None] * lmat, xdt)
    decay_in = jnp.exp(a_cum[:, :, -1:] - a_cum)
    states = jnp.einsum('bcjgn,bcjgrp->bcgrpn', br, xdt * decay_in[..., None])
    chunk_decay = jnp.exp(a_cum[:, :, -1])

    def step(hc, inp):
        st, dec = inp
        return hc * dec[..., None, None] + st, hc

    h_fin, h_prev = lax.scan(step, h0.astype(f32).reshape(b, G, R, P, N),
                             (jnp.moveaxis(states, 1, 0), jnp.moveaxis(chunk_decay, 1, 0)))
    h_prev = jnp.moveaxis(h_prev, 0, 1)
    y_off = jnp.einsum('bcign,bcgrpn->bcigrp', cr, h_prev) * jnp.exp(a_cum)[..., None]
    y = y_diag + y_off + xr * d_skip.astype(f32).reshape(G, R)[..., None]
    return y.reshape(b, L, H, P).astype(x.dtype), h_fin.reshape(b, H, P, N).astype(x.dtype)


def ssd_mixer(h, h0, w_in, w_conv, b_conv, dt_bias, a_log, d_skip, g_norm, w_out):
    b, t, _ = h.shape
    zxbcdt = h @ w_in
    z = zxbcdt[..., :SSD_INNER]
    xbc = jax.nn.silu(depthwise_conv(zxbcdt[..., SSD_INNER:SSD_INNER + SSD_CONV_DIM], w_conv, b_conv))
    dt_raw = zxbcdt[..., SSD_INNER + SSD_CONV_DIM:].reshape(b, t, 2, SSD_HEADS).astype(jnp.float32)
    gn = SSD_GROUPS * SSD_STATE
    x = xbc[..., :SSD_INNER].reshape(b, t, SSD_HEADS, SSD_HEADDIM)
    bm = xbc[..., SSD_INNER:SSD_INNER + gn].reshape(b, t, SSD_GROUPS, SSD_STATE)
    cm = xbc[..., SSD_INNER + gn:].reshape(b, t, SSD_GROUPS, SSD_STATE)
    dt = jax.nn.softplus(dt_raw + dt_bias.astype(jnp.float32))
    a = -jnp.exp(a_log.astype(jnp.float32))
    flip = lambda u: jnp.flip(u, axis=1)
    y_f, s_f = ssd_scan(x, dt[:, :, 0], a[0], bm, cm, d_skip[0], h0[:, 0])
    y_b, s_b = ssd_scan(flip(x), flip(dt[:, :, 1]), a[1], flip(bm), flip(cm), d_skip[1], h0[:, 1])
    y = (y_f + flip(y_b)).reshape(b, t, SSD_INNER)
    yz = (y * jax.nn.silu(z)).reshape(b, t, SSD_GROUPS, SSD_INNER // SSD_GROUPS)
    y = rmsnorm(yz, g_norm.reshape(SSD_GROUPS, SSD_INNER // SSD_GROUPS)).reshape(b, t, SSD_INNER)
    return y @ w_out, jnp.stack([s_f, s_b], axis=1)


def setup_inputs(seed: int = 0) -> dict:
    key = jax.random.key(seed)
    ks = iter(jax.random.split(key, 48))
    f32 = jnp.float32
    D = D_MODEL

    def nrm(shape, scale=1.0):
        return jax.random.normal(next(ks), shape, f32) * scale

    def gain(shape):
        return 1.0 + nrm(shape, 0.05)

    dt0 = jnp.exp(jax.random.uniform(next(ks), (N_SSD, 2, SSD_HEADS), f32,
                                     minval=np.log(0.001), maxval=np.log(0.1)))
    return {
        'x_prompt': nrm((BATCH, SEQ, D)),
        'x_sample': nrm((DEC_BATCH, DEC_SEQ, D)),
        'cache_ckv': nrm((DEC_BATCH, N_MLA, PAST_LEN, KV_LORA)),
        'cache_kpe': nrm((DEC_BATCH, N_MLA, PAST_LEN, QK_ROPE)),
        'state_ssm': nrm((DEC_BATCH, N_SSD, 2, SSD_HEADS, SSD_HEADDIM, SSD_STATE), 0.1),
        'c': nrm((DEC_BATCH, D)),
        'c_ctx': nrm((D,)),
        'w_ada': nrm((DEPTH, D, 6 * D), 0.5 * D ** -0.5),
        'b_ada': nrm((DEPTH, 6 * D), 0.02),
        'g_norm1': gain((DEPTH, D)),
        'g_norm2': gain((DEPTH, D)),
        'mla_w_dq': nrm((N_MLA, D, Q_LORA), D ** -0.5),
        'mla_g_q': gain((N_MLA, Q_LORA)),
        'mla_w_uq': nrm((N_MLA, Q_LORA, MLA_HEADS * QK_DIM), Q_LORA ** -0.5),
        'mla_w_dkv': nrm((N_MLA, D, KV_LORA + QK_ROPE), D ** -0.5),
        'mla_g_kv': gain((N_MLA, KV_LORA)),
        'mla_w_ukv': nrm((N_MLA, KV_LORA, MLA_HEADS * (QK_NOPE + V_HEAD)), KV_LORA ** -0.5),
        'mla_g_qn': gain((N_MLA, QK_DIM)),
        'mla_g_kn': gain((N_MLA, QK_DIM)),
        'mla_w_o': nrm((N_MLA, MLA_HEADS * V_HEAD, D), (MLA_HEADS * V_HEAD) ** -0.5),
        'cv_w_pw1': nrm((N_CONV, D, 2 * D), D ** -0.5),
        'cv_b_pw1': nrm((N_CONV, 2 * D), 0.02),
        'cv_w_dw': nrm((N_CONV, CONV_WIDTH, D), CONV_WIDTH ** -0.5),
        'cv_b_dw': nrm((N_CONV, D), 0.02),
        'cv_g_ln': gain((N_CONV, D)),
        'cv_b_ln': nrm((N_CONV, D), 0.02),
        'cv_w_pw2': nrm((N_CONV, D, D), D ** -0.5),
        'cv_b_pw2': nrm((N_CONV, D), 0.02),
        'ssd_w_in': nrm((N_SSD, D, SSD_IN_DIM), D ** -0.5),
        'ssd_w_conv': nrm((N_SSD, SSD_CONV, SSD_CONV_DIM), SSD_CONV ** -0.5),
        'ssd_b_conv': nrm((N_SSD, SSD_CONV_DIM), 0.02),
        'ssd_dt_bias': dt0 + jnp.log(-jnp.expm1(-dt0)),
        'ssd_a_log': jnp.log(jax.random.uniform(next(ks), (N_SSD, 2, SSD_HEADS), f32, minval=1.0, maxval=16.0)),
        'ssd_d': gain((N_SSD, 2, SSD_HEADS)),
        'ssd_g_norm': gain((N_SSD, SSD_INNER)),
        'ssd_w_out': nrm((N_SSD, SSD_INNER, D), SSD_INNER ** -0.5),
        'ffn_w_in': nrm((DEPTH, D, 2 * FFN_HIDDEN), D ** -0.5),
        'ffn_w_out': nrm((DEPTH, FFN_HIDDEN, D), FFN_HIDDEN ** -0.5),
    }


def reference(x_prompt, x_sample, cache_ckv, cache_kpe, state_ssm, c, c_ctx,
              w_ada, b_ada, g_norm1, g_norm2,
              mla_w_dq, mla_g_q, mla_w_uq, mla_w_dkv, mla_g_kv, mla_w_ukv, mla_g_qn, mla_g_kn, mla_w_o,
              cv_w_pw1, cv_b_pw1, cv_w_dw, cv_b_dw, cv_g_ln, cv_b_ln, cv_w_pw2, cv_b_pw2,
              ssd_w_in, ssd_w_conv, ssd_b_conv, ssd_dt_bias, ssd_a_log, ssd_d, ssd_g_norm, ssd_w_out,
              ffn_w_in, ffn_w_out):
    ang_r, ang_c = axial_angles(x_sample.shape[1])
    cond_ctx = c_ctx[None, None, :]
    cond_lat = c[:, None, :]
    xp, xs = x_prompt, x_sample
    ckv_list, kpe_list, ssm_list = [], [], []
    for i in range(DEPTH):
        kind, j = i % N_MIXERS, i // N_MIXERS
        p_sh1, p_sc1, p_g1, p_sh2, p_sc2, p_g2 = adaln(cond_ctx, w_ada[i], b_ada[i])
        s_sh1, s_sc1, s_g1, s_sh2, s_sc2, s_g2 = adaln(cond_lat, w_ada[i], b_ada[i])
        hp = modulate(rmsnorm(xp, g_norm1[i]), p_sh1, p_sc1)
        hs = modulate(rmsnorm(xs, g_norm1[i]), s_sh1, s_sc1)
        if kind == 0:
            mw = (mla_w_dq[j], mla_g_q[j], mla_w_uq[j], mla_w_dkv[j], mla_g_kv[j],
                  mla_w_ukv[j], mla_g_qn[j], mla_g_kn[j], mla_w_o[j])
            op, ckv, kpe = mla_context(hp, *mw)
            os_ = mla_latent(hs, cache_ckv[:, j], cache_kpe[:, j], ang_r, ang_c, *mw)
            ckv_list.append(ckv)
            kpe_list.append(kpe)
        elif kind == 1:
            cw = (cv_w_pw1[j], cv_b_pw1[j], cv_w_dw[j], cv_b_dw[j], cv_g_ln[j], cv_b_ln[j],
                  cv_w_pw2[j], cv_b_pw2[j])
            op = conv_module(hp, *cw)
            os_ = conv_module(hs, *cw)
        else:
            sw = (ssd_w_in[j], ssd_w_conv[j], ssd_b_conv[j], ssd_dt_bias[j], ssd_a_log[j],
                  ssd_d[j], ssd_g_norm[j], ssd_w_out[j])
            h0 = jnp.zeros((xp.shape[0], 2, SSD_HEADS, SSD_HEADDIM, SSD_STATE), xp.dtype)
            op, st = ssd_mixer(hp, h0, *sw)
            os_, _ = ssd_mixer(hs, state_ssm[:, j], *sw)
            ssm_list.append(st)
        xp = xp + p_g1 * op
        xs = xs + s_g1 * os_
        hp = modulate(rmsnorm(xp, g_norm2[i]), p_sh2, p_sc2)
        hs = modulate(rmsnorm(xs, g_norm2[i]), s_sh2, s_sc2)
        xp = xp + p_g2 * swiglu(hp, ffn_w_in[i], ffn_w_out[i])
        xs = xs + s_g2 * swiglu(hs, ffn_w_in[i], ffn_w_out[i])
    new_ckv = jnp.stack(ckv_list, axis=1)
    new_kpe = jnp.stack(kpe_list, axis=1)
    new_ssm = jnp.stack(ssm_list, axis=1)
    return (xp, xs, new_ckv, new_kpe, new_ssm)
```

```python
import numpy as np
import concourse.bass as bass
import concourse.mybir as mybir
from concourse.bass_utils import run_bass_kernel_spmd
from contextlib import ExitStack

F32 = mybir.dt.float32
BF16 = mybir.dt.bfloat16
AF = mybir.ActivationFunctionType
ALU = mybir.AluOpType
AX = mybir.AxisListType

ENGS = ['pe', 'act', 'dve', 'pool', 'sp']


class Buf:
    def __init__(self, t, name):
        self.t = t
        self.name = name
        self.writer = None
        self.readers = {}
        self.semkey = None
        self.dcount = 0
        self.is_psum = False

    def __getitem__(self, idx):
        return self.t[idx]


class Prog:
    def __init__(self, nc, stack, self_wait=True):
        self.nc = nc
        self.stack = stack
        self.ops = {e: [] for e in ENGS}
        self.count = {e: 0 for e in ENGS}
        self.semh = {}
        self.obs = {e: {} for e in ENGS}
        self.self_wait = self_wait
        for e in ENGS:
            self.semh[e] = stack.enter_context(nc.semaphore("sem_" + e))
        self.nbuf = 0
        self.out_tokens = []
        self.dma_final = {}
        self.used_names = set()
        self.eng = {'pe': nc.tensor, 'act': nc.scalar, 'dve': nc.vector, 'pool': nc.gpsimd, 'sp': nc.sync}

    def _emit(self, eng, waits, fn, inc):
        e = self.eng[eng]
        for (k, v) in waits:
            e.wait_ge(self.semh[k], v)
        if fn is not None:
            ins = fn(e)
            ins.then_inc(self.semh[inc[0]], inc[1])

    def sbuf(self, shape, dtype, name=None, stack=None):
        self.nbuf += 1
        name = name or f"sb{self.nbuf}"
        if name in self.used_names:
            name = f"{name}_u{self.nbuf}"
        self.used_names.add(name)
        t = (stack or self.stack).enter_context(self.nc.sbuf_tensor(name, list(shape), dtype))
        return Buf(t, name)

    def psum(self, shape, dtype=F32, name=None, stack=None):
        self.nbuf += 1
        name = name or f"ps{self.nbuf}"
        t = (stack or self.stack).enter_context(self.nc.psum_tensor(name, list(shape), dtype))
        b = Buf(t, name)
        b.is_psum = True
        return b

    def _dsem(self, b):
        if b.semkey is None:
            b.semkey = "d_" + b.name
            self.semh[b.semkey] = self.stack.enter_context(self.nc.semaphore("dsem_" + b.name))
        return b.semkey

    def _deps(self, eng, reads, writes):
        deps = set()
        for b in reads:
            if b.writer is not None:
                deps.add(b.writer)
            if b.is_psum:
                for rk, rt in b.readers.items():
                    if rk != eng:
                        deps.add(rt)
        for b in writes:
            if b.writer is not None:
                deps.add(b.writer)
            deps.update(b.readers.values())
        waits = []
        for (k, v) in sorted(deps, key=lambda kv: (str(kv[0]), kv[1])):
            if k == eng and (eng == 'pe' or not self.self_wait):
                continue
            if self.obs[eng].get(k, 0) < v:
                waits.append((k, v))
                self.obs[eng][k] = v
        return waits

    def op(self, eng, fn, reads=(), writes=()):
        waits = self._deps(eng, reads, writes)
        self.count[eng] += 1
        tok = (eng, self.count[eng])
        for b in reads:
            b.readers[eng] = tok
        for b in writes:
            b.writer = tok
            b.readers = {}
        self._emit(eng, waits, fn, (eng, 1))

    def dma(self, q, out_ap, in_ap, reads=(), writes=(), sembuf=None, is_output=False, **kw):
        waits = self._deps(q, reads, writes)
        k = self._dsem(sembuf)
        sembuf.dcount += 16
        tok = (k, sembuf.dcount)
        self.dma_final[k] = sembuf.dcount
        for b in reads:
            b.readers['dma_' + k] = tok
        for b in writes:
            b.writer = tok
            b.readers = {}
        if is_output:
            self.out_tokens.append(tok)
        fn = lambda e, o=out_ap, i=in_ap, kw=kw: e.dma_start(out=o, in_=i, **kw)
        self._emit(q, waits, fn, (k, 16))

    def barrier(self):
        for e in ENGS:
            waits = []
            for o in ENGS:
                if o == e or self.count[o] == 0:
                    continue
                if self.obs[e].get(o, 0) < self.count[o]:
                    waits.append((o, self.count[o]))
                    self.obs[e][o] = self.count[o]
            for k, v in self.dma_final.items():
                if self.obs[e].get(k, 0) < v:
                    waits.append((k, v))
                    self.obs[e][k] = v
            if waits:
                self._emit(e, waits, None, None)

    def finish(self):
        waits = []
        seen = {}
        for k, v in self.dma_final.items():
            seen[k] = max(seen.get(k, 0), v)
        for k, v in seen.items():
            waits.append((k, v))
        for o in ENGS:
            if o != 'sp' and self.count[o] > 0:
                waits.append((o, self.count[o]))
        self._emit('sp', waits, None, None)

    def emit(self):
        return
        nc = self.nc
        with nc.Block() as block:
            def run(eng_name, e):
                for (waits, fn, inc) in self.ops[eng_name]:
                    for (k, v) in waits:
                        e.wait_ge(self.semh[k], v)
                    if fn is not None:
                        ins = fn(e)
                        ins.then_inc(self.semh[inc[0]], inc[1])

            @block.tensor
            def _(e):
                run('pe', e)

            @block.scalar
            def _(e):
                run('act', e)

            @block.vector
            def _(e):
                run('dve', e)

            @block.gpsimd
            def _(e):
                run('pool', e)

            @block.sync
            def _(e):
                run('sp', e)


D_MODEL = 1024
NT = 1024
EPS = 1e-6
FFN_H = 2816
NEG = -30000.0

VEC_LAYOUT = []
for _i in range(4):
    VEC_LAYOUT += [(f"g1_{_i}", 8), (f"g2_{_i}", 8), (f"bada_{_i}", 48)]
VEC_LAYOUT += [("cond", 8)]
for _j in range(2):
    VEC_LAYOUT += [(f"gq_{_j}", 3), (f"gkv_{_j}", 2), (f"gqn_{_j}", 1), (f"gqnsw_{_j}", 1), (f"gkn_{_j}", 1), (f"gknsw_{_j}", 1)]
VEC_LAYOUT += [("bpw1", 16), ("wdw", 248), ("bdw", 8), ("gln", 8), ("bln", 8), ("bpw2", 8)]
VEC_LAYOUT += [("wconv", 120), ("bconv", 24)]
VEC_BASE = {}
_o = 0
for _n, _r in VEC_LAYOUT:
    VEC_BASE[_n] = _o
    _o += _r
NVEC = ((_o + 127) // 128) * 128


def build(cfg=None):
    cfg = cfg or {}
    en_mla = cfg.get("mla", True)
    en_conv = cfg.get("conv", True)
    en_ssd = cfg.get("ssd", True)
    en_ffn = cfg.get("ffn", True)
    nlayers = cfg.get("nlayers", 4)

    nc = bass.Bass("TRN2", target_bir_lowering=False)


    IN_SHAPES = {
        "x": [NT, 1024], "vecs": [NVEC, 128], "w_ada": [4, 1024, 6144],
        "ffn_w_in": [4, 1024, 5632], "ffn_w_out": [4, 2816, 1024],
        "mla_w_dq": [2, 1024, 384], "w_uq": [2, 384, 1536], "w_uq_sw": [2, 384, 1536],
        "mla_w_dkv": [2, 1024, 288], "wk": [2, 384, 1536], "wk_sw": [2, 384, 1536],
        "w_ukv_v": [2, 256, 1024], "mla_w_o": [2, 1024, 1024],
        "cache_ckv": [2, 256, 256], "cache_kpe": [2, 256, 32],
        "ropeq": [2, 96, 1024], "ropek": [2, 96, 1280], "maskb": [128, 40], "maskq": [8, 1024], "maskk": [8, 1280],
        "cv_w_pw1": [1024, 2048], "cv_w_pw2": [1024, 1024],
        "ssd_w_in": [1024, 5184], "ssd_w_out": [2048, 1024],
        "h0": [2, 32, 64, 128], "ssd_rows": [4, 64], "bconv_row": [1, 3072], "gnorm_row": [1, 2048],
        "flag": [128, 1], "tri": [3, 128, 128], "negmask": [2, 128, 128],
    }

    class _LazyD(dict):
        def __missing__(self, name):
            if name in OUT_SHAPES:
                ap = nc.dram_tensor(name, list(OUT_SHAPES[name]), F32, kind="ExternalOutput").ap()
                self[name] = ap
                return ap
            ap = nc.dram_tensor(name, list(IN_SHAPES[name]), F32, kind="ExternalInput").ap()
            self[name] = ap
            USED_INPUTS.append(name)
            return ap

    USED_INPUTS = []
    OUT_SHAPES = {"y": [NT, 1024], "ockv": [2, NT, 256], "okpe": [2, NT, 32], "ossm": [4, 2, 32, 64, 128]}
    D = _LazyD()
    with ExitStack() as st:
        P = Prog(nc, st)

        def mm(out, lhsT, rhs, start, stop, R, W):
            P.op('pe', lambda e: e.matmul(out, lhsT=lhsT, rhs=rhs, start=start, stop=stop), R, W)

        def tr(out, in_, ident, R, W):
            P.op('pe', lambda e: e.transpose(out=out, in_=in_, identity=ident), R, W)

        def act(out, in_, func, R, W, bias=None, scale=None, accum=None):
            kw = {}
            if bias is not None:
                kw['bias'] = bias
            if scale is not None:
                kw['scale'] = scale
            if accum is not None:
                kw['accum_out'] = accum
            P.op('act', lambda e: e.activation(out=out, in_=in_, func=func, **kw), R, W)

        def tt(eng, out, in0, in1, op, R, W):
            P.op(eng, lambda e: e.tensor_tensor(out=out, in0=in0, in1=in1, op=op), R, W)

        def ts(eng, out, in0, s1, s2, op0, op1, R, W):
            if op1 is None:
                P.op(eng, lambda e: e.tensor_scalar(out=out, in0=in0, scalar1=s1, scalar2=None, op0=op0), R, W)
            else:
                P.op(eng, lambda e: e.tensor_scalar(out=out, in0=in0, scalar1=s1, scalar2=s2, op0=op0, op1=op1), R, W)

        def stt(out, in0, scalar, in1, op0, op1, R, W):
            P.op('dve', lambda e: e.scalar_tensor_tensor(out=out, in0=in0, scalar=scalar, in1=in1, op0=op0, op1=op1), R, W)

        def cp(eng, out, in_, R, W):
            if eng == 'act':
                P.op('act', lambda e: e.copy(out=out, in_=in_), R, W)
            else:
                P.op(eng, lambda e: e.tensor_copy(out=out, in_=in_), R, W)

        def memset(eng, ap, val, W):
            P.op(eng, lambda e: e.memset(ap, val), (), W)

        _rr = {'n': 0}

        def alt(*engs):
            _rr['n'] += 1
            return engs[_rr['n'] % len(engs)]

        NPB = cfg.get("npsum", 8)
        PS = [P.psum([128, 512], F32, f"psb{i}") for i in range(NPB)]
        _ps = {'n': 0}

        def nps():
            _ps['n'] += 1
            return PS[_ps['n'] % (NPB - 3)]

        _pa = {'n': 0}

        def nacc():
            _pa['n'] += 1
            return PS[NPB - 2 + _pa['n'] % 2]

        xT = [P.sbuf([128, NT], F32, f"xT{i}") for i in range(8)]
        hT = [P.sbuf([128, NT], BF16, f"hT{i}") for i in range(8)]
        ident_f = P.sbuf([128, 128], F32, "ident_f")
        ident_b = P.sbuf([128, 128], BF16, "ident_b")
        ones_f = P.sbuf([128, 128], F32, "ones_f")
        ones_b = P.sbuf([128, 128], BF16, "ones_b")
        vcols = P.sbuf([128, NVEC], F32, "vcols")
        modb = [P.sbuf([128, 64], F32, f"modb{i}") for i in range(4)]
        s_bf = P.sbuf([128, 8], BF16, "s_bf")
        flag = P.sbuf([128, 1], F32, "flag_sb")
        NSLOT = 4
        SLOTW = 4096
        slots = [P.sbuf([128, SLOTW], BF16, f"wslot{i}") for i in range(NSLOT)]
        _sl = {'n': 0}

        def wload(dram_aps):
            _sl['n'] += 1
            s = slots[_sl['n'] % NSLOT]
            for (ap, off, shape) in dram_aps:
                n = 1
                for d in shape[1:]:
                    n *= d
                dst = s[:, off:off + n]
                if len(shape) == 3:
                    dst = dst.rearrange("p (a b) -> p a b", a=shape[1])
                P.dma('pool', dst, ap, writes=[s], sembuf=s)
            return s

        def vc(name, idx=0, n=1):
            b = VEC_BASE[name] + idx
            return vcols[:, b:b + n]

        sq_r = [P.sbuf([128, 512], BF16, f"sq_r{i}") for i in range(4)]
        rstd_t = P.sbuf([128, 512], F32, "rstd_t")
        tmp_r = [P.sbuf([128, 512], F32, f"tmp_r{i}") for i in range(3)]
        _tm = {'n': 0}

        def ntmp():
            _tm['n'] += 1
            return tmp_r[_tm['n'] % 3]

        memset('pool', ident_f[:, :], 1.0, [ident_f])
        P.op('pool', lambda e: e.affine_select(out=ident_f[:, :], in_=ident_f[:, :], pattern=[[-1, 128]],
                                               compare_op=ALU.is_equal, fill=0.0, base=0, channel_multiplier=1),
             [ident_f], [ident_f])
        cp('pool', ident_b[:, :], ident_f[:, :], [ident_f], [ident_b])
        memset('pool', ones_f[:, :], 1.0, [ones_f])
        memset('pool', ones_b[:, :], 1.0, [ones_b])
        P.dma('sp', flag[:, :], D["flag"], writes=[flag], sembuf=flag)

        s0 = ExitStack()
        xstage = [P.sbuf([128, 1024], F32, f"xstage{i}", stack=s0) for i in range(2)]
        for blk in range(NVEC // 128 if cfg.get('stop', 9) > 1 else 0):
            stg = xstage[blk % 2]
            P.dma('sp', stg[:, 0:128], D["vecs"][blk * 128:(blk + 1) * 128, :], writes=[stg], sembuf=stg)
            p = nps()
            tr(p[:, 0:128], stg[:, 0:128], ident_f[:, :], [stg, ident_f], [p])
            cp(alt('dve', 'act'), vcols[:, blk * 128:(blk + 1) * 128], p[:, 0:128], [p], [vcols])
        if cfg.get('stop', 9) > 2:
            act(s_bf[:, :], vc("cond", 0, 8), AF.Silu, [vcols], [s_bf])

        for t in range(cfg.get('nx', 8) if cfg.get('stop', 9) > 3 else 0):
            stg = xstage[t % 2]
            if cfg.get('xsplit', 1) == 1:
                P.dma('sp', stg[:, :], D["x"][t * 128:(t + 1) * 128, :], writes=[stg], sembuf=stg)
            else:
                for q8 in range(8):
                    P.dma('sp', stg[:, q8 * 128:(q8 + 1) * 128], D["x"][t * 128:(t + 1) * 128, q8 * 128:(q8 + 1) * 128], writes=[stg], sembuf=stg)
            for half in range(0 if cfg.get('noxt') else cfg.get('nhalf', 2)):
                p = nps()
                for q in range(cfg.get('nq', 4)):
                    fc = half * 4 + q
                    tr(p[:, q * 128:(q + 1) * 128], stg[:, fc * 128:(fc + 1) * 128], ident_f[:, :], [stg, ident_f], [p])
                for q in range(cfg.get('nq', 4) if not cfg.get('nocp') else 0):
                    fc = half * 4 + q
                    cp(alt('dve', 'act') if not cfg.get('cpdve') else 'dve', xT[fc][:, t * 128:(t + 1) * 128], p[:, q * 128:(q + 1) * 128], [p], [xT[fc]])

        P.barrier()
        s0.close()

        def adaln_gen(i):
            p = PS[NPB - 3]
            for k in range(12):
                s = wload([(D["w_ada"][i, :, k * 512:(k + 1) * 512].rearrange("(kc p) n -> p kc n", p=128), 0, [128, 8, 512])])
                sv = s[:, :].rearrange("p (kc n) -> p kc n", kc=8)
                for o4 in range(4):
                    oc = k * 4 + o4
                    for kc in range(8):
                        mm(p[:, oc:oc + 1], sv[:, kc, o4 * 128:(o4 + 1) * 128], s_bf[:, kc:kc + 1], kc == 0, kc == 7, [s, s_bf], [p])
                yield
            mb = modb[i]
            tt('dve', mb[:, 0:48], p[:, 0:48], vc(f"bada_{i}", 0, 48), ALU.add, [p, vcols], [mb])
            stt(mb[:, 48:56], mb[:, 8:16], 1.0, vc(f"g1_{i}", 0, 8), ALU.add, ALU.mult, [mb, vcols], [mb])
            stt(mb[:, 56:64], mb[:, 32:40], 1.0, vc(f"g2_{i}", 0, 8), ALU.add, ALU.mult, [mb, vcols], [mb])
            ts('dve', mb[:, 48:64], mb[:, 48:64], 32.0, None, ALU.mult, None, [mb], [mb])

        def adaln(i):
            for _ in adaln_gen(i):
                pass

        def norm_mod(i, which):
            mb = modb[i]
            acol = 48 if which == 1 else 56
            bcol = 0 if which == 1 else 24
            for th in range(2):
                cs = slice(th * 512, (th + 1) * 512)
                p = nps()
                for fc in range(8):
                    sq = sq_r[fc % 4]
                    act(sq[:, :], xT[fc][:, cs], AF.Square, [xT[fc]], [sq])
                    mm(p[:, :], ones_b[:, :], sq[:, :], fc == 0, fc == 7, [ones_b, sq], [p])
                act(rstd_t[:, :], p[:, :], AF.Ln, [p], [rstd_t], bias=1024.0 * EPS)
                act(rstd_t[:, :], rstd_t[:, :], AF.Exp, [rstd_t], [rstd_t], scale=-0.5)
                for fc in range(8):
                    tm = ntmp()
                    stt(tm[:, :], xT[fc][:, cs], mb[:, acol + fc:acol + fc + 1], rstd_t[:, :], ALU.mult, ALU.mult,
                        [xT[fc], mb, rstd_t], [tm])
                    act(hT[fc][:, cs], tm[:, :], AF.Identity, [tm, mb], [hT[fc]], bias=mb[:, bcol + fc:bcol + fc + 1])

        def ffn(i, lst, pump=None):
            norm_mod(i, 2)
            gT = [P.sbuf([128, NT], BF16, f"gT{i}_{k}", stack=lst) for k in range(22)]
            sa_r = [P.sbuf([128, 512], F32, f"sa{i}_{k}", stack=lst) for k in range(2)]
            mb = modb[i]
            win = D["ffn_w_in"]
            for hb in range(11):
                s = wload([
                    (win[i, :, hb * 256:(hb + 1) * 256].rearrange("(kc p) n -> p kc n", p=128), 0, [128, 8, 256]),
                    (win[i, :, 2816 + hb * 256:2816 + (hb + 1) * 256].rearrange("(kc p) n -> p kc n", p=128), 2048, [128, 8, 256]),
                ])
                sa_v = s[:, 0:2048].rearrange("p (kc n) -> p kc n", kc=8)
                su_v = s[:, 2048:4096].rearrange("p (kc n) -> p kc n", kc=8)
                for sub in range(2):
                    hc = hb * 2 + sub
                    for th in range(2):
                        cs = slice(th * 512, (th + 1) * 512)
                        pa = nps()
                        for kc in range(8):
                            mm(pa[:, :], sa_v[:, kc, sub * 128:(sub + 1) * 128], hT[kc][:, cs], kc == 0, kc == 7, [s, hT[kc]], [pa])
                        pu = nps()
                        for kc in range(8):
                            mm(pu[:, :], su_v[:, kc, sub * 128:(sub + 1) * 128], hT[kc][:, cs], kc == 0, kc == 7, [s, hT[kc]], [pu])
                        sa = sa_r[(hc * 2 + th) % 2]
                        act(sa[:, :], pa[:, :], AF.Silu, [pa], [sa])
                        tt('dve', gT[hc][:, cs], sa[:, :], pu[:, :], ALU.mult, [sa, pu], [gT[hc]])
                if pump is not None:
                    next(pump, None)
            wout = D["ffn_w_out"]
            for oc in range(8):
                s1 = wload([(wout[i, 0:1408, oc * 128:(oc + 1) * 128].rearrange("(kc p) n -> p kc n", p=128), 0, [128, 11, 128])])
                s2 = wload([(wout[i, 1408:2816, oc * 128:(oc + 1) * 128].rearrange("(kc p) n -> p kc n", p=128), 0, [128, 11, 128])])
                v1 = s1[:, 0:1408].rearrange("p (kc n) -> p kc n", kc=11)
                v2 = s2[:, 0:1408].rearrange("p (kc n) -> p kc n", kc=11)
                for th in range(2):
                    cs = slice(th * 512, (th + 1) * 512)
                    p = nps()
                    for hc in range(22):
                        sv, ss = (v1, s1) if hc < 11 else (v2, s2)
                        mm(p[:, :], sv[:, hc % 11, :], gT[hc][:, cs], hc == 0, hc == 21, [ss, gT[hc]], [p])
                    stt(xT[oc][:, cs], p[:, :], mb[:, 40 + oc:41 + oc], xT[oc][:, cs], ALU.mult, ALU.add, [p, mb, xT[oc]], [xT[oc]])

        MIXERS = {}
        def conv_mixer(i, j, lst):
            mb = modb[i]
            ubuf = [P.sbuf([128, 4 * 286], BF16, f"ubuf{c}", stack=lst) for c in range(8)]
            vbuf = [P.sbuf([128, NT], F32, f"vbuf{c}", stack=lst) for c in range(8)]
            sg_r = [P.sbuf([128, 512], F32, f"sg{k}", stack=lst) for k in range(2)]
            DwE = [P.sbuf([128, 16 * 128], BF16, f"DwE{k}", stack=lst) for k in range(2)]
            DwO = [P.sbuf([128, 15 * 128], BF16, f"DwO{k}", stack=lst) for k in range(2)]
            mean_t = P.sbuf([128, 512], F32, "cv_mean", stack=lst)
            var_t = P.sbuf([128, 512], F32, "cv_var", stack=lst)
            w1 = D["cv_w_pw1"]
            for c in range(8):
                memset('pool', ubuf[c][:, :], 0.0, [ubuf[c]])
            WS = {}

            def tap(cc, w):
                if w % 2 == 0:
                    return DwE[cc % 2], DwE[cc % 2][:, (w // 2) * 128:(w // 2 + 1) * 128]
                return DwO[cc % 2], DwO[cc % 2][:, (w // 2) * 128:(w // 2 + 1) * 128]

            def build_dw(cc):
                for w in range(31):
                    b, ap = tap(cc, w)
                    if w % 2 == 0:
                        ts('dve', ap, ident_f[:, :], vc("wdw", w * 8 + cc), None, ALU.mult, None, [ident_f, vcols], [b])
                    else:
                        act(ap, ident_f[:, :], AF.Identity, [ident_f, vcols], [b], scale=vc("wdw", w * 8 + cc))

            def pw1(cc):
                c2, sub = cc // 2, cc % 2
                if sub == 0:
                    WS[c2] = wload([
                        (w1[:, c2 * 256:(c2 + 1) * 256].rearrange("(kc p) n -> p kc n", p=128), 0, [128, 8, 256]),
                        (w1[:, 1024 + c2 * 256:1024 + (c2 + 1) * 256].rearrange("(kc p) n -> p kc n", p=128), 2048, [128, 8, 256]),
                    ])
                s = WS[c2]
                sa_v = s[:, 0:2048].rearrange("p (kc n) -> p kc n", kc=8)
                sg_v = s[:, 2048:4096].rearrange("p (kc n) -> p kc n", kc=8)
                ub = ubuf[cc][:, :].rearrange("p (s t) -> p s t", s=4)
                for th in range(2):
                    cs = slice(th * 512, (th + 1) * 512)
                    pa = nps()
                    for kc in range(8):
                        mm(pa[:, :], sa_v[:, kc, sub * 128:(sub + 1) * 128], hT[kc][:, cs], kc == 0, kc == 7, [s, hT[kc]], [pa])
                    pg = nps()
                    for kc in range(8):
                        mm(pg[:, :], sg_v[:, kc, sub * 128:(sub + 1) * 128], hT[kc][:, cs], kc == 0, kc == 7, [s, hT[kc]], [pg])
                    sg = sg_r[th]
                    act(sg[:, :], pg[:, :], AF.Sigmoid, [pg, vcols], [sg], bias=vc("bpw1", 8 + cc))
                    stt(ub[:, 2 * th:2 * th + 2, 15:271], pa[:, :].rearrange("p (s t) -> p s t", s=2), vc("bpw1", cc),
                        sg[:, :].rearrange("p (s t) -> p s t", s=2), ALU.add, ALU.mult, [pa, sg, vcols], [ubuf[cc]])
                ts('dve', ub[:, 1:4, 0:15], ub[:, 0:3, 256:271], flag[:, 0:1], None, ALU.mult, None, [ubuf[cc], flag], [ubuf[cc]])
                ts('dve', ub[:, 0:3, 271:286], ub[:, 1:4, 15:30], flag[:, 0:1], None, ALU.mult, None, [ubuf[cc], flag], [ubuf[cc]])

            def dconv(cc):
                ub = ubuf[cc][:, :].rearrange("p (s t) -> p s t", s=4)
                for sp in range(2):
                    p = nps()
                    for w in range(31):
                        b, ap = tap(cc, w)
                        mm(p[:, :], ap, ub[:, 2 * sp:2 * sp + 2, w:w + 256], w == 0, w == 30, [b, ubuf[cc]], [p])
                    act(vbuf[cc][:, sp * 512:(sp + 1) * 512], p[:, :], AF.Identity, [p, vcols], [vbuf[cc]], bias=vc("bdw", cc))

            pw1(0)
            build_dw(0)
            for cc in range(8):
                if cc + 1 < 8:
                    pw1(cc + 1)
                    build_dw(cc + 1)
                dconv(cc)
            for th in range(2):
                cs = slice(th * 512, (th + 1) * 512)
                p1 = nps()
                for cc in range(8):
                    mm(p1[:, :], ones_f[:, :], vbuf[cc][:, cs], cc == 0, cc == 7, [ones_f, vbuf[cc]], [p1])
                p2 = nps()
                for cc in range(8):
                    sq = sq_r[cc % 2]
                    act(sq[:, :], vbuf[cc][:, cs], AF.Square, [vbuf[cc]], [sq])
                    mm(p2[:, :], ones_b[:, :], sq[:, :], cc == 0, cc == 7, [ones_b, sq], [p2])
                act(mean_t[:, :], p1[:, :], AF.Identity, [p1], [mean_t], scale=1.0 / 1024)
                tt('dve', var_t[:, :], mean_t[:, :], mean_t[:, :], ALU.mult, [mean_t], [var_t])
                stt(var_t[:, :], p2[:, :], 1.0 / 1024, var_t[:, :], ALU.mult, ALU.subtract, [p2, var_t], [var_t])
                act(rstd_t[:, :], var_t[:, :], AF.Ln, [var_t], [rstd_t], bias=EPS)
                act(rstd_t[:, :], rstd_t[:, :], AF.Exp, [rstd_t], [rstd_t], scale=-0.5)
                for cc in range(8):
                    tm = ntmp()
                    tt('dve', tm[:, :], vbuf[cc][:, cs], mean_t[:, :], ALU.subtract, [vbuf[cc], mean_t], [tm])
                    tt('pool', tm[:, :], tm[:, :], rstd_t[:, :], ALU.mult, [tm, rstd_t], [tm])
                    act(hT[cc][:, cs], tm[:, :], AF.Silu, [tm, vcols], [hT[cc]], bias=vc("bln", cc), scale=vc("gln", cc))
            w2 = D["cv_w_pw2"]
            for oc2 in range(2):
                s = wload([(w2[:, oc2 * 512:(oc2 + 1) * 512].rearrange("(kc p) n -> p kc n", p=128), 0, [128, 8, 512])])
                sv = s[:, :].rearrange("p (kc n) -> p kc n", kc=8)
                for o4 in range(4):
                    oc = oc2 * 4 + o4
                    for th in range(2):
                        cs = slice(th * 512, (th + 1) * 512)
                        p = nps()
                        for kc in range(8):
                            mm(p[:, :], sv[:, kc, o4 * 128:(o4 + 1) * 128], hT[kc][:, cs], kc == 0, kc == 7, [s, hT[kc]], [p])
                        tm = ntmp()
                        ts('dve', tm[:, :], p[:, :], vc("bpw2", oc), None, ALU.add, None, [p, vcols], [tm])
                        stt(xT[oc][:, cs], tm[:, :], mb[:, 16 + oc:17 + oc], xT[oc][:, cs], ALU.mult, ALU.add, [tm, mb, xT[oc]], [xT[oc]])

        MIXERS['conv'] = conv_mixer

        def mla_mixer(i, j, lst):
            mb = modb[i]
            cqf = [P.sbuf([128, 512], F32, f"cqf{k}", stack=lst) for k in range(3)]
            cqn = [P.sbuf([128, NT], BF16, f"cqn{k}", stack=lst) for k in range(3)]
            kvl = [P.sbuf([128, 1280], BF16, f"kvl{k}", stack=lst) for k in range(3)]
            rq = P.sbuf([96, 2048], F32, "rq", stack=lst)
            rk = P.sbuf([96, 2560], F32, "rk", stack=lst)
            mkb = P.sbuf([128, 40], F32, "mkb", stack=lst)
            cstg = P.sbuf([128, 2 * 288], F32, "cstg", stack=lst)
            ostg = [P.sbuf([128, 288], F32, f"ostg{k}", stack=lst) for k in range(2)]
            qf_r = [P.sbuf([104, NT], BF16, f"qf{k}", stack=lst) for k in range(2)]
            kf_r = [P.sbuf([104, 1280], BF16, f"kf{k}", stack=lst) for k in range(2)]
            vaug = P.sbuf([128, 10 * 4 * 128], BF16, "vaug", stack=lst)
            E_r = [P.sbuf([128, 512], BF16, f"E{k}", stack=lst) for k in range(4)]
            sqh = [P.sbuf([96, 512], BF16, f"sqh{k}", stack=lst) for k in range(2)]
            rs_h = [P.sbuf([96, 512], F32, f"rsh{k}", stack=lst) for k in range(2)]
            rec = [P.sbuf([128, 512], F32, f"rec{k}", stack=lst) for k in range(2)]
            oT = hT
            cnt = {'e': 0, 'h': 0}

            P.dma('sp', rq[:, 0:1024], D["ropeq"][0], writes=[rq], sembuf=rq)
            P.dma('sp', rq[:, 1024:2048], D["ropeq"][1], writes=[rq], sembuf=rq)
            P.dma('sp', rk[:, 0:1280], D["ropek"][0], writes=[rk], sembuf=rk)
            P.dma('sp', rk[:, 1280:2560], D["ropek"][1], writes=[rk], sembuf=rk)
            cv = cstg[:, :].rearrange("p (t f) -> p t f", t=2)
            P.dma('sp', cv[:, :, 0:256], D["cache_ckv"][j].rearrange("(t p) f -> p t f", p=128), writes=[cstg], sembuf=cstg)
            P.dma('sp', cv[:, :, 256:288], D["cache_kpe"][j].rearrange("(t p) f -> p t f", p=128), writes=[cstg], sembuf=cstg)

            def vcp(name, n=96):
                b = VEC_BASE[name]
                return vcols[0:n, b:b + 1]
            ts('dve', rq[:, 0:1024], rq[:, 0:1024], vcp(f"gqn_{j}"), None, ALU.mult, None, [rq, vcols], [rq])
            ts('dve', rq[:, 1024:2048], rq[:, 1024:2048], vcp(f"gqnsw_{j}"), None, ALU.mult, None, [rq, vcols], [rq])
            ts('dve', rk[:, 0:1280], rk[:, 0:1280], vcp(f"gkn_{j}"), None, ALU.mult, None, [rk, vcols], [rk])
            ts('dve', rk[:, 1280:2560], rk[:, 1280:2560], vcp(f"gknsw_{j}"), None, ALU.mult, None, [rk, vcols], [rk])
            memset('pool', vaug[:, :], 1.0, [vaug])
            for k in range(2):
                P.dma('pool', qf_r[k][96:104, :], D["maskq"], writes=[qf_r[k]], sembuf=qf_r[k])
                P.dma('pool', kf_r[k][96:104, :], D["maskk"], writes=[kf_r[k]], sembuf=kf_r[k])

            sdq = wload([(D["mla_w_dq"][j].rearrange("(kc p) n -> p kc n", p=128), 0, [128, 8, 384])])
            dqv = sdq[:, 0:3072].rearrange("p (kc n) -> p kc n", kc=8)
            sdkv = wload([(D["mla_w_dkv"][j].rearrange("(kc p) n -> p kc n", p=128), 0, [128, 8, 288])])
            dkvv = sdkv[:, 0:2304].rearrange("p (kc n) -> p kc n", kc=8)
            for th in range(2):
                cs = slice(th * 512, (th + 1) * 512)
                pss = nacc()
                for oc in range(3):
                    p = nps()
                    for kc in range(8):
                        mm(p[:, :], dqv[:, kc, oc * 128:(oc + 1) * 128], hT[kc][:, cs], kc == 0, kc == 7, [sdq, hT[kc]], [p])
                    cp('dve', cqf[oc][:, :], p[:, :], [p], [cqf[oc]])
                    sq = sq_r[oc % 2]
                    act(sq[:, :], p[:, :], AF.Square, [p], [sq])
                    mm(pss[:, :], ones_b[:, :], sq[:, :], oc == 0, oc == 2, [ones_b, sq], [pss])
                act(rstd_t[:, :], pss[:, :], AF.Ln, [pss], [rstd_t], bias=EPS, scale=1.0 / 384)
                act(rstd_t[:, :], rstd_t[:, :], AF.Exp, [rstd_t], [rstd_t], scale=-0.5)
                for oc in range(3):
                    stt(cqn[oc][:, cs], cqf[oc][:, :], vc(f"gq_{j}", oc), rstd_t[:, :], ALU.mult, ALU.mult, [cqf[oc], vcols, rstd_t], [cqn[oc]])
                pss = nacc()
                ckvf = [ntmp(), ntmp()]
                for oc in range(2):
                    p = nps()
                    for kc in range(8):
                        mm(p[:, :], dkvv[:, kc, oc * 128:(oc + 1) * 128], hT[kc][:, cs], kc == 0, kc == 7, [sdkv, hT[kc]], [p])
                    cp('dve', ckvf[oc][:, :], p[:, :], [p], [ckvf[oc]])
                    sq = sq_r[oc % 2]
                    act(sq[:, :], p[:, :], AF.Square, [p], [sq])
                    mm(pss[:, :], ones_b[:, :], sq[:, :], oc == 0, oc == 1, [ones_b, sq], [pss])
                pk = nps()
                for kc in range(8):
                    mm(pk[0:32, :], dkvv[:, kc, 256:288], hT[kc][:, cs], kc == 0, kc == 7, [sdkv, hT[kc]], [pk])
                kpef = ntmp()
                cp('dve', kpef[0:32, :], pk[0:32, :], [pk], [kpef])
                cp('act', kvl[2][0:32, 256 + th * 512:256 + (th + 1) * 512], pk[0:32, :], [pk], [kvl[2]])
                act(rstd_t[:, :], pss[:, :], AF.Ln, [pss], [rstd_t], bias=EPS, scale=1.0 / 256)
                act(rstd_t[:, :], rstd_t[:, :], AF.Exp, [rstd_t], [rstd_t], scale=-0.5)
                for oc in range(2):
                    stt(ckvf[oc][:, :], ckvf[oc][:, :], vc(f"gkv_{j}", oc), rstd_t[:, :], ALU.mult, ALU.mult, [ckvf[oc], vcols, rstd_t], [ckvf[oc]])
                    cp('act', kvl[oc][:, 256 + th * 512:256 + (th + 1) * 512], ckvf[oc][:, :], [ckvf[oc]], [kvl[oc]])
                for t4 in range(4):
                    t = th * 4 + t4
                    p = nps()
                    for oc in range(2):
                        tr(p[:, oc * 128:(oc + 1) * 128], ckvf[oc][:, t4 * 128:(t4 + 1) * 128], ident_f[:, :], [ckvf[oc], ident_f], [p])
                    tr(p[:, 256:288], kpef[0:32, t4 * 128:(t4 + 1) * 128], ident_f[0:32, 0:32], [kpef, ident_f], [p])
                    og = ostg[t % 2]
                    cp(alt('dve', 'act'), og[:, :], p[:, 0:288], [p], [og])
                    P.dma('sp', D["ockv"][j, t * 128:(t + 1) * 128, :], og[:, 0:256], reads=[og], sembuf=og, is_output=True)
                    P.dma('sp', D["okpe"][j, t * 128:(t + 1) * 128, :], og[:, 256:288], reads=[og], sembuf=og, is_output=True)

            for t in range(2):
                p = nps()
                for fc in range(2):
                    tr(p[:, fc * 128:(fc + 1) * 128], cv[:, t, fc * 128:(fc + 1) * 128], ident_f[:, :], [cstg, ident_f], [p])
                tr(p[0:32, 256:384], cv[:, t, 256:288], ident_f[:, :], [cstg, ident_f], [p])
                for fc in range(2):
                    cp(alt('dve', 'act'), kvl[fc][:, t * 128:(t + 1) * 128], p[:, fc * 128:(fc + 1) * 128], [p], [kvl[fc]])
                cp('dve', kvl[2][0:32, t * 128:(t + 1) * 128], p[0:32, 256:384], [p], [kvl[2]])

            vv = vaug[:, :].rearrange("p (k a b c) -> p k a b c", k=10, a=2, b=2)
            tpe = P.sbuf([96, 1280], F32, "tpe", stack=lst)
            KB = [(0, 512), (512, 512), (1024, 256)]
            for hg in range(4):
                sA = wload([
                    (D["w_uq"][j, :, hg * 384:(hg + 1) * 384].rearrange("(kc p) n -> p kc n", p=128), 0, [128, 3, 384]),
                    (D["w_uq_sw"][j, :, hg * 384:(hg + 1) * 384].rearrange("(kc p) n -> p kc n", p=128), 1152, [128, 3, 384]),
                    (D["w_ukv_v"][j, :, hg * 256:(hg + 1) * 256].rearrange("(kc p) n -> p kc n", p=128), 2304, [128, 2, 256]),
                ])
                wqv = sA[:, 0:1152].rearrange("p (kc n) -> p kc n", kc=3)
                wqsv = sA[:, 1152:2304].rearrange("p (kc n) -> p kc n", kc=3)
                wvv = sA[:, 2304:2816].rearrange("p (kc n) -> p kc n", kc=2)
                sB = wload([
                    (D["wk"][j, :, hg * 384:(hg + 1) * 384].rearrange("(kc p) n -> p kc n", p=128), 0, [128, 3, 384]),
                    (D["wk_sw"][j, :, hg * 384:(hg + 1) * 384].rearrange("(kc p) n -> p kc n", p=128), 1152, [128, 3, 384]),
                ])
                wkv = sB[:, 0:1152].rearrange("p (kc n) -> p kc n", kc=3)
                wksv = sB[:, 1152:2304].rearrange("p (kc n) -> p kc n", kc=3)
                for kc in range(10):
                    p = nps()
                    for c2 in range(2):
                        mm(p[:, 0:256], kvl[c2][:, kc * 128:(kc + 1) * 128], wvv[:, c2, :], c2 == 0, c2 == 1, [kvl[c2], sA], [p])
                    pv4 = p[:, 0:256].rearrange("p (a b c) -> p a b c", a=2, b=2)
                    cp('dve', vv[:, kc, :, 0, 0:64], pv4[:, :, 0, :], [p], [vaug])
                    cp('act', vv[:, kc, :, 1, 64:128], pv4[:, :, 1, :], [p], [vaug])
                ATT = [PS[0], PS[1], PS[2]]
                NRB = [PS[3], PS[4], PS[5]]

                def nr_gen(dst, dcols, mma, mmb, n, tab, toff0, toff1, tcols, tpe=None):
                    pa, pb, pss = NRB
                    for (l, r, st_, sp_, R_) in mma:
                        mm(pa[0:96, 0:n], l, r, st_, sp_, R_, [pa])
                    if tpe is None:
                        for (l, r, st_, sp_, R_) in mmb:
                            mm(pb[0:96, 0:n], l, r, st_, sp_, R_, [pb])
                    k2 = cnt['h'] % 2
                    cnt['h'] += 1
                    sq = sqh[k2]
                    act(sq[:, 0:n], pa[0:96, 0:n], AF.Square, [pa], [sq])
                    yield
                    mm(pss[0:96, 0:n], ones_b[0:96, 0:96], sq[:, 0:n], True, True, [ones_b, sq], [pss])
                    rs = rs_h[k2]
                    act(rs[:, 0:n], pss[0:96, 0:n], AF.Ln, [pss], [rs], bias=EPS, scale=1.0 / 96)
                    act(rs[:, 0:n], rs[:, 0:n], AF.Exp, [rs], [rs], scale=-0.5)
                    t1 = ntmp()
                    if tpe is None:
                        t2 = ntmp()
                        tt('dve', t1[0:96, 0:n], pa[0:96, 0:n], tab[:, toff0 + tcols.start:toff0 + tcols.stop], ALU.mult, [pa, tab], [t1])
                        tt('dve', t2[0:96, 0:n], pb[0:96, 0:n], tab[:, toff1 + tcols.start:toff1 + tcols.stop], ALU.mult, [pb, tab], [t2])
                        tt('pool', t1[0:96, 0:n], t1[0:96, 0:n], t2[0:96, 0:n], ALU.add, [t1, t2], [t1])
                        tt('pool', dst[0:96, dcols], t1[0:96, 0:n], rs[:, 0:n], ALU.mult, [t1, rs], [dst])
                    else:
                        tt('dve', t1[0:64, 0:n], pa[0:64, 0:n], tab[0:64, toff0 + tcols.start:toff0 + tcols.stop], ALU.mult, [pa, tab], [t1])
                        tt('pool', dst[0:64, dcols], t1[0:64, 0:n], rs[0:64, 0:n], ALU.mult, [t1, rs], [dst])
                        tt('pool', dst[64:96, dcols], tpe[64:96, tcols], rs[64:96, 0:n], ALU.mult, [tpe, rs], [dst])
                    yield

                def head_feeder(hl):
                    h = hg * 4 + hl
                    qf = qf_r[h % 2]
                    kf = kf_r[h % 2]
                    for th in range(2):
                        cs = slice(th * 512, (th + 1) * 512)
                        mma = [(wqv[:, kc, hl * 96:(hl + 1) * 96], cqn[kc][:, cs], kc == 0, kc == 2, [sA, cqn[kc]]) for kc in range(3)]
                        mmb = [(wqsv[:, kc, hl * 96:(hl + 1) * 96], cqn[kc][:, cs], kc == 0, kc == 2, [sA, cqn[kc]]) for kc in range(3)]
                        yield from nr_gen(qf, cs, mma, mmb, 512, rq, 0, 1024, cs)
                    for (k0, n) in KB:
                        ks = slice(k0, k0 + n)
                        mma = []
                        mmb = []
                        for kc in range(3):
                            ksz = 128 if kc < 2 else 32
                            mma.append((wkv[0:ksz, kc, hl * 96:(hl + 1) * 96], kvl[kc][0:ksz, ks], kc == 0, kc == 2, [sB, kvl[kc]]))
                            mmb.append((wksv[0:ksz, kc, hl * 96:(hl + 1) * 96], kvl[kc][0:ksz, ks], kc == 0, kc == 2, [sB, kvl[kc]]))
                        yield from nr_gen(kf, ks, mma, mmb, n, rk, 0, 1280, ks)

                def attention(hl, feeder):
                    h = hg * 4 + hl
                    qf = qf_r[h % 2]
                    kf = kf_r[h % 2]
                    a, b = hl // 2, hl % 2
                    for th in range(2):
                        cs = slice(th * 512, (th + 1) * 512)
                        po = nacc()
                        psts = {}

                        def issue_qk(kc):
                            pst = ATT[cnt['p'] % 3]
                            cnt['p'] += 1
                            mm(pst[:, :], kf[0:104, kc * 128:(kc + 1) * 128], qf[0:104, cs], True, True, [kf, qf], [pst])
                            psts[kc] = pst
                        issue_qk(0)
                        issue_qk(1)
                        for kc in range(10):
                            pst = psts[kc]
                            E = E_r[cnt['e'] % 4]
                            cnt['e'] += 1
                            act(E[:, :], pst[:, :], AF.Exp, [pst], [E], scale=float(96.0 ** -0.5))
                            if kc + 2 < 10:
                                issue_qk(kc + 2)
                            mm(po[:, :], vv[:, kc, a, b, :], E[:, :], kc == 0, kc == 9, [vaug, E], [po])
                            if feeder is not None:
                                next(feeder, None)
                        rc = rec[th]
                        if b == 0:
                            P.op('dve', lambda e, rc=rc, po=po: e.reciprocal(out=rc[0:64, :], in_=po[64:128, :]), [po], [rc])
                            tt('dve', oT[h // 2][0:64, cs], po[0:64, :], rc[0:64, :], ALU.mult, [po, rc], [oT[h // 2]])
                        else:
                            P.op('dve', lambda e, rc=rc, po=po: e.reciprocal(out=rc[64:128, :], in_=po[0:64, :]), [po], [rc])
                            tt('dve', oT[h // 2][64:128, cs], po[64:128, :], rc[64:128, :], ALU.mult, [po, rc], [oT[h // 2]])

                if hg == 0:
                    for (k0, n) in KB:
                        ks = slice(k0, k0 + n)
                        pa, pb, _ = NRB
                        for kc in range(3):
                            ksz = 128 if kc < 2 else 32
                            mm(pa[0:96, 0:n], wkv[0:ksz, kc, 0:96], kvl[kc][0:ksz, ks], kc == 0, kc == 2, [sB, kvl[kc]], [pa])
                        for kc in range(3):
                            ksz = 128 if kc < 2 else 32
                            mm(pb[0:96, 0:n], wksv[0:ksz, kc, 0:96], kvl[kc][0:ksz, ks], kc == 0, kc == 2, [sB, kvl[kc]], [pb])
                        t1 = ntmp()
                        t2 = ntmp()
                        tt('dve', t1[0:96, 0:n], pa[0:96, 0:n], rk[:, ks], ALU.mult, [pa, rk], [t1])
                        tt('dve', t2[0:96, 0:n], pb[0:96, 0:n], rk[:, 1280 + k0:1280 + k0 + n], ALU.mult, [pb, rk], [t2])
                        tt('pool', tpe[0:96, ks], t1[0:96, 0:n], t2[0:96, 0:n], ALU.add, [t1, t2], [tpe])
                cnt.setdefault('p', 0)
                for _ in head_feeder(0):
                    pass
                for hl in range(4):
                    nxt = head_feeder(hl + 1) if hl < 3 else None
                    attention(hl, nxt)
                    if nxt is not None:
                        for _ in nxt:
                            pass

            wo = D["mla_w_o"]
            for oc2 in range(2):
                s = wload([(wo[j, :, oc2 * 512:(oc2 + 1) * 512].rearrange("(kc p) n -> p kc n", p=128), 0, [128, 8, 512])])
                sv = s[:, :].rearrange("p (kc n) -> p kc n", kc=8)
                for o4 in range(4):
                    oc = oc2 * 4 + o4
                    for th in range(2):
                        cs = slice(th * 512, (th + 1) * 512)
                        p = nps()
                        for kc in range(8):
                            mm(p[:, :], sv[:, kc, o4 * 128:(o4 + 1) * 128], oT[kc][:, cs], kc == 0, kc == 7, [s, oT[kc]], [p])
                        stt(xT[oc][:, cs], p[:, :], mb[:, 16 + oc:17 + oc], xT[oc][:, cs], ALU.mult, ALU.add, [p, mb, xT[oc]], [xT[oc]])

        MIXERS['mla'] = mla_mixer

        def ssd_mixer(i, j, lst):
            mb = modb[i]
            W = D["ssd_w_in"]

            def sb(shape, dt, name):
                return P.sbuf(shape, dt, "ssd_" + name, stack=lst)
            tri_f = sb([128, 128], F32, "tri_f"); tri_b = sb([128, 128], F32, "tri_b")
            negm = sb([128, 256], BF16, "negm")
            rows3 = sb([128, 192], F32, "rows3")
            a_b = sb([128, 64], F32, "a_b"); dsum = sb([128, 32], F32, "dsum")
            oh2 = sb([128, 64], BF16, "oh2")
            sel = sb([128, 16 * 128], BF16, "sel")
            gn_b = sb([128, 512], F32, "gn_b")
            brow = sb([1, 640], BF16, "brow")
            ones1 = sb([1, 128], BF16, "ones1")
            dtc = [sb([128, 64], F32, f"dt{c}") for c in range(8)]
            acum = [sb([128, 64], F32, f"acum{c}") for c in range(8)]
            ea = [sb([128, 64], F32, f"ea{c}") for c in range(8)]
            cd = [sb([128, 64], F32, f"cd{c}") for c in range(8)]
            ddt = [sb([128, 64], F32, f"ddt{c}") for c in range(8)]
            AT2 = [sb([128, 128], BF16, f"AT2{c}") for c in range(8)]
            NA2 = [sb([128, 128], BF16, f"NA2{c}") for c in range(8)]
            a2s = sb([128, 128], F32, "a2s"); n2s = sb([128, 128], F32, "n2s")
            t64 = [sb([128, 64], F32, f"t64{k}") for k in range(3)]
            x_tok = [sb([128, 512], BF16, f"xtok{c}") for c in range(8)]
            Btok = [sb([128, 128], BF16, f"btok{c}") for c in range(8)]
            BT = sb([128, NT], BF16, "BT"); CT = sb([128, NT], BF16, "CT")
            pbuf = [sb([128, 4 * 260], BF16, f"pbuf{k}") for k in range(2)]
            Dw5 = [sb([128, 5 * 128], BF16, f"dw5{k}") for k in range(2)]
            H = [sb([128, 512], F32, f"H{d}") for d in range(2)]
            Hinb = [sb([128, 512], BF16, f"hinb{c}") for c in range(8)]
            Hinf = [sb([128, 512], BF16, f"hinf{c}") for c in range(8)]
            xdd_r = [sb([128, 512], BF16, f"xdd{k}") for k in range(2)]
            cbT = sb([128, 128], F32, "cbT")
            Eb = [sb([128, 512], BF16, f"Eb{k}") for k in range(2)]
            Mt = [[sb([128, 512], BF16, f"Mt{d}{q}") for q in range(2)] for d in range(2)]
            y1_ = [sb([128, 512], F32, f"y1_{k}") for k in range(2)]
            y2_ = [sb([128, 512], F32, f"y2_{k}") for k in range(2)]
            yd_ = [sb([128, 512], F32, f"yd_{k}") for k in range(2)]
            sz2_ = [sb([128, 512], F32, f"sz2_{k}") for k in range(2)]
            yn_ = [sb([128, 512], F32, f"yn_{k}") for k in range(2)]
            ssc_ = [sb([128, 2], F32, f"ssc_{k}") for k in range(2)]
            ynT = sb([128, 4 * NT], BF16, "ynT")
            hst = [sb([128, 512], F32, f"hst{k}") for k in range(2)]
            ynv = ynT[:, :].rearrange("p (k t) -> p k t", k=4)

            P.dma('sp', tri_f[:, :], D["tri"][0], writes=[tri_f], sembuf=tri_f)
            P.dma('sp', tri_b[:, :], D["tri"][1], writes=[tri_b], sembuf=tri_b)
            P.dma('pool', negm[:, :].rearrange("p (d i) -> p d i", d=2), D["negmask"].rearrange("d p i -> p d i"), writes=[negm], sembuf=negm)
            for r in range(3):
                P.dma('sp', rows3[:, r * 64:(r + 1) * 64], D["ssd_rows"][r:r + 1, :].partition_broadcast(128), writes=[rows3], sembuf=rows3)
            act(a_b[:, :], rows3[:, 64:128], AF.Exp, [rows3], [a_b])
            ts('dve', a_b[:, :], a_b[:, :], -1.0, None, ALU.mult, None, [a_b], [a_b])
            tt('dve', dsum[:, :], rows3[:, 128:160], rows3[:, 160:192], ALU.add, [rows3], [dsum])
            tt('dve', oh2[:, :], ident_f[:, 0:64], ident_f[:, 64:128], ALU.add, [ident_f], [oh2])
            memset('pool', ones1[:, :], 1.0, [ones1])
            for k in range(2):
                memset('pool', pbuf[k][:, :], 0.0, [pbuf[k]])
            negv = negm[:, :].rearrange("p (d i) -> p d i", d=2)
            negm4 = sb([128, 2 * 512], BF16, "negm4")
            for d in range(2):
                cp('dve', negm4[:, d * 512:(d + 1) * 512].rearrange("p (h i) -> p h i", h=4), negv[:, d, :].unsqueeze(1).to_broadcast([128, 4, 128]), [negm], [negm4])

            sdt = wload([(W[:, 5120:5184].rearrange("(kc p) n -> p kc n", p=128), 0, [128, 8, 64])])
            dtv = sdt[:, 0:512].rearrange("p (kc n) -> p kc n", kc=8)
            for c in range(8):
                cc = slice(c * 128, (c + 1) * 128)
                pdt = nps()
                for kc in range(8):
                    mm(pdt[:, 0:64], hT[kc][:, cc], dtv[:, kc, :], kc == 0, kc == 7, [hT[kc], sdt], [pdt])
                ta, tl, td = t64
                tt('dve', ta[:, :], pdt[:, 0:64], rows3[:, 0:64], ALU.add, [pdt, rows3], [ta])
                act(ta[:, :], ta[:, :], AF.Exp, [ta], [ta])
                act(dtc[c][:, :], ta[:, :], AF.Ln, [ta], [dtc[c]], bias=1.0)
                act(tl[:, :], dtc[c][:, :], AF.Ln, [dtc[c]], [tl])
                tt('dve', td[:, :], dtc[c][:, :], a_b[:, :], ALU.mult, [dtc[c], a_b], [td])
                pc = nps()
                mm(pc[:, 0:32], tri_f[:, :], td[:, 0:32], True, True, [tri_f, td], [pc])
                mm(pc[:, 32:64], tri_b[:, :], td[:, 32:64], True, True, [tri_b, td], [pc])
                mm(pc[:, 64:128], ones_f[:, :], td[:, 0:64], True, True, [ones_f, td], [pc])
                cp('dve', acum[c][:, :], pc[:, 0:64], [pc], [acum[c]])
                act(ea[c][:, :], pc[:, 0:64], AF.Exp, [pc], [ea[c]])
                act(cd[c][:, :], pc[:, 64:128], AF.Exp, [pc], [cd[c]])
                tt('dve', ta[:, :], pc[:, 64:128], acum[c][:, :], ALU.subtract, [pc, acum[c]], [ta])
                act(ta[:, :], ta[:, :], AF.Exp, [ta], [ta])
                tt('dve', ddt[c][:, :], dtc[c][:, :], ta[:, :], ALU.mult, [dtc[c], ta], [ddt[c]])
                tt('dve', tl[:, :], tl[:, :], acum[c][:, :], ALU.subtract, [tl, acum[c]], [tl])
                for hf in range(2):
                    cp('dve', a2s[:, hf * 64:(hf + 1) * 64], acum[c][:, :], [acum[c]], [a2s])
                    cp('pool', n2s[:, hf * 64:(hf + 1) * 64], tl[:, :], [tl], [n2s])
                pT = nps()
                tr(pT[:, 0:128], a2s[:, :], ident_f[:, :], [a2s, ident_f], [pT])
                tr(pT[:, 128:256], n2s[:, :], ident_f[:, :], [n2s, ident_f], [pT])
                for (dst, off) in ((AT2[c], 0), (NA2[c], 128)):
                    cp('act', dst[:, :], pT[:, off:off + 128], [pT], [dst])
                    tt('dve', dst[64:128, :], pT[64:128, off:off + 128], dst[64:128, :], ALU.subtract, [pT, dst], [dst])

            def state_out(Hd, seq, d, g):
                pt = nps()
                for blk in range(4):
                    tr(pt[:, blk * 128:(blk + 1) * 128], Hd[:, blk * 128:(blk + 1) * 128], ident_f[:, :], [Hd, ident_f], [pt])
                hs = hst[(seq + d) % 2]
                cp(alt('dve', 'act'), hs[:, :], pt[:, :], [pt], [hs])
                P.dma('sp', D["ossm"][seq, d, 8 * g:8 * g + 8].rearrange("(blk hh) p n -> (hh p) blk n", hh=2),
                      hs[:, :].rearrange("p (blk n) -> p blk n", blk=4), reads=[hs], sembuf=hs, is_output=True)

            def state_step(Hd, c, d, g):
                xd = xdd_r[d]
                tt('dve', xd[:, :].rearrange("p (h q) -> p h q", h=8), x_tok[c][:, :].rearrange("p (h q) -> p h q", h=8),
                   ddt[c][:, d * 32 + 8 * g:d * 32 + 8 * g + 8].unsqueeze(2).to_broadcast([128, 8, 64]), ALU.mult, [x_tok[c], ddt[c]], [xd])
                ps = nps()
                mm(ps[:, :], Btok[c][:, :], xd[:, :], True, True, [Btok[c], xd], [ps])
                tt('dve', Hd[:, :].rearrange("p (h q) -> p h q", h=8), Hd[:, :].rearrange("p (h q) -> p h q", h=8),
                   cd[c][:, d * 32 + 8 * g:d * 32 + 8 * g + 8].unsqueeze(2).to_broadcast([128, 8, 64]), ALU.mult, [Hd, cd[c]], [Hd])
                tt('dve', Hd[:, :], Hd[:, :], ps[:, :], ALU.add, [Hd, ps], [Hd])

            for g in range(cfg.get('ssd_groups', 4)):
                P.dma('sp', gn_b[:, :], D["gnorm_row"][0:1, g * 512:(g + 1) * 512].partition_broadcast(128), writes=[gn_b], sembuf=gn_b)
                P.dma('pool', brow[:, 0:512], D["bconv_row"][0:1, g * 512:(g + 1) * 512], writes=[brow], sembuf=brow)
                P.dma('pool', brow[:, 512:640], D["bconv_row"][0:1, 2048 + g * 128:2048 + (g + 1) * 128], writes=[brow], sembuf=brow)
                selv = sel[:, :].rearrange("p (s m) -> p s m", s=16)
                for d in range(2):
                    cp('dve', selv[:, d * 8:(d + 1) * 8, :], oh2[:, d * 32 + 8 * g:d * 32 + 8 * g + 8].unsqueeze(2).to_broadcast([128, 8, 128]), [oh2], [sel])
                sx = wload([(W[:, 2048 + g * 512:2048 + (g + 1) * 512].rearrange("(kc p) n -> p kc n", p=128), 0, [128, 8, 512])])
                sxv = sx[:, :].rearrange("p (kc n) -> p kc n", kc=8)
                sbc = wload([(W[:, 4096 + g * 128:4096 + (g + 1) * 128].rearrange("(kc p) n -> p kc n", p=128), 0, [128, 8, 128]),
                             (W[:, 4608 + g * 128:4608 + (g + 1) * 128].rearrange("(kc p) n -> p kc n", p=128), 1024, [128, 8, 128])])
                sbv = sbc[:, 0:1024].rearrange("p (kc n) -> p kc n", kc=8)
                scv = sbc[:, 1024:2048].rearrange("p (kc n) -> p kc n", kc=8)
                def qinfo(q):
                    if q < 4:
                        return sxv, sx, q * 128, 4 * g + q
                    elif q == 4:
                        return sbv, sbc, 0, 16 + g
                    return scv, sbc, 0, 20 + g

                def inproj(q):
                    wv_, ws_, wc0, ccg = qinfo(q)
                    pb_ = pbuf[q % 2]
                    pb = pb_[:, :].rearrange("p (s t) -> p s t", s=4)
                    for th in range(2):
                        cs = slice(th * 512, (th + 1) * 512)
                        p = nps()
                        for kc in range(8):
                            mm(p[:, :], wv_[:, kc, wc0:wc0 + 128], hT[kc][:, cs], kc == 0, kc == 7, [ws_, hT[kc]], [p])
                        cp('act', pb[:, 2 * th:2 * th + 2, 2:258], p[:, :].rearrange("p (s t) -> p s t", s=2), [p], [pb_])
                    ts('dve', pb[:, 1:4, 0:2], pb[:, 0:3, 256:258], flag[:, 0:1], None, ALU.mult, None, [pb_, flag], [pb_])
                    ts('dve', pb[:, 0:3, 258:260], pb[:, 1:4, 2:4], flag[:, 0:1], None, ALU.mult, None, [pb_, flag], [pb_])
                    dw_ = Dw5[q % 2]
                    dwv = dw_[:, :].rearrange("p (w n) -> p w n", w=5)
                    for w in range(5):
                        ts('dve', dwv[:, w, :], ident_f[:, :], vc("wconv", w * 24 + ccg), None, ALU.mult, None, [ident_f, vcols], [dw_])

                def sconv(q):
                    wv_, ws_, wc0, ccg = qinfo(q)
                    pb_ = pbuf[q % 2]
                    pb = pb_[:, :].rearrange("p (s t) -> p s t", s=4)
                    dw_ = Dw5[q % 2]
                    dwv = dw_[:, :].rearrange("p (w n) -> p w n", w=5)
                    if q < 5:
                        for t in range(8):
                            seg, off = t // 2, (t % 2) * 128
                            p = nps()
                            for w in range(5):
                                mm(p[:, 0:128], pb[:, seg, off + w:off + w + 128], dwv[:, w, :], w == 0, False, [pb_, dw_], [p])
                            bc0 = q * 128 if q < 4 else 512
                            mm(p[:, 0:128], ones1[0:1, :], brow[0:1, bc0:bc0 + 128], False, True, [ones1, brow], [p])
                            if q < 4:
                                act(x_tok[t][:, q * 128:(q + 1) * 128], p[:, 0:128], AF.Silu, [p], [x_tok[t]])
                            else:
                                act(Btok[t][:, :], p[:, 0:128], AF.Silu, [p], [Btok[t]])
                    if q >= 4:
                        dstT = BT if q == 4 else CT
                        for th in range(2):
                            cs = slice(th * 512, (th + 1) * 512)
                            p = nps()
                            for w in range(5):
                                mm(p[:, :], dwv[:, w, :], pb[:, 2 * th:2 * th + 2, w:w + 256], w == 0, w == 4, [dw_, pb_], [p])
                            act(dstT[:, cs], p[:, :], AF.Silu, [p, vcols], [dstT], bias=vc("bconv", ccg))

                SPH = cfg.get('ssd_phase', 9)
                inproj(0)
                for q in range(6):
                    if q + 1 < 6:
                        inproj(q + 1)
                    sconv(q)
                sz_ = wload([(W[:, g * 512:(g + 1) * 512].rearrange("(kc p) n -> p kc n", p=128), 0, [128, 8, 512])])
                szv = sz_[:, :].rearrange("p (kc n) -> p kc n", kc=8)
                for d in range(2):
                    hs = hst[d]
                    P.dma('sp', hs[:, :].rearrange("p (blk n) -> p blk n", blk=4),
                          D["h0"][d, 8 * g:8 * g + 8].rearrange("(blk hh) p n -> (hh p) blk n", hh=2), writes=[hs], sembuf=hs)
                    pt = nps()
                    for blk in range(4):
                        tr(pt[:, blk * 128:(blk + 1) * 128], hs[:, blk * 128:(blk + 1) * 128], ident_f[:, :], [hs, ident_f], [pt])
                    cp('dve', H[d][:, :], pt[:, :], [pt], [H[d]])
                for k8 in range(8 if SPH >= 2 else 0):
                    c = 7 - k8
                    if c in (5, 3, 1):
                        ts('dve', H[1][:, :], H[1][:, :], flag[:, 0:1], None, ALU.mult, None, [H[1], flag], [H[1]])
                    cp('act', Hinb[c][:, :], H[1][:, :], [H[1]], [Hinb[c]])
                    state_step(H[1], c, 1, g)
                    if c in (6, 4, 2, 0):
                        state_out(H[1], c // 2, 1, g)
                    c = k8
                    if c in (2, 4, 6):
                        ts('dve', H[0][:, :], H[0][:, :], flag[:, 0:1], None, ALU.mult, None, [H[0], flag], [H[0]])
                    cp('act', Hinf[c][:, :], H[0][:, :], [H[0]], [Hinf[c]])
                    state_step(H[0], c, 0, g)
                    if c in (1, 3, 5, 7):
                        state_out(H[0], c // 2, 0, g)
                v8 = lambda ap: ap.rearrange("p (h q) -> p h q", h=8)

                def head(c):
                    cc = slice(c * 128, (c + 1) * 128)
                    k = c % 2
                    y1, y2, yd, sz2 = y1_[k], y2_[k], yd_[k], sz2_[k]
                    pcb = nps()
                    mm(pcb[:, 0:128], BT[:, cc], CT[:, cc], True, True, [BT, CT], [pcb])
                    cp('act', cbT[:, :], pcb[:, 0:128], [pcb], [cbT])
                    psegs = {}
                    for d in range(2):
                        for quad in range(2):
                            pseg = nps()
                            psegs[(d, quad)] = pseg
                            si0 = d * 8 + quad * 4
                            mm(pseg[:, :], NA2[c][:, :], sel[:, si0 * 128:(si0 + 4) * 128], True, False, [sel, NA2[c]], [pseg])
                            mm(pseg[:, :], ident_b[:, :], negm4[:, d * 512:(d + 1) * 512], False, False, [ident_b, negm4], [pseg])
                            for hq in range(4):
                                si = si0 + hq
                                o = pseg[:, hq * 128:(hq + 1) * 128]
                                mm(o, selv[:, si, :], AT2[c][:, :], False, hq == 3, [sel, AT2[c]], [pseg])
                            E = Eb[(d * 2 + quad) % len(Eb)]
                            act(E[:, :], pseg[:, :], AF.Exp, [pseg], [E])
                            tt('dve', Mt[d][quad][:, :].rearrange("p (h i) -> p h i", h=4), E[:, :].rearrange("p (h i) -> p h i", h=4),
                               cbT[:, :].unsqueeze(1).to_broadcast([128, 4, 128]), ALU.mult, [E, cbT], [Mt[d][quad]])
                    pyf = nps()
                    mm(pyf[:, :], CT[:, cc], Hinf[c][:, :], True, True, [CT, Hinf[c]], [pyf])
                    pyb = nps()
                    mm(pyb[:, :], CT[:, cc], Hinb[c][:, :], True, True, [CT, Hinb[c]], [pyb])
                    pz = nacc()
                    for kc in range(8):
                        mm(pz[:, :], hT[kc][:, cc], szv[:, kc, :], kc == 0, kc == 7, [hT[kc], sz_], [pz])
                    pyd = nacc()
                    for hl in range(8):
                        for d in range(2):
                            mm(pyd[:, hl * 64:(hl + 1) * 64], Mt[d][hl // 4][:, (hl % 4) * 128:(hl % 4 + 1) * 128], x_tok[c][:, hl * 64:(hl + 1) * 64],
                               d == 0, d == 1, [Mt[d][hl // 4], x_tok[c]], [pyd])
                    tt('dve', v8(y1[:, :]), v8(pyf[:, :]), ea[c][:, 8 * g:8 * g + 8].unsqueeze(2).to_broadcast([128, 8, 64]), ALU.mult, [pyf, ea[c]], [y1])
                    tt('dve', v8(y2[:, :]), v8(pyb[:, :]), ea[c][:, 32 + 8 * g:32 + 8 * g + 8].unsqueeze(2).to_broadcast([128, 8, 64]), ALU.mult, [pyb, ea[c]], [y2])
                    cp('dve', yd[:, :], pyd[:, :], [pyd], [yd])
                    act(sz2[:, :], pz[:, :], AF.Silu, [pz], [sz2])

                def tail(c):
                    cc = slice(c * 128, (c + 1) * 128)
                    k = c % 2
                    y1, y2, yd, sz2, yn, ssc = y1_[k], y2_[k], yd_[k], sz2_[k], yn_[k], ssc_[k]
                    tt('pool', y1[:, :], y1[:, :], y2[:, :], ALU.add, [y1, y2], [y1])
                    tt('dve', v8(y2[:, :]), v8(x_tok[c][:, :]), dsum[:, 8 * g:8 * g + 8].unsqueeze(2).to_broadcast([128, 8, 64]), ALU.mult, [x_tok[c], dsum], [y2])
                    tt('pool', y1[:, :], y1[:, :], y2[:, :], ALU.add, [y1, y2], [y1])
                    tt('pool', y1[:, :], y1[:, :], yd[:, :], ALU.add, [y1, yd], [y1])
                    tt('pool', y1[:, :], y1[:, :], sz2[:, :], ALU.mult, [y1, sz2], [y1])
                    act(y2[:, :], y1[:, :], AF.Square, [y1], [y2, ssc], accum=ssc[:, 0:1])
                    act(ssc[:, 1:2], ssc[:, 0:1], AF.Ln, [ssc], [ssc], bias=EPS, scale=1.0 / 512)
                    act(ssc[:, 1:2], ssc[:, 1:2], AF.Exp, [ssc], [ssc], scale=-0.5)
                    stt(yn[:, :], y1[:, :], ssc[:, 1:2], gn_b[:, :], ALU.mult, ALU.mult, [y1, ssc, gn_b], [yn])
                    pt = nps()
                    for blk in range(4):
                        tr(pt[:, blk * 128:(blk + 1) * 128], yn[:, blk * 128:(blk + 1) * 128], ident_f[:, :], [yn, ident_f], [pt])
                    cp(alt('dve', 'act'), ynv[:, :, cc], pt[:, :].rearrange("p (k t) -> p k t", k=4), [pt], [ynT])

                if SPH >= 3:
                    head(0)
                for c in range(8 if SPH >= 3 else 0):
                    if c + 1 < 8:
                        head(c + 1)
                    tail(c)
                wo = D["ssd_w_out"]
                for oc2 in range(2 if SPH >= 4 else 0):
                    s = wload([(wo[g * 512:(g + 1) * 512, oc2 * 512:(oc2 + 1) * 512].rearrange("(kc p) n -> p kc n", p=128), 0, [128, 4, 512])])
                    sv = s[:, 0:2048].rearrange("p (kc n) -> p kc n", kc=4)
                    for o4 in range(4):
                        oc = oc2 * 4 + o4
                        for th in range(2):
                            cs = slice(th * 512, (th + 1) * 512)
                            p = nps()
                            for kc in range(4):
                                mm(p[:, :], sv[:, kc, o4 * 128:(o4 + 1) * 128], ynv[:, kc, cs], kc == 0, kc == 3, [s, ynT], [p])
                            stt(xT[oc][:, cs], p[:, :], mb[:, 16 + oc:17 + oc], xT[oc][:, cs], ALU.mult, ALU.add, [p, mb, xT[oc]], [xT[oc]])

        MIXERS['ssd'] = ssd_mixer


        if cfg.get('adaln', True):
            adaln(0)
        for i in range(nlayers):
            kind, j = i % 3, i // 3
            with ExitStack() as lst:
                if kind == 0 and en_mla:
                    norm_mod(i, 1)
                    MIXERS['mla'](i, j, lst)
                elif kind == 1 and en_conv:
                    norm_mod(i, 1)
                    MIXERS['conv'](i, j, lst)
                elif kind == 2 and en_ssd:
                    norm_mod(i, 1)
                    MIXERS['ssd'](i, j, lst)
                P.barrier()
            pump = adaln_gen(i + 1) if (i + 1 < nlayers and cfg.get('adaln', True)) else None
            if en_ffn:
                with ExitStack() as lst:
                    ffn(i, lst, pump)
                    if pump is not None:
                        for _ in pump:
                            pass
                    P.barrier()
            elif pump is not None:
                for _ in pump:
                    pass

        P.barrier()
        xstage = [P.sbuf([128, 1024], F32, f"xstageo{i}") for i in range(2)]
        for t in range(8):
            stg = xstage[t % 2]
            for half in range(2):
                p = nps()
                for q in range(4):
                    fc = half * 4 + q
                    tr(p[:, q * 128:(q + 1) * 128], xT[fc][:, t * 128:(t + 1) * 128], ident_f[:, :], [xT[fc], ident_f], [p])
                cp(alt('dve', 'act'), stg[:, half * 512:(half + 1) * 512], p[:, :], [p], [stg])
            P.dma('sp', D["y"][t * 128:(t + 1) * 128, :], stg[:, :], reads=[stg], sembuf=stg, is_output=True)
        P.finish()
        P.emit()
    nc._used_inputs = USED_INPUTS
    return nc, USED_INPUTS


def _partner(d):
    if d < 64:
        return d
    e = d - 64
    blk, r = e // 16, e % 16
    return 64 + blk * 16 + (r + 8) % 16


def _rope_tables(is_sample):
    T = 1024
    cosq = np.ones((96, T), np.float32)
    sinq = np.zeros((96, T), np.float32)
    if is_sample:
        t = np.arange(T)
        row = (t // 64).astype(np.float32)
        col = (t % 64).astype(np.float32)
        inv = (10000.0 ** (-np.arange(0, 16, 2, dtype=np.float32) / 16)).astype(np.float32)
        ang_r = row[None, :] * inv[:, None]
        ang_c = col[None, :] * inv[:, None]
        for blk, ang in ((0, ang_r), (1, ang_c)):
            c = np.cos(ang).astype(np.float32)
            s = np.sin(ang).astype(np.float32)
            b = 64 + blk * 16
            cosq[b:b + 8] = c
            cosq[b + 8:b + 16] = c
            sinq[b:b + 8] = -s
            sinq[b + 8:b + 16] = s
    ropeq = np.stack([cosq, sinq])
    cosk = np.ones((96, 1280), np.float32)
    sink = np.zeros((96, 1280), np.float32)
    cosk[:, 256:] = cosq
    sink[:, 256:] = sinq
    ropek = np.stack([cosk, sink])
    return ropeq, ropek


_NC_CACHE = {}


def _get_nc(cfg_key, cfg):
    if cfg_key not in _NC_CACHE:
        _NC_CACHE[cfg_key] = build(cfg)
    return _NC_CACHE[cfg_key]


def kernel(_cfg=None, **inp):
    f32 = np.float32
    g = {k: np.asarray(v) for k, v in inp.items()}
    cfg = _cfg or {}
    perm = np.array([h * 96 + _partner(d) for h in range(16) for d in range(96)])
    w_uq = np.ascontiguousarray(g["mla_w_uq"], f32)
    w_uq_sw = np.ascontiguousarray(w_uq[:, :, perm])
    wk = np.zeros((2, 384, 1536), f32)
    wk_sw = np.zeros((2, 384, 1536), f32)
    ukv = g["mla_w_ukv"].reshape(2, 256, 16, 128)
    for h in range(16):
        wk[:, 0:256, h * 96:h * 96 + 64] = ukv[:, :, h, 0:64]
        wk_sw[:, 0:256, h * 96:h * 96 + 64] = ukv[:, :, h, 0:64]
        for e in range(32):
            wk[:, 256 + e, h * 96 + 64 + e] = 1.0
            wk_sw[:, 256 + (_partner(64 + e) - 64), h * 96 + 64 + e] = 1.0
    w_ukv_v = np.ascontiguousarray(ukv[:, :, :, 64:128].reshape(2, 256, 1024))
    pq = np.array([_partner(d) for d in range(96)])

    def pad128(v):
        o = np.zeros((1, 128), f32)
        o[0, :v.shape[0]] = v
        return o

    tri = np.zeros((3, 128, 128), f32)
    k_ = np.arange(128)
    tri[0] = (k_[:, None] <= k_[None, :])
    tri[1] = (k_[:, None] >= k_[None, :])
    negmask = np.zeros((2, 128, 128), f32)
    negmask[0] = np.where(k_[None, :] >= k_[:, None], 0.0, NEG)
    negmask[1] = np.where(k_[None, :] <= k_[:, None], 0.0, NEG)
    ssd_rows = np.zeros((4, 64), f32)
    ssd_rows[0] = g["ssd_dt_bias"][0].reshape(64)
    ssd_rows[1] = g["ssd_a_log"][0].reshape(64)
    ssd_rows[2] = g["ssd_d"][0].reshape(64)

    shared = {
        "w_ada": g["w_ada"], "ffn_w_in": g["ffn_w_in"], "ffn_w_out": g["ffn_w_out"],
        "mla_w_dq": g["mla_w_dq"], "w_uq": w_uq, "w_uq_sw": w_uq_sw, "mla_w_dkv": g["mla_w_dkv"],
        "wk": wk, "wk_sw": wk_sw, "w_ukv_v": w_ukv_v, "mla_w_o": g["mla_w_o"],
        "cv_w_pw1": g["cv_w_pw1"][0], "cv_w_pw2": g["cv_w_pw2"][0],
        "ssd_w_in": g["ssd_w_in"][0], "ssd_w_out": g["ssd_w_out"][0],
        "ssd_rows": ssd_rows, "bconv_row": g["ssd_b_conv"][0].reshape(1, 3072), "gnorm_row": g["ssd_g_norm"][0].reshape(1, 2048),
        "tri": tri, "negmask": negmask,
    }
    shared = {k: np.ascontiguousarray(v, f32) for k, v in shared.items()}
    in_maps = []
    for c in range(8):
        is_sample = c >= 4
        m = dict(shared)
        if is_sample:
            b = c - 4
            m["x"] = np.ascontiguousarray(g["x_sample"][b], f32)
            cond = g["c"][b]
            m["cache_ckv"] = np.ascontiguousarray(g["cache_ckv"][b], f32)
            m["cache_kpe"] = np.ascontiguousarray(g["cache_kpe"][b], f32)
            m["h0"] = np.ascontiguousarray(g["state_ssm"][b, 0], f32)
            maskb = np.zeros((128, 40), f32)
            m["flag"] = np.ones((128, 1), f32)
        else:
            m["x"] = np.ascontiguousarray(g["x_prompt"][4 * c:4 * c + 4].reshape(1024, 1024), f32)
            cond = g["c_ctx"]
            m["cache_ckv"] = np.zeros((2, 256, 256), f32)
            m["cache_kpe"] = np.zeros((2, 256, 32), f32)
            m["h0"] = np.zeros((2, 32, 64, 128), f32)
            maskb = np.full((128, 40), NEG, f32)
            for kc in range(2, 10):
                maskb[:, kc * 4 + (kc - 2) // 2] = 0.0
            m["flag"] = np.zeros((128, 1), f32)
        m["maskb"] = maskb
        mq = np.zeros((8, 1024), f32)
        mk = np.zeros((8, 1280), f32)
        if not is_sample:
            for jj in range(4):
                mq[jj, jj * 256:(jj + 1) * 256] = 1.0
                mk[jj, :] = NEG
                mk[jj, 256 + jj * 256:256 + (jj + 1) * 256] = 0.0
        m["maskq"], m["maskk"] = mq, mk
        rq, rk = _rope_tables(is_sample)
        m["ropeq"], m["ropek"] = rq, rk
        rows = []
        for i in range(4):
            rows += [g["g_norm1"][i].reshape(8, 128), g["g_norm2"][i].reshape(8, 128), g["b_ada"][i].reshape(48, 128)]
        rows += [np.asarray(cond).reshape(8, 128)]
        for j in range(2):
            rows += [g["mla_g_q"][j].reshape(3, 128), g["mla_g_kv"][j].reshape(2, 128),
                     pad128(g["mla_g_qn"][j]), pad128(g["mla_g_qn"][j][pq]),
                     pad128(g["mla_g_kn"][j]), pad128(g["mla_g_kn"][j][pq])]
        rows += [g["cv_b_pw1"][0].reshape(16, 128), g["cv_w_dw"][0].reshape(248, 128), g["cv_b_dw"][0].reshape(8, 128),
                 g["cv_g_ln"][0].reshape(8, 128), g["cv_b_ln"][0].reshape(8, 128), g["cv_b_pw2"][0].reshape(8, 128)]
        rows += [g["ssd_w_conv"][0].reshape(120, 128), g["ssd_b_conv"][0].reshape(24, 128)]
        v = np.concatenate([np.asarray(r, f32) for r in rows], axis=0)
        vecs = np.zeros((NVEC, 128), f32)
        vecs[:v.shape[0]] = v
        m["vecs"] = vecs
        in_maps.append(m)

    nc, used = _get_nc(str(sorted(cfg.items())), cfg)
    in_maps = [{k: m[k] for k in used} for m in in_maps]
    res = run_bass_kernel_spmd(nc, in_maps, core_ids=list(range(8)))
    R = res.results
    y_prompt = np.stack([R[c]["y"].reshape(4, 256, 1024) for c in range(4)]).reshape(16, 256, 1024)
    y_sample = np.stack([R[4 + b]["y"] for b in range(4)])
    _z = {"ockv": np.zeros((2, 1024, 256), f32), "okpe": np.zeros((2, 1024, 32), f32), "ossm": np.zeros((4, 2, 32, 64, 128), f32)}
    R = [{k: (r[k] if k in r else _z[k]) for k in ("y", "ockv", "okpe", "ossm")} for r in R]
    new_ckv = np.stack([R[c]["ockv"].reshape(2, 4, 256, 256).transpose(1, 0, 2, 3) for c in range(4)]).reshape(16, 2, 256, 256)
    new_kpe = np.stack([R[c]["okpe"].reshape(2, 4, 256, 32).transpose(1, 0, 2, 3) for c in range(4)]).reshape(16, 2, 256, 32)
    new_ssm = np.stack([R[c]["ossm"] for c in range(4)]).reshape(16, 1, 2, 32, 64, 128)
    outs = (y_prompt.astype(f32), y_sample.astype(f32), np.ascontiguousarray(new_ckv, f32),
            np.ascontiguousarray(new_kpe, f32), np.ascontiguousarray(new_ssm, f32))
    return outs
```

```python
import numpy as np
import concourse.bass as bass
import concourse.mybir as mybir
from concourse.bass_utils import run_bass_kernel_spmd
from contextlib import ExitStack

F32 = mybir.dt.float32
BF16 = mybir.dt.bfloat16
AF = mybir.ActivationFunctionType
ALU = mybir.AluOpType
AX = mybir.AxisListType

ENGS = ['pe', 'act', 'dve', 'pool', 'sp']


class Buf:
    def __init__(self, t, name):
        self.t = t
        self.name = name
        self.writer = None
        self.readers = {}
        self.semkey = None
        self.dcount = 0
        self.is_psum = False

    def __getitem__(self, idx):
        return self.t[idx]


class Prog:
    def __init__(self, nc, stack, self_wait=True):
        self.nc = nc
        self.stack = stack
        self.ops = {e: [] for e in ENGS}
        self.count = {e: 0 for e in ENGS}
        self.semh = {}
        self.obs = {e: {} for e in ENGS}
        self.self_wait = self_wait
        for e in ENGS:
            self.semh[e] = stack.enter_context(nc.semaphore("sem_" + e))
        self.nbuf = 0
        self.out_tokens = []
        self.dma_final = {}
        self.used_names = set()
        self.eng = {'pe': nc.tensor, 'act': nc.scalar, 'dve': nc.vector, 'pool': nc.gpsimd, 'sp': nc.sync}

    def _emit(self, eng, waits, fn, inc):
        e = self.eng[eng]
        for (k, v) in waits:
            e.wait_ge(self.semh[k], v)
        if fn is not None:
            ins = fn(e)
            ins.then_inc(self.semh[inc[0]], inc[1])

    def sbuf(self, shape, dtype, name=None, stack=None):
        self.nbuf += 1
        name = name or f"sb{self.nbuf}"
        if name in self.used_names:
            name = f"{name}_u{self.nbuf}"
        self.used_names.add(name)
        t = (stack or self.stack).enter_context(self.nc.sbuf_tensor(name, list(shape), dtype))
        return Buf(t, name)

    def psum(self, shape, dtype=F32, name=None, stack=None):
        self.nbuf += 1
        name = name or f"ps{self.nbuf}"
        t = (stack or self.stack).enter_context(self.nc.psum_tensor(name, list(shape), dtype))
        b = Buf(t, name)
        b.is_psum = True
        return b

    def _dsem(self, b):
        if b.semkey is None:
            b.semkey = "d_" + b.name
            self.semh[b.semkey] = self.stack.enter_context(self.nc.semaphore("dsem_" + b.name))
        return b.semkey

    def _deps(self, eng, reads, writes):
        deps = set()
        for b in reads:
            if b.writer is not None:
                deps.add(b.writer)
            if b.is_psum:
                for rk, rt in b.readers.items():
                    if rk != eng:
                        deps.add(rt)
        for b in writes:
            if b.writer is not None:
                deps.add(b.writer)
            deps.update(b.readers.values())
        waits = []
        for (k, v) in sorted(deps, key=lambda kv: (str(kv[0]), kv[1])):
            if k == eng and (eng == 'pe' or not self.self_wait):
                continue
            if self.obs[eng].get(k, 0) < v:
                waits.append((k, v))
                self.obs[eng][k] = v
        return waits

    def op(self, eng, fn, reads=(), writes=()):
        waits = self._deps(eng, reads, writes)
        self.count[eng] += 1
        tok = (eng, self.count[eng])
        for b in reads:
            b.readers[eng] = tok
        for b in writes:
            b.writer = tok
            b.readers = {}
        self._emit(eng, waits, fn, (eng, 1))

    def dma(self, q, out_ap, in_ap, reads=(), writes=(), sembuf=None, is_output=False, **kw):
        waits = self._deps(q, reads, writes)
        k = self._dsem(sembuf)
        sembuf.dcount += 16
        tok = (k, sembuf.dcount)
        self.dma_final[k] = sembuf.dcount
        for b in reads:
            b.readers['dma_' + k] = tok
        for b in writes:
            b.writer = tok
            b.readers = {}
        if is_output:
            self.out_tokens.append(tok)
        fn = lambda e, o=out_ap, i=in_ap, kw=kw: e.dma_start(out=o, in_=i, **kw)
        self._emit(q, waits, fn, (k, 16))

    def barrier(self):
        for e in ENGS:
            waits = []
            for o in ENGS:
                if o == e or self.count[o] == 0:
                    continue
                if self.obs[e].get(o, 0) < self.count[o]:
                    waits.append((o, self.count[o]))
                    self.obs[e][o] = self.count[o]
            for k, v in self.dma_final.items():
                if self.obs[e].get(k, 0) < v:
                    waits.append((k, v))
                    self.obs[e][k] = v
            if waits:
                self._emit(e, waits, None, None)

    def finish(self):
        waits = []
        seen = {}
        for k, v in self.dma_final.items():
            seen[k] = max(seen.get(k, 0), v)
        for k, v in seen.items():
            waits.append((k, v))
        for o in ENGS:
            if o != 'sp' and self.count[o] > 0:
                waits.append((o, self.count[o]))
        self._emit('sp', waits, None, None)

    def emit(self):
        return
        nc = self.nc
        with nc.Block() as block:
            def run(eng_name, e):
                for (waits, fn, inc) in self.ops[eng_name]:
                    for (k, v) in waits:
                        e.wait_ge(self.semh[k], v)
                    if fn is not None:
                        ins = fn(e)
                        ins.then_inc(self.semh[inc[0]], inc[1])

            @block.tensor
            def _(e):
                run('pe', e)

            @block.scalar
            def _(e):
                run('act', e)

            @block.vector
            def _(e):
                run('dve', e)

            @block.gpsimd
            def _(e):
                run('pool', e)

            @block.sync
            def _(e):
                run('sp', e)


D_MODEL = 1024
NT = 1024
EPS = 1e-6
FFN_H = 2816
NEG = -30000.0

VEC_LAYOUT = []
for _i in range(4):
    VEC_LAYOUT += [(f"g1_{_i}", 8), (f"g2_{_i}", 8), (f"bada_{_i}", 48)]
VEC_LAYOUT += [("cond", 8)]
for _j in range(2):
    VEC_LAYOUT += [(f"gq_{_j}", 3), (f"gkv_{_j}", 2), (f"gqn_{_j}", 1), (f"gqnsw_{_j}", 1), (f"gkn_{_j}", 1), (f"gknsw_{_j}", 1)]
VEC_LAYOUT += [("bpw1", 16), ("wdw", 248), ("bdw", 8), ("gln", 8), ("bln", 8), ("bpw2", 8)]
VEC_LAYOUT += [("wconv", 120), ("bconv", 24)]
VEC_BASE = {}
_o = 0
for _n, _r in VEC_LAYOUT:
    VEC_BASE[_n] = _o
    _o += _r
NVEC = ((_o + 127) // 128) * 128


def build(cfg=None):
    cfg = cfg or {}
    en_mla = cfg.get("mla", True)
    en_conv = cfg.get("conv", True)
    en_ssd = cfg.get("ssd", True)
    en_ffn = cfg.get("ffn", True)
    nlayers = cfg.get("nlayers", 4)

    nc = bass.Bass("TRN2", target_bir_lowering=False)


    IN_SHAPES = {
        "x": [NT, 1024], "vecs": [NVEC, 128], "w_ada": [4, 1024, 6144],
        "ffn_w_in": [4, 1024, 5632], "ffn_w_out": [4, 2816, 1024],
        "mla_w_dq": [2, 1024, 384], "w_uq": [2, 384, 1536], "w_uq_sw": [2, 384, 1536],
        "mla_w_dkv": [2, 1024, 288], "wk": [2, 384, 1536], "wk_sw": [2, 384, 1536],
        "w_ukv_v": [2, 256, 1024], "mla_w_o": [2, 1024, 1024],
        "cache_ckv": [2, 256, 256], "cache_kpe": [2, 256, 32],
        "ropeq": [2, 96, 1024], "ropek": [2, 96, 1280], "maskb": [128, 40], "maskq": [8, 1024], "maskk": [8, 1280],
        "cv_w_pw1": [1024, 2048], "cv_w_pw2": [1024, 1024],
        "ssd_w_in": [1024, 5184], "ssd_w_out": [2048, 1024],
        "h0": [2, 32, 64, 128], "ssd_rows": [4, 64], "bconv_row": [1, 3072], "gnorm_row": [1, 2048],
        "flag": [128, 1], "tri": [3, 128, 128], "negmask": [2, 128, 128],
    }

    class _LazyD(dict):
        def __missing__(self, name):
            if name in OUT_SHAPES:
                ap = nc.dram_tensor(name, list(OUT_SHAPES[name]), F32, kind="ExternalOutput").ap()
                self[name] = ap
                return ap
            ap = nc.dram_tensor(name, list(IN_SHAPES[name]), F32, kind="ExternalInput").ap()
            self[name] = ap
            USED_INPUTS.append(name)
            return ap

    USED_INPUTS = []
    OUT_SHAPES = {"y": [NT, 1024], "ockv": [2, NT, 256], "okpe": [2, NT, 32], "ossm": [4, 2, 32, 64, 128]}
    D = _LazyD()
    with ExitStack() as st:
        P = Prog(nc, st)

        def mm(out, lhsT, rhs, start, stop, R, W):
            P.op('pe', lambda e: e.matmul(out, lhsT=lhsT, rhs=rhs, start=start, stop=stop), R, W)

        def tr(out, in_, ident, R, W):
            P.op('pe', lambda e: e.transpose(out=out, in_=in_, identity=ident), R, W)

        def act(out, in_, func, R, W, bias=None, scale=None, accum=None):
            kw = {}
            if bias is not None:
                kw['bias'] = bias
            if scale is not None:
                kw['scale'] = scale
            if accum is not None:
                kw['accum_out'] = accum
            P.op('act', lambda e: e.activation(out=out, in_=in_, func=func, **kw), R, W)

        def tt(eng, out, in0, in1, op, R, W):
            P.op(eng, lambda e: e.tensor_tensor(out=out, in0=in0, in1=in1, op=op), R, W)

        def ts(eng, out, in0, s1, s2, op0, op1, R, W):
            if op1 is None:
                P.op(eng, lambda e: e.tensor_scalar(out=out, in0=in0, scalar1=s1, scalar2=None, op0=op0), R, W)
            else:
                P.op(eng, lambda e: e.tensor_scalar(out=out, in0=in0, scalar1=s1, scalar2=s2, op0=op0, op1=op1), R, W)

        def stt(out, in0, scalar, in1, op0, op1, R, W):
            P.op('dve', lambda e: e.scalar_tensor_tensor(out=out, in0=in0, scalar=scalar, in1=in1, op0=op0, op1=op1), R, W)

        def cp(eng, out, in_, R, W):
            if eng == 'act':
                P.op('act', lambda e: e.copy(out=out, in_=in_), R, W)
            else:
                P.op(eng, lambda e: e.tensor_copy(out=out, in_=in_), R, W)

        def memset(eng, ap, val, W):
            P.op(eng, lambda e: e.memset(ap, val), (), W)

        _rr = {'n': 0}

        def alt(*engs):
            _rr['n'] += 1
            return engs[_rr['n'] % len(engs)]

        NPB = cfg.get("npsum", 8)
        PS = [P.psum([128, 512], F32, f"psb{i}") for i in range(NPB)]
        _ps = {'n': 0}

        def nps():
            _ps['n'] += 1
            return PS[_ps['n'] % (NPB - 3)]

        _pa = {'n': 0}

        def nacc():
            _pa['n'] += 1
            return PS[NPB - 2 + _pa['n'] % 2]

        xT = [P.sbuf([128, NT], F32, f"xT{i}") for i in range(8)]
        hT = [P.sbuf([128, NT], BF16, f"hT{i}") for i in range(8)]
        ident_f = P.sbuf([128, 128], F32, "ident_f")
        ident_b = P.sbuf([128, 128], BF16, "ident_b")
        ones_f = P.sbuf([128, 128], F32, "ones_f")
        ones_b = P.sbuf([128, 128], BF16, "ones_b")
        vcols = P.sbuf([128, NVEC], F32, "vcols")
        modb = [P.sbuf([128, 64], F32, f"modb{i}") for i in range(4)]
        s_bf = P.sbuf([128, 8], BF16, "s_bf")
        flag = P.sbuf([128, 1], F32, "flag_sb")
        NSLOT = 4
        SLOTW = 4096
        slots = [P.sbuf([128, SLOTW], BF16, f"wslot{i}") for i in range(NSLOT)]
        _sl = {'n': 0}

        def wload(dram_aps):
            _sl['n'] += 1
            s = slots[_sl['n'] % NSLOT]
            for (ap, off, shape) in dram_aps:
                n = 1
                for d in shape[1:]:
                    n *= d
                dst = s[:, off:off + n]
                if len(shape) == 3:
                    dst = dst.rearrange("p (a b) -> p a b", a=shape[1])
                P.dma('pool', dst, ap, writes=[s], sembuf=s)
            return s

        def vc(name, idx=0, n=1):
            b = VEC_BASE[name] + idx
            return vcols[:, b:b + n]

        sq_r = [P.sbuf([128, 512], BF16, f"sq_r{i}") for i in range(4)]
        rstd_t = P.sbuf([128, 512], F32, "rstd_t")
        tmp_r = [P.sbuf([128, 512], F32, f"tmp_r{i}") for i in range(3)]
        _tm = {'n': 0}

        def ntmp():
            _tm['n'] += 1
            return tmp_r[_tm['n'] % 3]

        memset('pool', ident_f[:, :], 1.0, [ident_f])
        P.op('pool', lambda e: e.affine_select(out=ident_f[:, :], in_=ident_f[:, :], pattern=[[-1, 128]],
                                               compare_op=ALU.is_equal, fill=0.0, base=0, channel_multiplier=1),
             [ident_f], [ident_f])
        cp('pool', ident_b[:, :], ident_f[:, :], [ident_f], [ident_b])
        memset('pool', ones_f[:, :], 1.0, [ones_f])
        memset('pool', ones_b[:, :], 1.0, [ones_b])
        P.dma('sp', flag[:, :], D["flag"], writes=[flag], sembuf=flag)

        s0 = ExitStack()
        xstage = [P.sbuf([128, 1024], F32, f"xstage{i}", stack=s0) for i in range(2)]
        for blk in range(NVEC // 128 if cfg.get('stop', 9) > 1 else 0):
            stg = xstage[blk % 2]
            P.dma('sp', stg[:, 0:128], D["vecs"][blk * 128:(blk + 1) * 128, :], writes=[stg], sembuf=stg)
            p = nps()
            tr(p[:, 0:128], stg[:, 0:128], ident_f[:, :], [stg, ident_f], [p])
            cp(alt('dve', 'act'), vcols[:, blk * 128:(blk + 1) * 128], p[:, 0:128], [p], [vcols])
        if cfg.get('stop', 9) > 2:
            act(s_bf[:, :], vc("cond", 0, 8), AF.Silu, [vcols], [s_bf])

        for t in range(cfg.get('nx', 8) if cfg.get('stop', 9) > 3 else 0):
            stg = xstage[t % 2]
            if cfg.get('xsplit', 1) == 1:
                P.dma('sp', stg[:, :], D["x"][t * 128:(t + 1) * 128, :], writes=[stg], sembuf=stg)
            else:
                for q8 in range(8):
                    P.dma('sp', stg[:, q8 * 128:(q8 + 1) * 128], D["x"][t * 128:(t + 1) * 128, q8 * 128:(q8 + 1) * 128], writes=[stg], sembuf=stg)
            for half in range(0 if cfg.get('noxt') else cfg.get('nhalf', 2)):
                p = nps()
                for q in range(cfg.get('nq', 4)):
                    fc = half * 4 + q
                    tr(p[:, q * 128:(q + 1) * 128], stg[:, fc * 128:(fc + 1) * 128], ident_f[:, :], [stg, ident_f], [p])
                for q in range(cfg.get('nq', 4) if not cfg.get('nocp') else 0):
                    fc = half * 4 + q
                    cp(alt('dve', 'act') if not cfg.get('cpdve') else 'dve', xT[fc][:, t * 128:(t + 1) * 128], p[:, q * 128:(q + 1) * 128], [p], [xT[fc]])

        P.barrier()
        s0.close()

        def adaln_gen(i):
            p = PS[NPB - 3]
            for k in range(12):
                s = wload([(D["w_ada"][i, :, k * 512:(k + 1) * 512].rearrange("(kc p) n -> p kc n", p=128), 0, [128, 8, 512])])
                sv = s[:, :].rearrange("p (kc n) -> p kc n", kc=8)
                for o4 in range(4):
                    oc = k * 4 + o4
                    for kc in range(8):
                        mm(p[:, oc:oc + 1], sv[:, kc, o4 * 128:(o4 + 1) * 128], s_bf[:, kc:kc + 1], kc == 0, kc == 7, [s, s_bf], [p])
                yield
            mb = modb[i]
            tt('dve', mb[:, 0:48], p[:, 0:48], vc(f"bada_{i}", 0, 48), ALU.add, [p, vcols], [mb])
            stt(mb[:, 48:56], mb[:, 8:16], 1.0, vc(f"g1_{i}", 0, 8), ALU.add, ALU.mult, [mb, vcols], [mb])
            stt(mb[:, 56:64], mb[:, 32:40], 1.0, vc(f"g2_{i}", 0, 8), ALU.add, ALU.mult, [mb, vcols], [mb])
            ts('dve', mb[:, 48:64], mb[:, 48:64], 32.0, None, ALU.mult, None, [mb], [mb])

        def adaln(i):
            for _ in adaln_gen(i):
                pass

        def norm_mod(i, which):
            mb = modb[i]
            acol = 48 if which == 1 else 56
            bcol = 0 if which == 1 else 24
            for th in range(2):
                cs = slice(th * 512, (th + 1) * 512)
                p = nps()
                for fc in range(8):
                    sq = sq_r[fc % 4]
                    act(sq[:, :], xT[fc][:, cs], AF.Square, [xT[fc]], [sq])
                    mm(p[:, :], ones_b[:, :], sq[:, :], fc == 0, fc == 7, [ones_b, sq], [p])
                act(rstd_t[:, :], p[:, :], AF.Ln, [p], [rstd_t], bias=1024.0 * EPS)
                act(rstd_t[:, :], rstd_t[:, :], AF.Exp, [rstd_t], [rstd_t], scale=-0.5)
                for fc in range(8):
                    tm = ntmp()
                    stt(tm[:, :], xT[fc][:, cs], mb[:, acol + fc:acol + fc + 1], rstd_t[:, :], ALU.mult, ALU.mult,
                        [xT[fc], mb, rstd_t], [tm])
                    act(hT[fc][:, cs], tm[:, :], AF.Identity, [tm, mb], [hT[fc]], bias=mb[:, bcol + fc:bcol + fc + 1])

        def ffn(i, lst, pump=None):
            norm_mod(i, 2)
            gT = [P.sbuf([128, NT], BF16, f"gT{i}_{k}", stack=lst) for k in range(22)]
            sa_r = [P.sbuf([128, 512], F32, f"sa{i}_{k}", stack=lst) for k in range(2)]
            mb = modb[i]
            win = D["ffn_w_in"]
            for hb in range(11):
                s = wload([
                    (win[i, :, hb * 256:(hb + 1) * 256].rearrange("(kc p) n -> p kc n", p=128), 0, [128, 8, 256]),
                    (win[i, :, 2816 + hb * 256:2816 + (hb + 1) * 256].rearrange("(kc p) n -> p kc n", p=128), 2048, [128, 8, 256]),
                ])
                sa_v = s[:, 0:2048].rearrange("p (kc n) -> p kc n", kc=8)
                su_v = s[:, 2048:4096].rearrange("p (kc n) -> p kc n", kc=8)
                for sub in range(2):
                    hc = hb * 2 + sub
                    for th in range(2):
                        cs = slice(th * 512, (th + 1) * 512)
                        pa = nps()
                        for kc in range(8):
                            mm(pa[:, :], sa_v[:, kc, sub * 128:(sub + 1) * 128], hT[kc][:, cs], kc == 0, kc == 7, [s, hT[kc]], [pa])
                        pu = nps()
                        for kc in range(8):
                            mm(pu[:, :], su_v[:, kc, sub * 128:(sub + 1) * 128], hT[kc][:, cs], kc == 0, kc == 7, [s, hT[kc]], [pu])
                        sa = sa_r[(hc * 2 + th) % 2]
                        act(sa[:, :], pa[:, :], AF.Silu, [pa], [sa])
                        tt('dve', gT[hc][:, cs], sa[:, :], pu[:, :], ALU.mult, [sa, pu], [gT[hc]])
                if pump is not None:
                    next(pump, None)
            wout = D["ffn_w_out"]
            for oc in range(8):
                s1 = wload([(wout[i, 0:1408, oc * 128:(oc + 1) * 128].rearrange("(kc p) n -> p kc n", p=128), 0, [128, 11, 128])])
                s2 = wload([(wout[i, 1408:2816, oc * 128:(oc + 1) * 128].rearrange("(kc p) n -> p kc n", p=128), 0, [128, 11, 128])])
                v1 = s1[:, 0:1408].rearrange("p (kc n) -> p kc n", kc=11)
                v2 = s2[:, 0:1408].rearrange("p (kc n) -> p kc n", kc=11)
                for th in range(2):
                    cs = slice(th * 512, (th + 1) * 512)
                    p = nps()
                    for hc in range(22):
                        sv, ss = (v1, s1) if hc < 11 else (v2, s2)
                        mm(p[:, :], sv[:, hc % 11, :], gT[hc][:, cs], hc == 0, hc == 21, [ss, gT[hc]], [p])
                    stt(xT[oc][:, cs], p[:, :], mb[:, 40 + oc:41 + oc], xT[oc][:, cs], ALU.mult, ALU.add, [p, mb, xT[oc]], [xT[oc]])

        MIXERS = {}
        def conv_mixer(i, j, lst):
            mb = modb[i]
            ubuf = [P.sbuf([128, 4 * 286], BF16, f"ubuf{c}", stack=lst) for c in range(8)]
            vbuf = [P.sbuf([128, NT], F32, f"vbuf{c}", stack=lst) for c in range(8)]
            sg_r = [P.sbuf([128, 512], F32, f"sg{k}", stack=lst) for k in range(2)]
            DwE = [P.sbuf([128, 16 * 128], BF16, f"DwE{k}", stack=lst) for k in range(2)]
            DwO = [P.sbuf([128, 15 * 128], BF16, f"DwO{k}", stack=lst) for k in range(2)]
            mean_t = P.sbuf([128, 512], F32, "cv_mean", stack=lst)
            var_t = P.sbuf([128, 512], F32, "cv_var", stack=lst)
            w1 = D["cv_w_pw1"]
            for c in range(8):
                memset('pool', ubuf[c][:, :], 0.0, [ubuf[c]])
            WS = {}

            def tap(cc, w):
                if w % 2 == 0:
                    return DwE[cc % 2], DwE[cc % 2][:, (w // 2) * 128:(w // 2 + 1) * 128]
                return DwO[cc % 2], DwO[cc % 2][:, (w // 2) * 128:(w // 2 + 1) * 128]

            def build_dw(cc):
                for w in range(31):
                    b, ap = tap(cc, w)
                    if w % 2 == 0:
                        ts('dve', ap, ident_f[:, :], vc("wdw", w * 8 + cc), None, ALU.mult, None, [ident_f, vcols], [b])
                    else:
                        act(ap, ident_f[:, :], AF.Identity, [ident_f, vcols], [b], scale=vc("wdw", w * 8 + cc))

            def pw1(cc):
                c2, sub = cc // 2, cc % 2
                if sub == 0:
                    WS[c2] = wload([
                        (w1[:, c2 * 256:(c2 + 1) * 256].rearrange("(kc p) n -> p kc n", p=128), 0, [128, 8, 256]),
                        (w1[:, 1024 + c2 * 256:1024 + (c2 + 1) * 256].rearrange("(kc p) n -> p kc n", p=128), 2048, [128, 8, 256]),
                    ])
                s = WS[c2]
                sa_v = s[:, 0:2048].rearrange("p (kc n) -> p kc n", kc=8)
                sg_v = s[:, 2048:4096].rearrange("p (kc n) -> p kc n", kc=8)
                ub = ubuf[cc][:, :].rearrange("p (s t) -> p s t", s=4)
                for th in range(2):
                    cs = slice(th * 512, (th + 1) * 512)
                    pa = nps()
                    for kc in range(8):
                        mm(pa[:, :], sa_v[:, kc, sub * 128:(sub + 1) * 128], hT[kc][:, cs], kc == 0, kc == 7, [s, hT[kc]], [pa])
                    pg = nps()
                    for kc in range(8):
                        mm(pg[:, :], sg_v[:, kc, sub * 128:(sub + 1) * 128], hT[kc][:, cs], kc == 0, kc == 7, [s, hT[kc]], [pg])
                    sg = sg_r[th]
                    act(sg[:, :], pg[:, :], AF.Sigmoid, [pg, vcols], [sg], bias=vc("bpw1", 8 + cc))
                    stt(ub[:, 2 * th:2 * th + 2, 15:271], pa[:, :].rearrange("p (s t) -> p s t", s=2), vc("bpw1", cc),
                        sg[:, :].rearrange("p (s t) -> p s t", s=2), ALU.add, ALU.mult, [pa, sg, vcols], [ubuf[cc]])
                ts('dve', ub[:, 1:4, 0:15], ub[:, 0:3, 256:271], flag[:, 0:1], None, ALU.mult, None, [ubuf[cc], flag], [ubuf[cc]])
                ts('dve', ub[:, 0:3, 271:286], ub[:, 1:4, 15:30], flag[:, 0:1], None, ALU.mult, None, [ubuf[cc], flag], [ubuf[cc]])

            def dconv(cc):
                ub = ubuf[cc][:, :].rearrange("p (s t) -> p s t", s=4)
                for sp in range(2):
                    p = nps()
                    for w in range(31):
                        b, ap = tap(cc, w)
                        mm(p[:, :], ap, ub[:, 2 * sp:2 * sp + 2, w:w + 256], w == 0, w == 30, [b, ubuf[cc]], [p])
                    act(vbuf[cc][:, sp * 512:(sp + 1) * 512], p[:, :], AF.Identity, [p, vcols], [vbuf[cc]], bias=vc("bdw", cc))

            pw1(0)
            build_dw(0)
            for cc in range(8):
                if cc + 1 < 8:
                    pw1(cc + 1)
                    build_dw(cc + 1)
                dconv(cc)
            for th in range(2):
                cs = slice(th * 512, (th + 1) * 512)
                p1 = nps()
                for cc in range(8):
                    mm(p1[:, :], ones_f[:, :], vbuf[cc][:, cs], cc == 0, cc == 7, [ones_f, vbuf[cc]], [p1])
                p2 = nps()
                for cc in range(8):
                    sq = sq_r[cc % 2]
                    act(sq[:, :], vbuf[cc][:, cs], AF.Square, [vbuf[cc]], [sq])
                    mm(p2[:, :], ones_b[:, :], sq[:, :], cc == 0, cc == 7, [ones_b, sq], [p2])
                act(mean_t[:, :], p1[:, :], AF.Identity, [p1], [mean_t], scale=1.0 / 1024)
                tt('dve', var_t[:, :], mean_t[:, :], mean_t[:, :], ALU.mult, [mean_t], [var_t])
                stt(var_t[:, :], p2[:, :], 1.0 / 1024, var_t[:, :], ALU.mult, ALU.subtract, [p2, var_t], [var_t])
                act(rstd_t[:, :], var_t[:, :], AF.Ln, [var_t], [rstd_t], bias=EPS)
                act(rstd_t[:, :], rstd_t[:, :], AF.Exp, [rstd_t], [rstd_t], scale=-0.5)
                for cc in range(8):
                    tm = ntmp()
                    tt('dve', tm[:, :], vbuf[cc][:, cs], mean_t[:, :], ALU.subtract, [vbuf[cc], mean_t], [tm])
                    tt('pool', tm[:, :], tm[:, :], rstd_t[:, :], ALU.mult, [tm, rstd_t], [tm])
                    act(hT[cc][:, cs], tm[:, :], AF.Silu, [tm, vcols], [hT[cc]], bias=vc("bln", cc), scale=vc("gln", cc))
            w2 = D["cv_w_pw2"]
            for oc2 in range(2):
                s = wload([(w2[:, oc2 * 512:(oc2 + 1) * 512].rearrange("(kc p) n -> p kc n", p=128), 0, [128, 8, 512])])
                sv = s[:, :].rearrange("p (kc n) -> p kc n", kc=8)
                for o4 in range(4):
                    oc = oc2 * 4 + o4
                    for th in range(2):
                        cs = slice(th * 512, (th + 1) * 512)
                        p = nps()
                        for kc in range(8):
                            mm(p[:, :], sv[:, kc, o4 * 128:(o4 + 1) * 128], hT[kc][:, cs], kc == 0, kc == 7, [s, hT[kc]], [p])
                        tm = ntmp()
                        ts('dve', tm[:, :], p[:, :], vc("bpw2", oc), None, ALU.add, None, [p, vcols], [tm])
                        stt(xT[oc][:, cs], tm[:, :], mb[:, 16 + oc:17 + oc], xT[oc][:, cs], ALU.mult, ALU.add, [tm, mb, xT[oc]], [xT[oc]])

        MIXERS['conv'] = conv_mixer

        def mla_mixer(i, j, lst):
            mb = modb[i]
            cqf = [P.sbuf([128, 512], F32, f"cqf{k}", stack=lst) for k in range(3)]
            cqn = [P.sbuf([128, NT], BF16, f"cqn{k}", stack=lst) for k in range(3)]
            kvl = [P.sbuf([128, 1280], BF16, f"kvl{k}", stack=lst) for k in range(3)]
            rq = P.sbuf([96, 2048], F32, "rq", stack=lst)
            rk = P.sbuf([96, 2560], F32, "rk", stack=lst)
            mkb = P.sbuf([128, 40], F32, "mkb", stack=lst)
            cstg = P.sbuf([128, 2 * 288], F32, "cstg", stack=lst)
            ostg = [P.sbuf([128, 288], F32, f"ostg{k}", stack=lst) for k in range(2)]
            qf_r = [P.sbuf([104, NT], BF16, f"qf{k}", stack=lst) for k in range(2)]
            kf_r = [P.sbuf([104, 1280], BF16, f"kf{k}", stack=lst) for k in range(2)]
            vaug = P.sbuf([128, 10 * 4 * 128], BF16, "vaug", stack=lst)
            E_r = [P.sbuf([128, 512], BF16, f"E{k}", stack=lst) for k in range(4)]
            sqh = [P.sbuf([96, 512], BF16, f"sqh{k}", stack=lst) for k in range(2)]
            rs_h = [P.sbuf([96, 512], F32, f"rsh{k}", stack=lst) for k in range(2)]
            rec = [P.sbuf([128, 512], F32, f"rec{k}", stack=lst) for k in range(2)]
            oT = hT
            cnt = {'e': 0, 'h': 0}

            P.dma('sp', rq[:, 0:1024], D["ropeq"][0], writes=[rq], sembuf=rq)
            P.dma('sp', rq[:, 1024:2048], D["ropeq"][1], writes=[rq], sembuf=rq)
            P.dma('sp', rk[:, 0:1280], D["ropek"][0], writes=[rk], sembuf=rk)
            P.dma('sp', rk[:, 1280:2560], D["ropek"][1], writes=[rk], sembuf=rk)
            cv = cstg[:, :].rearrange("p (t f) -> p t f", t=2)
            P.dma('sp', cv[:, :, 0:256], D["cache_ckv"][j].rearrange("(t p) f -> p t f", p=128), writes=[cstg], sembuf=cstg)
            P.dma('sp', cv[:, :, 256:288], D["cache_kpe"][j].rearrange("(t p) f -> p t f", p=128), writes=[cstg], sembuf=cstg)

            def vcp(name, n=96):
                b = VEC_BASE[name]
                return vcols[0:n, b:b + 1]
            ts('dve', rq[:, 0:1024], rq[:, 0:1024], vcp(f"gqn_{j}"), None, ALU.mult, None, [rq, vcols], [rq])
            ts('dve', rq[:, 1024:2048], rq[:, 1024:2048], vcp(f"gqnsw_{j}"), None, ALU.mult, None, [rq, vcols], [rq])
            ts('dve', rk[:, 0:1280], rk[:, 0:1280], vcp(f"gkn_{j}"), None, ALU.mult, None, [rk, vcols], [rk])
            ts('dve', rk[:, 1280:2560], rk[:, 1280:2560], vcp(f"gknsw_{j}"), None, ALU.mult, None, [rk, vcols], [rk])
            memset('pool', vaug[:, :], 1.0, [vaug])
            for k in range(2):
                P.dma('pool', qf_r[k][96:104, :], D["maskq"], writes=[qf_r[k]], sembuf=qf_r[k])
                P.dma('pool', kf_r[k][96:104, :], D["maskk"], writes=[kf_r[k]], sembuf=kf_r[k])

            sdq = wload([(D["mla_w_dq"][j].rearrange("(kc p) n -> p kc n", p=128), 0, [128, 8, 384])])
            dqv = sdq[:, 0:3072].rearrange("p (kc n) -> p kc n", kc=8)
            sdkv = wload([(D["mla_w_dkv"][j].rearrange("(kc p) n -> p kc n", p=128), 0, [128, 8, 288])])
            dkvv = sdkv[:, 0:2304].rearrange("p (kc n) -> p kc n", kc=8)
            for th in range(2):
                cs = slice(th * 512, (th + 1) * 512)
                pss = nacc()
                for oc in range(3):
                    p = nps()
                    for kc in range(8):
                        mm(p[:, :], dqv[:, kc, oc * 128:(oc + 1) * 128], hT[kc][:, cs], kc == 0, kc == 7, [sdq, hT[kc]], [p])
                    cp('dve', cqf[oc][:, :], p[:, :], [p], [cqf[oc]])
                    sq = sq_r[oc % 2]
                    act(sq[:, :], p[:, :], AF.Square, [p], [sq])
                    mm(pss[:, :], ones_b[:, :], sq[:, :], oc == 0, oc == 2, [ones_b, sq], [pss])
                act(rstd_t[:, :], pss[:, :], AF.Ln, [pss], [rstd_t], bias=EPS, scale=1.0 / 384)
                act(rstd_t[:, :], rstd_t[:, :], AF.Exp, [rstd_t], [rstd_t], scale=-0.5)
                for oc in range(3):
                    stt(cqn[oc][:, cs], cqf[oc][:, :], vc(f"gq_{j}", oc), rstd_t[:, :], ALU.mult, ALU.mult, [cqf[oc], vcols, rstd_t], [cqn[oc]])
                pss = nacc()
                ckvf = [ntmp(), ntmp()]
                for oc in range(2):
                    p = nps()
                    for kc in range(8):
                        mm(p[:, :], dkvv[:, kc, oc * 128:(oc + 1) * 128], hT[kc][:, cs], kc == 0, kc == 7, [sdkv, hT[kc]], [p])
                    cp('dve', ckvf[oc][:, :], p[:, :], [p], [ckvf[oc]])
                    sq = sq_r[oc % 2]
                    act(sq[:, :], p[:, :], AF.Square, [p], [sq])
                    mm(pss[:, :], ones_b[:, :], sq[:, :], oc == 0, oc == 1, [ones_b, sq], [pss])
                pk = nps()
                for kc in range(8):
                    mm(pk[0:32, :], dkvv[:, kc, 256:288], hT[kc][:, cs], kc == 0, kc == 7, [sdkv, hT[kc]], [pk])
                kpef = ntmp()
                cp('dve', kpef[0:32, :], pk[0:32, :], [pk], [kpef])
                cp('act', kvl[2][0:32, 256 + th * 512:256 + (th + 1) * 512], pk[0:32, :], [pk], [kvl[2]])
                act(rstd_t[:, :], pss[:, :], AF.Ln, [pss], [rstd_t], bias=EPS, scale=1.0 / 256)
                act(rstd_t[:, :], rstd_t[:, :], AF.Exp, [rstd_t], [rstd_t], scale=-0.5)
                for oc in range(2):
                    stt(ckvf[oc][:, :], ckvf[oc][:, :], vc(f"gkv_{j}", oc), rstd_t[:, :], ALU.mult, ALU.mult, [ckvf[oc], vcols, rstd_t], [ckvf[oc]])
                    cp('act', kvl[oc][:, 256 + th * 512:256 + (th + 1) * 512], ckvf[oc][:, :], [ckvf[oc]], [kvl[oc]])
                for t4 in range(4):
                    t = th * 4 + t4
                    p = nps()
                    for oc in range(2):
                        tr(p[:, oc * 128:(oc + 1) * 128], ckvf[oc][:, t4 * 128:(t4 + 1) * 128], ident_f[:, :], [ckvf[oc], ident_f], [p])
                    tr(p[:, 256:288], kpef[0:32, t4 * 128:(t4 + 1) * 128], ident_f[0:32, 0:32], [kpef, ident_f], [p])
                    og = ostg[t % 2]
                    cp(alt('dve', 'act'), og[:, :], p[:, 0:288], [p], [og])
                    P.dma('sp', D["ockv"][j, t * 128:(t + 1) * 128, :], og[:, 0:256], reads=[og], sembuf=og, is_output=True)
                    P.dma('sp', D["okpe"][j, t * 128:(t + 1) * 128, :], og[:, 256:288], reads=[og], sembuf=og, is_output=True)

            for t in range(2):
                p = nps()
                for fc in range(2):
                    tr(p[:, fc * 128:(fc + 1) * 128], cv[:, t, fc * 128:(fc + 1) * 128], ident_f[:, :], [cstg, ident_f], [p])
                tr(p[0:32, 256:384], cv[:, t, 256:288], ident_f[:, :], [cstg, ident_f], [p])
                for fc in range(2):
                    cp(alt('dve', 'act'), kvl[fc][:, t * 128:(t + 1) * 128], p[:, fc * 128:(fc + 1) * 128], [p], [kvl[fc]])
                cp('dve', kvl[2][0:32, t * 128:(t + 1) * 128], p[0:32, 256:384], [p], [kvl[2]])

            vv = vaug[:, :].rearrange("p (k a b c) -> p k a b c", k=10, a=2, b=2)
            tpe = P.sbuf([96, 1280], F32, "tpe", stack=lst)
            KB = [(0, 512), (512, 512), (1024, 256)]
            for hg in range(4):
                sA = wload([
                    (D["w_uq"][j, :, hg * 384:(hg + 1) * 384].rearrange("(kc p) n -> p kc n", p=128), 0, [128, 3, 384]),
                    (D["w_uq_sw"][j, :, hg * 384:(hg + 1) * 384].rearrange("(kc p) n -> p kc n", p=128), 1152, [128, 3, 384]),
                    (D["w_ukv_v"][j, :, hg * 256:(hg + 1) * 256].rearrange("(kc p) n -> p kc n", p=128), 2304, [128, 2, 256]),
                ])
                wqv = sA[:, 0:1152].rearrange("p (kc n) -> p kc n", kc=3)
                wqsv = sA[:, 1152:2304].rearrange("p (kc n) -> p kc n", kc=3)
                wvv = sA[:, 2304:2816].rearrange("p (kc n) -> p kc n", kc=2)
                sB = wload([
                    (D["wk"][j, :, hg * 384:(hg + 1) * 384].rearrange("(kc p) n -> p kc n", p=128), 0, [128, 3, 384]),
                    (D["wk_sw"][j, :, hg * 384:(hg + 1) * 384].rearrange("(kc p) n -> p kc n", p=128), 1152, [128, 3, 384]),
                ])
                wkv = sB[:, 0:1152].rearrange("p (kc n) -> p kc n", kc=3)
                wksv = sB[:, 1152:2304].rearrange("p (kc n) -> p kc n", kc=3)
                for kc in range(10):
                    p = nps()
                    for c2 in range(2):
                        mm(p[:, 0:256], kvl[c2][:, kc * 128:(kc + 1) * 128], wvv[:, c2, :], c2 == 0, c2 == 1, [kvl[c2], sA], [p])
                    pv4 = p[:, 0:256].rearrange("p (a b c) -> p a b c", a=2, b=2)
                    cp('dve', vv[:, kc, :, 0, 0:64], pv4[:, :, 0, :], [p], [vaug])
                    cp('act', vv[:, kc, :, 1, 64:128], pv4[:, :, 1, :], [p], [vaug])
                ATT = [PS[0], PS[1], PS[2]]
                NRB = [PS[3], PS[4], PS[5]]

                def nr_gen(dst, dcols, mma, mmb, n, tab, toff0, toff1, tcols, tpe=None):
                    pa, pb, pss = NRB
                    for (l, r, st_, sp_, R_) in mma:
                        mm(pa[0:96, 0:n], l, r, st_, sp_, R_, [pa])
                    if tpe is None:
                        for (l, r, st_, sp_, R_) in mmb:
                            mm(pb[0:96, 0:n], l, r, st_, sp_, R_, [pb])
                    k2 = cnt['h'] % 2
                    cnt['h'] += 1
                    sq = sqh[k2]
                    act(sq[:, 0:n], pa[0:96, 0:n], AF.Square, [pa], [sq])
                    yield
                    mm(pss[0:96, 0:n], ones_b[0:96, 0:96], sq[:, 0:n], True, True, [ones_b, sq], [pss])
                    rs = rs_h[k2]
                    act(rs[:, 0:n], pss[0:96, 0:n], AF.Ln, [pss], [rs], bias=EPS, scale=1.0 / 96)
                    act(rs[:, 0:n], rs[:, 0:n], AF.Exp, [rs], [rs], scale=-0.5)
                    t1 = ntmp()
                    if tpe is None:
                        t2 = ntmp()
                        tt('dve', t1[0:96, 0:n], pa[0:96, 0:n], tab[:, toff0 + tcols.start:toff0 + tcols.stop], ALU.mult, [pa, tab], [t1])
                        tt('dve', t2[0:96, 0:n], pb[0:96, 0:n], tab[:, toff1 + tcols.start:toff1 + tcols.stop], ALU.mult, [pb, tab], [t2])
                        tt('pool', t1[0:96, 0:n], t1[0:96, 0:n], t2[0:96, 0:n], ALU.add, [t1, t2], [t1])
                        tt('pool', dst[0:96, dcols], t1[0:96, 0:n], rs[:, 0:n], ALU.mult, [t1, rs], [dst])
                    else:
                        tt('dve', t1[0:64, 0:n], pa[0:64, 0:n], tab[0:64, toff0 + tcols.start:toff0 + tcols.stop], ALU.mult, [pa, tab], [t1])
                        tt('pool', dst[0:64, dcols], t1[0:64, 0:n], rs[0:64, 0:n], ALU.mult, [t1, rs], [dst])
                        tt('pool', dst[64:96, dcols], tpe[64:96, tcols], rs[64:96, 0:n], ALU.mult, [tpe, rs], [dst])
                    yield

                def head_feeder(hl):
                    h = hg * 4 + hl
                    qf = qf_r[h % 2]
                    kf = kf_r[h % 2]
                    for th in range(2):
                        cs = slice(th * 512, (th + 1) * 512)
                        mma = [(wqv[:, kc, hl * 96:(hl + 1) * 96], cqn[kc][:, cs], kc == 0, kc == 2, [sA, cqn[kc]]) for kc in range(3)]
                        mmb = [(wqsv[:, kc, hl * 96:(hl + 1) * 96], cqn[kc][:, cs], kc == 0, kc == 2, [sA, cqn[kc]]) for kc in range(3)]
                        yield from nr_gen(qf, cs, mma, mmb, 512, rq, 0, 1024, cs)
                    for (k0, n) in KB:
                        ks = slice(k0, k0 + n)
                        mma = []
                        mmb = []
                        for kc in range(3):
                            ksz = 128 if kc < 2 else 32
                            mma.append((wkv[0:ksz, kc, hl * 96:(hl + 1) * 96], kvl[kc][0:ksz, ks], kc == 0, kc == 2, [sB, kvl[kc]]))
                            mmb.append((wksv[0:ksz, kc, hl * 96:(hl + 1) * 96], kvl[kc][0:ksz, ks], kc == 0, kc == 2, [sB, kvl[kc]]))
                        yield from nr_gen(kf, ks, mma, mmb, n, rk, 0, 1280, ks)

                def attention(hl, feeder):
                    h = hg * 4 + hl
                    qf = qf_r[h % 2]
                    kf = kf_r[h % 2]
                    a, b = hl // 2, hl % 2
                    for th in range(2):
                        cs = slice(th * 512, (th + 1) * 512)
                        po = nacc()
                        psts = {}

                        def issue_qk(kc):
                            pst = ATT[cnt['p'] % 3]
                            cnt['p'] += 1
                            mm(pst[:, :], kf[0:104, kc * 128:(kc + 1) * 128], qf[0:104, cs], True, True, [kf, qf], [pst])
                            psts[kc] = pst
                        issue_qk(0)
                        issue_qk(1)
                        for kc in range(10):
                            pst = psts[kc]
                            E = E_r[cnt['e'] % 4]
                            cnt['e'] += 1
                            act(E[:, :], pst[:, :], AF.Exp, [pst], [E], scale=float(96.0 ** -0.5))
                            if kc + 2 < 10:
                                issue_qk(kc + 2)
                            mm(po[:, :], vv[:, kc, a, b, :], E[:, :], kc == 0, kc == 9, [vaug, E], [po])
                            if feeder is not None:
                                next(feeder, None)
                        rc = rec[th]
                        if b == 0:
                            P.op('dve', lambda e, rc=rc, po=po: e.reciprocal(out=rc[0:64, :], in_=po[64:128, :]), [po], [rc])
                            tt('dve', oT[h // 2][0:64, cs], po[0:64, :], rc[0:64, :], ALU.mult, [po, rc], [oT[h // 2]])
                        else:
                            P.op('dve', lambda e, rc=rc, po=po: e.reciprocal(out=rc[64:128, :], in_=po[0:64, :]), [po], [rc])
                            tt('dve', oT[h // 2][64:128, cs], po[64:128, :], rc[64:128, :], ALU.mult, [po, rc], [oT[h // 2]])

                if hg == 0:
                    for (k0, n) in KB:
                        ks = slice(k0, k0 + n)
                        pa, pb, _ = NRB
                        for kc in range(3):
                            ksz = 128 if kc < 2 else 32
                            mm(pa[0:96, 0:n], wkv[0:ksz, kc, 0:96], kvl[kc][0:ksz, ks], kc == 0, kc == 2, [sB, kvl[kc]], [pa])
                        for kc in range(3):
                            ksz = 128 if kc < 2 else 32
                            mm(pb[0:96, 0:n], wksv[0:ksz, kc, 0:96], kvl[kc][0:ksz, ks], kc == 0, kc == 2, [sB, kvl[kc]], [pb])
                        t1 = ntmp()
                        t2 = ntmp()
                        tt('dve', t1[0:96, 0:n], pa[0:96, 0:n], rk[:, ks], ALU.mult, [pa, rk], [t1])
                        tt('dve', t2[0:96, 0:n], pb[0:96, 0:n], rk[:, 1280 + k0:1280 + k0 + n], ALU.mult, [pb, rk], [t2])
                        tt('pool', tpe[0:96, ks], t1[0:96, 0:n], t2[0:96, 0:n], ALU.add, [t1, t2], [tpe])
                cnt.setdefault('p', 0)
                for _ in head_feeder(0):
                    pass
                for hl in range(4):
                    nxt = head_feeder(hl + 1) if hl < 3 else None
                    attention(hl, nxt)
                    if nxt is not None:
                        for _ in nxt:
                            pass

            wo = D["mla_w_o"]
            for oc2 in range(2):
                s = wload([(wo[j, :, oc2 * 512:(oc2 + 1) * 512].rearrange("(kc p) n -> p kc n", p=128), 0, [128, 8, 512])])
                sv = s[:, :].rearrange("p (kc n) -> p kc n", kc=8)
                for o4 in range(4):
                    oc = oc2 * 4 + o4
                    for th in range(2):
                        cs = slice(th * 512, (th + 1) * 512)
                        p = nps()
                        for kc in range(8):
                            mm(p[:, :], sv[:, kc, o4 * 128:(o4 + 1) * 128], oT[kc][:, cs], kc == 0, kc == 7, [s, oT[kc]], [p])
                        stt(xT[oc][:, cs], p[:, :], mb[:, 16 + oc:17 + oc], xT[oc][:, cs], ALU.mult, ALU.add, [p, mb, xT[oc]], [xT[oc]])

        MIXERS['mla'] = mla_mixer

        def ssd_mixer(i, j, lst):
            mb = modb[i]
            W = D["ssd_w_in"]

            def sb(shape, dt, name):
                return P.sbuf(shape, dt, "ssd_" + name, stack=lst)
            tri_f = sb([128, 128], F32, "tri_f"); tri_b = sb([128, 128], F32, "tri_b")
            negm = sb([128, 256], BF16, "negm")
            rows3 = sb([128, 192], F32, "rows3")
            a_b = sb([128, 64], F32, "a_b"); dsum = sb([128, 32], F32, "dsum")
            oh2 = sb([128, 64], BF16, "oh2")
            sel = sb([128, 16 * 128], BF16, "sel")
            gn_b = sb([128, 512], F32, "gn_b")
            brow = sb([1, 640], BF16, "brow")
            ones1 = sb([1, 128], BF16, "ones1")
            dtc = [sb([128, 64], F32, f"dt{c}") for c in range(8)]
            acum = [sb([128, 64], F32, f"acum{c}") for c in range(8)]
            ea = [sb([128, 64], F32, f"ea{c}") for c in range(8)]
            cd = [sb([128, 64], F32, f"cd{c}") for c in range(8)]
            ddt = [sb([128, 64], F32, f"ddt{c}") for c in range(8)]
            AT2 = [sb([128, 128], BF16, f"AT2{c}") for c in range(8)]
            NA2 = [sb([128, 128], BF16, f"NA2{c}") for c in range(8)]
            a2s = sb([128, 128], F32, "a2s"); n2s = sb([128, 128], F32, "n2s")
            t64 = [sb([128, 64], F32, f"t64{k}") for k in range(3)]
            x_tok = [sb([128, 512], BF16, f"xtok{c}") for c in range(8)]
            Btok = [sb([128, 128], BF16, f"btok{c}") for c in range(8)]
            BT = sb([128, NT], BF16, "BT"); CT = sb([128, NT], BF16, "CT")
            pbuf = [sb([128, 4 * 260], BF16, f"pbuf{k}") for k in range(2)]
            Dw5 = [sb([128, 5 * 128], BF16, f"dw5{k}") for k in range(2)]
            H = [sb([128, 512], F32, f"H{d}") for d in range(2)]
            Hinb = [sb([128, 512], BF16, f"hinb{c}") for c in range(8)]
            Hinf = [sb([128, 512], BF16, f"hinf{c}") for c in range(8)]
            xdd_r = [sb([128, 512], BF16, f"xdd{k}") for k in range(2)]
            cbT = sb([128, 128], F32, "cbT")
            Eb = [sb([128, 512], BF16, f"Eb{k}") for k in range(2)]
            Mt = [[sb([128, 512], BF16, f"Mt{d}{q}") for q in range(2)] for d in range(2)]
            y1_ = [sb([128, 512], F32, f"y1_{k}") for k in range(2)]
            y2_ = [sb([128, 512], F32, f"y2_{k}") for k in range(2)]
            yd_ = [sb([128, 512], F32, f"yd_{k}") for k in range(2)]
            sz2_ = [sb([128, 512], F32, f"sz2_{k}") for k in range(2)]
            yn_ = [sb([128, 512], F32, f"yn_{k}") for k in range(2)]
            ssc_ = [sb([128, 2], F32, f"ssc_{k}") for k in range(2)]
            ynT = sb([128, 4 * NT], BF16, "ynT")
            hst = [sb([128, 512], F32, f"hst{k}") for k in range(2)]
            ynv = ynT[:, :].rearrange("p (k t) -> p k t", k=4)

            P.dma('sp', tri_f[:, :], D["tri"][0], writes=[tri_f], sembuf=tri_f)
            P.dma('sp', tri_b[:, :], D["tri"][1], writes=[tri_b], sembuf=tri_b)
            P.dma('pool', negm[:, :].rearrange("p (d i) -> p d i", d=2), D["negmask"].rearrange("d p i -> p d i"), writes=[negm], sembuf=negm)
            for r in range(3):
                P.dma('sp', rows3[:, r * 64:(r + 1) * 64], D["ssd_rows"][r:r + 1, :].partition_broadcast(128), writes=[rows3], sembuf=rows3)
            act(a_b[:, :], rows3[:, 64:128], AF.Exp, [rows3], [a_b])
            ts('dve', a_b[:, :], a_b[:, :], -1.0, None, ALU.mult, None, [a_b], [a_b])
            tt('dve', dsum[:, :], rows3[:, 128:160], rows3[:, 160:192], ALU.add, [rows3], [dsum])
            tt('dve', oh2[:, :], ident_f[:, 0:64], ident_f[:, 64:128], ALU.add, [ident_f], [oh2])
            memset('pool', ones1[:, :], 1.0, [ones1])
            for k in range(2):
                memset('pool', pbuf[k][:, :], 0.0, [pbuf[k]])
            negv = negm[:, :].rearrange("p (d i) -> p d i", d=2)
            negm4 = sb([128, 2 * 512], BF16, "negm4")
            for d in range(2):
                cp('dve', negm4[:, d * 512:(d + 1) * 512].rearrange("p (h i) -> p h i", h=4), negv[:, d, :].unsqueeze(1).to_broadcast([128, 4, 128]), [negm], [negm4])

            sdt = wload([(W[:, 5120:5184].rearrange("(kc p) n -> p kc n", p=128), 0, [128, 8, 64])])
            dtv = sdt[:, 0:512].rearrange("p (kc n) -> p kc n", kc=8)
            def dt_gen():
                for c in range(8):
                    cc = slice(c * 128, (c + 1) * 128)
                    pdt = PS[NPB - 2]
                    for kc in range(8):
                        mm(pdt[:, 0:64], hT[kc][:, cc], dtv[:, kc, :], kc == 0, kc == 7, [hT[kc], sdt], [pdt])
                    yield
                    ta, tl, td = t64
                    tt('dve', ta[:, :], pdt[:, 0:64], rows3[:, 0:64], ALU.add, [pdt, rows3], [ta])
                    act(ta[:, :], ta[:, :], AF.Exp, [ta], [ta])
                    act(dtc[c][:, :], ta[:, :], AF.Ln, [ta], [dtc[c]], bias=1.0)
                    act(tl[:, :], dtc[c][:, :], AF.Ln, [dtc[c]], [tl])
                    tt('dve', td[:, :], dtc[c][:, :], a_b[:, :], ALU.mult, [dtc[c], a_b], [td])
                    yield
                    pc = PS[NPB - 1]
                    mm(pc[:, 0:32], tri_f[:, :], td[:, 0:32], True, True, [tri_f, td], [pc])
                    mm(pc[:, 32:64], tri_b[:, :], td[:, 32:64], True, True, [tri_b, td], [pc])
                    mm(pc[:, 64:128], ones_f[:, :], td[:, 0:64], True, True, [ones_f, td], [pc])
                    yield
                    cp('dve', acum[c][:, :], pc[:, 0:64], [pc], [acum[c]])
                    act(ea[c][:, :], pc[:, 0:64], AF.Exp, [pc], [ea[c]])
                    act(cd[c][:, :], pc[:, 64:128], AF.Exp, [pc], [cd[c]])
                    tt('dve', ta[:, :], pc[:, 64:128], acum[c][:, :], ALU.subtract, [pc, acum[c]], [ta])
                    act(ta[:, :], ta[:, :], AF.Exp, [ta], [ta])
                    tt('dve', ddt[c][:, :], dtc[c][:, :], ta[:, :], ALU.mult, [dtc[c], ta], [ddt[c]])
                    tt('dve', tl[:, :], tl[:, :], acum[c][:, :], ALU.subtract, [tl, acum[c]], [tl])
                    for hf in range(2):
                        cp('dve', a2s[:, hf * 64:(hf + 1) * 64], acum[c][:, :], [acum[c]], [a2s])
                        cp('pool', n2s[:, hf * 64:(hf + 1) * 64], tl[:, :], [tl], [n2s])
                    yield
                    pT = PS[NPB - 2]
                    tr(pT[:, 0:128], a2s[:, :], ident_f[:, :], [a2s, ident_f], [pT])
                    tr(pT[:, 128:256], n2s[:, :], ident_f[:, :], [n2s, ident_f], [pT])
                    yield
                    for (dst, off) in ((AT2[c], 0), (NA2[c], 128)):
                        cp('act', dst[:, :], pT[:, off:off + 128], [pT], [dst])
                        tt('dve', dst[64:128, :], pT[64:128, off:off + 128], dst[64:128, :], ALU.subtract, [pT, dst], [dst])
                    yield

            DTG = {'g': dt_gen()}

            def pump_dt():
                if DTG['g'] is not None:
                    try:
                        next(DTG['g'])
                    except StopIteration:
                        DTG['g'] = None

            def drain_dt():
                while DTG['g'] is not None:
                    pump_dt()

            def state_out(Hd, seq, d, g):
                pt = nps()
                for blk in range(4):
                    tr(pt[:, blk * 128:(blk + 1) * 128], Hd[:, blk * 128:(blk + 1) * 128], ident_f[:, :], [Hd, ident_f], [pt])
                hs = hst[(seq + d) % 2]
                cp(alt('dve', 'act'), hs[:, :], pt[:, :], [pt], [hs])
                P.dma('sp', D["ossm"][seq, d, 8 * g:8 * g + 8].rearrange("(blk hh) p n -> (hh p) blk n", hh=2),
                      hs[:, :].rearrange("p (blk n) -> p blk n", blk=4), reads=[hs], sembuf=hs, is_output=True)

            def state_step(Hd, c, d, g):
                xd = xdd_r[d]
                tt('dve', xd[:, :].rearrange("p (h q) -> p h q", h=8), x_tok[c][:, :].rearrange("p (h q) -> p h q", h=8),
                   ddt[c][:, d * 32 + 8 * g:d * 32 + 8 * g + 8].unsqueeze(2).to_broadcast([128, 8, 64]), ALU.mult, [x_tok[c], ddt[c]], [xd])
                ps = nps()
                mm(ps[:, :], Btok[c][:, :], xd[:, :], True, True, [Btok[c], xd], [ps])
                tt('dve', Hd[:, :].rearrange("p (h q) -> p h q", h=8), Hd[:, :].rearrange("p (h q) -> p h q", h=8),
                   cd[c][:, d * 32 + 8 * g:d * 32 + 8 * g + 8].unsqueeze(2).to_broadcast([128, 8, 64]), ALU.mult, [Hd, cd[c]], [Hd])
                tt('dve', Hd[:, :], Hd[:, :], ps[:, :], ALU.add, [Hd, ps], [Hd])

            for g in range(cfg.get('ssd_groups', 4)):
                P.dma('sp', gn_b[:, :], D["gnorm_row"][0:1, g * 512:(g + 1) * 512].partition_broadcast(128), writes=[gn_b], sembuf=gn_b)
                P.dma('pool', brow[:, 0:512], D["bconv_row"][0:1, g * 512:(g + 1) * 512], writes=[brow], sembuf=brow)
                P.dma('pool', brow[:, 512:640], D["bconv_row"][0:1, 2048 + g * 128:2048 + (g + 1) * 128], writes=[brow], sembuf=brow)
                selv = sel[:, :].rearrange("p (s m) -> p s m", s=16)
                for d in range(2):
                    cp('dve', selv[:, d * 8:(d + 1) * 8, :], oh2[:, d * 32 + 8 * g:d * 32 + 8 * g + 8].unsqueeze(2).to_broadcast([128, 8, 128]), [oh2], [sel])
                sx = wload([(W[:, 2048 + g * 512:2048 + (g + 1) * 512].rearrange("(kc p) n -> p kc n", p=128), 0, [128, 8, 512])])
                sxv = sx[:, :].rearrange("p (kc n) -> p kc n", kc=8)
                sbc = wload([(W[:, 4096 + g * 128:4096 + (g + 1) * 128].rearrange("(kc p) n -> p kc n", p=128), 0, [128, 8, 128]),
                             (W[:, 4608 + g * 128:4608 + (g + 1) * 128].rearrange("(kc p) n -> p kc n", p=128), 1024, [128, 8, 128])])
                sbv = sbc[:, 0:1024].rearrange("p (kc n) -> p kc n", kc=8)
                scv = sbc[:, 1024:2048].rearrange("p (kc n) -> p kc n", kc=8)
                def qinfo(q):
                    if q < 4:
                        return sxv, sx, q * 128, 4 * g + q
                    elif q == 4:
                        return sbv, sbc, 0, 16 + g
                    return scv, sbc, 0, 20 + g

                def inproj(q):
                    wv_, ws_, wc0, ccg = qinfo(q)
                    pb_ = pbuf[q % 2]
                    pb = pb_[:, :].rearrange("p (s t) -> p s t", s=4)
                    for th in range(2):
                        cs = slice(th * 512, (th + 1) * 512)
                        p = nps()
                        for kc in range(8):
                            mm(p[:, :], wv_[:, kc, wc0:wc0 + 128], hT[kc][:, cs], kc == 0, kc == 7, [ws_, hT[kc]], [p])
                        cp('act', pb[:, 2 * th:2 * th + 2, 2:258], p[:, :].rearrange("p (s t) -> p s t", s=2), [p], [pb_])
                        pump_dt()
                    ts('dve', pb[:, 1:4, 0:2], pb[:, 0:3, 256:258], flag[:, 0:1], None, ALU.mult, None, [pb_, flag], [pb_])
                    ts('dve', pb[:, 0:3, 258:260], pb[:, 1:4, 2:4], flag[:, 0:1], None, ALU.mult, None, [pb_, flag], [pb_])
                    dw_ = Dw5[q % 2]
                    dwv = dw_[:, :].rearrange("p (w n) -> p w n", w=5)
                    for w in range(5):
                        ts('dve', dwv[:, w, :], ident_f[:, :], vc("wconv", w * 24 + ccg), None, ALU.mult, None, [ident_f, vcols], [dw_])

                def sconv(q):
                    wv_, ws_, wc0, ccg = qinfo(q)
                    pb_ = pbuf[q % 2]
                    pb = pb_[:, :].rearrange("p (s t) -> p s t", s=4)
                    dw_ = Dw5[q % 2]
                    dwv = dw_[:, :].rearrange("p (w n) -> p w n", w=5)
                    if q < 5:
                        for t in range(8):
                            seg, off = t // 2, (t % 2) * 128
                            p = nps()
                            for w in range(5):
                                mm(p[:, 0:128], pb[:, seg, off + w:off + w + 128], dwv[:, w, :], w == 0, False, [pb_, dw_], [p])
                            bc0 = q * 128 if q < 4 else 512
                            mm(p[:, 0:128], ones1[0:1, :], brow[0:1, bc0:bc0 + 128], False, True, [ones1, brow], [p])
                            if q < 4:
                                act(x_tok[t][:, q * 128:(q + 1) * 128], p[:, 0:128], AF.Silu, [p], [x_tok[t]])
                            else:
                                act(Btok[t][:, :], p[:, 0:128], AF.Silu, [p], [Btok[t]])
                            pump_dt()
                    if q >= 4:
                        dstT = BT if q == 4 else CT
                        for th in range(2):
                            cs = slice(th * 512, (th + 1) * 512)
                            p = nps()
                            for w in range(5):
                                mm(p[:, :], dwv[:, w, :], pb[:, 2 * th:2 * th + 2, w:w + 256], w == 0, w == 4, [dw_, pb_], [p])
                            act(dstT[:, cs], p[:, :], AF.Silu, [p, vcols], [dstT], bias=vc("bconv", ccg))

                SPH = cfg.get('ssd_phase', 9)
                inproj(0)
                for q in range(6):
                    if q + 1 < 6:
                        inproj(q + 1)
                    sconv(q)
                drain_dt()
                sz_ = wload([(W[:, g * 512:(g + 1) * 512].rearrange("(kc p) n -> p kc n", p=128), 0, [128, 8, 512])])
                szv = sz_[:, :].rearrange("p (kc n) -> p kc n", kc=8)
                for d in range(2):
                    hs = hst[d]
                    P.dma('sp', hs[:, :].rearrange("p (blk n) -> p blk n", blk=4),
                          D["h0"][d, 8 * g:8 * g + 8].rearrange("(blk hh) p n -> (hh p) blk n", hh=2), writes=[hs], sembuf=hs)
                    pt = nps()
                    for blk in range(4):
                        tr(pt[:, blk * 128:(blk + 1) * 128], hs[:, blk * 128:(blk + 1) * 128], ident_f[:, :], [hs, ident_f], [pt])
                    cp('dve', H[d][:, :], pt[:, :], [pt], [H[d]])
                for k8 in range(8 if SPH >= 2 else 0):
                    c = 7 - k8
                    if c in (5, 3, 1):
                        ts('dve', H[1][:, :], H[1][:, :], flag[:, 0:1], None, ALU.mult, None, [H[1], flag], [H[1]])
                    cp('act', Hinb[c][:, :], H[1][:, :], [H[1]], [Hinb[c]])
                    state_step(H[1], c, 1, g)
                    if c in (6, 4, 2, 0):
                        state_out(H[1], c // 2, 1, g)
                    c = k8
                    if c in (2, 4, 6):
                        ts('dve', H[0][:, :], H[0][:, :], flag[:, 0:1], None, ALU.mult, None, [H[0], flag], [H[0]])
                    cp('act', Hinf[c][:, :], H[0][:, :], [H[0]], [Hinf[c]])
                    state_step(H[0], c, 0, g)
                    if c in (1, 3, 5, 7):
                        state_out(H[0], c // 2, 0, g)
                v8 = lambda ap: ap.rearrange("p (h q) -> p h q", h=8)

                def head(c):
                    cc = slice(c * 128, (c + 1) * 128)
                    k = c % 2
                    y1, y2, yd, sz2 = y1_[k], y2_[k], yd_[k], sz2_[k]
                    pcb = nps()
                    mm(pcb[:, 0:128], BT[:, cc], CT[:, cc], True, True, [BT, CT], [pcb])
                    cp('act', cbT[:, :], pcb[:, 0:128], [pcb], [cbT])
                    psegs = {}
                    for d in range(2):
                        for quad in range(2):
                            pseg = nps()
                            psegs[(d, quad)] = pseg
                            si0 = d * 8 + quad * 4
                            mm(pseg[:, :], NA2[c][:, :], sel[:, si0 * 128:(si0 + 4) * 128], True, False, [sel, NA2[c]], [pseg])
                            mm(pseg[:, :], ident_b[:, :], negm4[:, d * 512:(d + 1) * 512], False, False, [ident_b, negm4], [pseg])
                            for hq in range(4):
                                si = si0 + hq
                                o = pseg[:, hq * 128:(hq + 1) * 128]
                                mm(o, selv[:, si, :], AT2[c][:, :], False, hq == 3, [sel, AT2[c]], [pseg])
                            E = Eb[(d * 2 + quad) % len(Eb)]
                            act(E[:, :], pseg[:, :], AF.Exp, [pseg], [E])
                            tt('dve', Mt[d][quad][:, :].rearrange("p (h i) -> p h i", h=4), E[:, :].rearrange("p (h i) -> p h i", h=4),
                               cbT[:, :].unsqueeze(1).to_broadcast([128, 4, 128]), ALU.mult, [E, cbT], [Mt[d][quad]])
                    pyf = nps()
                    mm(pyf[:, :], CT[:, cc], Hinf[c][:, :], True, True, [CT, Hinf[c]], [pyf])
                    pyb = nps()
                    mm(pyb[:, :], CT[:, cc], Hinb[c][:, :], True, True, [CT, Hinb[c]], [pyb])
                    pz = nacc()
                    for kc in range(8):
                        mm(pz[:, :], hT[kc][:, cc], szv[:, kc, :], kc == 0, kc == 7, [hT[kc], sz_], [pz])
                    pyd = nacc()
                    for hl in range(8):
                        for d in range(2):
                            mm(pyd[:, hl * 64:(hl + 1) * 64], Mt[d][hl // 4][:, (hl % 4) * 128:(hl % 4 + 1) * 128], x_tok[c][:, hl * 64:(hl + 1) * 64],
                               d == 0, d == 1, [Mt[d][hl // 4], x_tok[c]], [pyd])
                    tt('dve', v8(y1[:, :]), v8(pyf[:, :]), ea[c][:, 8 * g:8 * g + 8].unsqueeze(2).to_broadcast([128, 8, 64]), ALU.mult, [pyf, ea[c]], [y1])
                    tt('dve', v8(y2[:, :]), v8(pyb[:, :]), ea[c][:, 32 + 8 * g:32 + 8 * g + 8].unsqueeze(2).to_broadcast([128, 8, 64]), ALU.mult, [pyb, ea[c]], [y2])
                    cp('dve', yd[:, :], pyd[:, :], [pyd], [yd])
                    act(sz2[:, :], pz[:, :], AF.Silu, [pz], [sz2])

                def tail(c):
                    cc = slice(c * 128, (c + 1) * 128)
                    k = c % 2
                    y1, y2, yd, sz2, yn, ssc = y1_[k], y2_[k], yd_[k], sz2_[k], yn_[k], ssc_[k]
                    tt('pool', y1[:, :], y1[:, :], y2[:, :], ALU.add, [y1, y2], [y1])
                    tt('dve', v8(y2[:, :]), v8(x_tok[c][:, :]), dsum[:, 8 * g:8 * g + 8].unsqueeze(2).to_broadcast([128, 8, 64]), ALU.mult, [x_tok[c], dsum], [y2])
                    tt('pool', y1[:, :], y1[:, :], y2[:, :], ALU.add, [y1, y2], [y1])
                    tt('pool', y1[:, :], y1[:, :], yd[:, :], ALU.add, [y1, yd], [y1])
                    tt('pool', y1[:, :], y1[:, :], sz2[:, :], ALU.mult, [y1, sz2], [y1])
                    act(y2[:, :], y1[:, :], AF.Square, [y1], [y2, ssc], accum=ssc[:, 0:1])
                    act(ssc[:, 1:2], ssc[:, 0:1], AF.Ln, [ssc], [ssc], bias=EPS, scale=1.0 / 512)
                    act(ssc[:, 1:2], ssc[:, 1:2], AF.Exp, [ssc], [ssc], scale=-0.5)
                    stt(yn[:, :], y1[:, :], ssc[:, 1:2], gn_b[:, :], ALU.mult, ALU.mult, [y1, ssc, gn_b], [yn])
                    pt = nps()
                    for blk in range(4):
                        tr(pt[:, blk * 128:(blk + 1) * 128], yn[:, blk * 128:(blk + 1) * 128], ident_f[:, :], [yn, ident_f], [pt])
                    cp(alt('dve', 'act'), ynv[:, :, cc], pt[:, :].rearrange("p (k t) -> p k t", k=4), [pt], [ynT])

                if SPH >= 3:
                    head(0)
                for c in range(8 if SPH >= 3 else 0):
                    if c + 1 < 8:
                        head(c + 1)
                    tail(c)
                wo = D["ssd_w_out"]
                for oc2 in range(2 if SPH >= 4 else 0):
                    s = wload([(wo[g * 512:(g + 1) * 512, oc2 * 512:(oc2 + 1) * 512].rearrange("(kc p) n -> p kc n", p=128), 0, [128, 4, 512])])
                    sv = s[:, 0:2048].rearrange("p (kc n) -> p kc n", kc=4)
                    for o4 in range(4):
                        oc = oc2 * 4 + o4
                        for th in range(2):
                            cs = slice(th * 512, (th + 1) * 512)
                            p = nps()
                            for kc in range(4):
                                mm(p[:, :], sv[:, kc, o4 * 128:(o4 + 1) * 128], ynv[:, kc, cs], kc == 0, kc == 3, [s, ynT], [p])
                            stt(xT[oc][:, cs], p[:, :], mb[:, 16 + oc:17 + oc], xT[oc][:, cs], ALU.mult, ALU.add, [p, mb, xT[oc]], [xT[oc]])

        MIXERS['ssd'] = ssd_mixer


        if cfg.get('adaln', True):
            adaln(0)
        for i in range(nlayers):
            kind, j = i % 3, i // 3
            with ExitStack() as lst:
                if kind == 0 and en_mla:
                    norm_mod(i, 1)
                    MIXERS['mla'](i, j, lst)
                elif kind == 1 and en_conv:
                    norm_mod(i, 1)
                    MIXERS['conv'](i, j, lst)
                elif kind == 2 and en_ssd:
                    norm_mod(i, 1)
                    MIXERS['ssd'](i, j, lst)
                P.barrier()
            pump = adaln_gen(i + 1) if (i + 1 < nlayers and cfg.get('adaln', True)) else None
            if en_ffn:
                with ExitStack() as lst:
                    ffn(i, lst, pump)
                    if pump is not None:
                        for _ in pump:
                            pass
                    P.barrier()
            elif pump is not None:
                for _ in pump:
                    pass

        P.barrier()
        xstage = [P.sbuf([128, 1024], F32, f"xstageo{i}") for i in range(2)]
        for t in range(8):
            stg = xstage[t % 2]
            for half in range(2):
                p = nps()
                for q in range(4):
                    fc = half * 4 + q
                    tr(p[:, q * 128:(q + 1) * 128], xT[fc][:, t * 128:(t + 1) * 128], ident_f[:, :], [xT[fc], ident_f], [p])
                cp(alt('dve', 'act'), stg[:, half * 512:(half + 1) * 512], p[:, :], [p], [stg])
            P.dma('sp', D["y"][t * 128:(t + 1) * 128, :], stg[:, :], reads=[stg], sembuf=stg, is_output=True)
        P.finish()
        P.emit()
    nc._used_inputs = USED_INPUTS
    return nc, USED_INPUTS


def _partner(d):
    if d < 64:
        return d
    e = d - 64
    blk, r = e // 16, e % 16
    return 64 + blk * 16 + (r + 8) % 16


def _rope_tables(is_sample):
    T = 1024
    cosq = np.ones((96, T), np.float32)
    sinq = np.zeros((96, T), np.float32)
    if is_sample:
        t = np.arange(T)
        row = (t // 64).astype(np.float32)
        col = (t % 64).astype(np.float32)
        inv = (10000.0 ** (-np.arange(0, 16, 2, dtype=np.float32) / 16)).astype(np.float32)
        ang_r = row[None, :] * inv[:, None]
        ang_c = col[None, :] * inv[:, None]
        for blk, ang in ((0, ang_r), (1, ang_c)):
            c = np.cos(ang).astype(np.float32)
            s = np.sin(ang).astype(np.float32)
            b = 64 + blk * 16
            cosq[b:b + 8] = c
            cosq[b + 8:b + 16] = c
            sinq[b:b + 8] = -s
            sinq[b + 8:b + 16] = s
    ropeq = np.stack([cosq, sinq])
    cosk = np.ones((96, 1280), np.float32)
    sink = np.zeros((96, 1280), np.float32)
    cosk[:, 256:] = cosq
    sink[:, 256:] = sinq
    ropek = np.stack([cosk, sink])
    return ropeq, ropek


_NC_CACHE = {}


def _get_nc(cfg_key, cfg):
    if cfg_key not in _NC_CACHE:
        _NC_CACHE[cfg_key] = build(cfg)
    return _NC_CACHE[cfg_key]


def kernel(_cfg=None, **inp):
    f32 = np.float32
    g = {k: np.asarray(v) for k, v in inp.items()}
    cfg = _cfg or {}
    perm = np.array([h * 96 + _partner(d) for h in range(16) for d in range(96)])
    w_uq = np.ascontiguousarray(g["mla_w_uq"], f32)
    w_uq_sw = np.ascontiguousarray(w_uq[:, :, perm])
    wk = np.zeros((2, 384, 1536), f32)
    wk_sw = np.zeros((2, 384, 1536), f32)
    ukv = g["mla_w_ukv"].reshape(2, 256, 16, 128)
    for h in range(16):
        wk[:, 0:256, h * 96:h * 96 + 64] = ukv[:, :, h, 0:64]
        wk_sw[:, 0:256, h * 96:h * 96 + 64] = ukv[:, :, h, 0:64]
        for e in range(32):
            wk[:, 256 + e, h * 96 + 64 + e] = 1.0
            wk_sw[:, 256 + (_partner(64 + e) - 64), h * 96 + 64 + e] = 1.0
    w_ukv_v = np.ascontiguousarray(ukv[:, :, :, 64:128].reshape(2, 256, 1024))
    pq = np.array([_partner(d) for d in range(96)])

    def pad128(v):
        o = np.zeros((1, 128), f32)
        o[0, :v.shape[0]] = v
        return o

    tri = np.zeros((3, 128, 128), f32)
    k_ = np.arange(128)
    tri[0] = (k_[:, None] <= k_[None, :])
    tri[1] = (k_[:, None] >= k_[None, :])
    negmask = np.zeros((2, 128, 128), f32)
    negmask[0] = np.where(k_[None, :] >= k_[:, None], 0.0, NEG)
    negmask[1] = np.where(k_[None, :] <= k_[:, None], 0.0, NEG)
    ssd_rows = np.zeros((4, 64), f32)
    ssd_rows[0] = g["ssd_dt_bias"][0].reshape(64)
    ssd_rows[1] = g["ssd_a_log"][0].reshape(64)
    ssd_rows[2] = g["ssd_d"][0].reshape(64)

    shared = {
        "w_ada": g["w_ada"], "ffn_w_in": g["ffn_w_in"], "ffn_w_out": g["ffn_w_out"],
        "mla_w_dq": g["mla_w_dq"], "w_uq": w_uq, "w_uq_sw": w_uq_sw, "mla_w_dkv": g["mla_w_dkv"],
        "wk": wk, "wk_sw": wk_sw, "w_ukv_v": w_ukv_v, "mla_w_o": g["mla_w_o"],
        "cv_w_pw1": g["cv_w_pw1"][0], "cv_w_pw2": g["cv_w_pw2"][0],
        "ssd_w_in": g["ssd_w_in"][0], "ssd_w_out": g["ssd_w_out"][0],
        "ssd_rows": ssd_rows, "bconv_row": g["ssd_b_conv"][0].reshape(1, 3072), "gnorm_row": g["ssd_g_norm"][0].reshape(1, 2048),
        "tri": tri, "negmask": negmask,
    }
    shared = {k: np.ascontiguousarray(v, f32) for k, v in shared.items()}
    in_maps = []
    for c in range(8):
        is_sample = c >= 4
        m = dict(shared)
        if is_sample:
            b = c - 4
            m["x"] = np.ascontiguousarray(g["x_sample"][b], f32)
            cond = g["c"][b]
            m["cache_ckv"] = np.ascontiguousarray(g["cache_ckv"][b], f32)
            m["cache_kpe"] = np.ascontiguousarray(g["cache_kpe"][b], f32)
            m["h0"] = np.ascontiguousarray(g["state_ssm"][b, 0], f32)
            maskb = np.zeros((128, 40), f32)
            m["flag"] = np.ones((128, 1), f32)
        else:
            m["x"] = np.ascontiguousarray(g["x_prompt"][4 * c:4 * c + 4].reshape(1024, 1024), f32)
            cond = g["c_ctx"]
            m["cache_ckv"] = np.zeros((2, 256, 256), f32)
            m["cache_kpe"] = np.zeros((2, 256, 32), f32)
            m["h0"] = np.zeros((2, 32, 64, 128), f32)
            maskb = np.full((128, 40), NEG, f32)
            for kc in range(2, 10):
                maskb[:, kc * 4 + (kc - 2) // 2] = 0.0
            m["flag"] = np.zeros((128, 1), f32)
        m["maskb"] = maskb
        mq = np.zeros((8, 1024), f32)
        mk = np.zeros((8, 1280), f32)
        if not is_sample:
            for jj in range(4):
                mq[jj, jj * 256:(jj + 1) * 256] = 1.0
                mk[jj, :] = NEG
                mk[jj, 256 + jj * 256:256 + (jj + 1) * 256] = 0.0
        m["maskq"], m["maskk"] = mq, mk
        rq, rk = _rope_tables(is_sample)
        m["ropeq"], m["ropek"] = rq, rk
        rows = []
        for i in range(4):
            rows += [g["g_norm1"][i].reshape(8, 128), g["g_norm2"][i].reshape(8, 128), g["b_ada"][i].reshape(48, 128)]
        rows += [np.asarray(cond).reshape(8, 128)]
        for j in range(2):
            rows += [g["mla_g_q"][j].reshape(3, 128), g["mla_g_kv"][j].reshape(2, 128),
                     pad128(g["mla_g_qn"][j]), pad128(g["mla_g_qn"][j][pq]),
                     pad128(g["mla_g_kn"][j]), pad128(g["mla_g_kn"][j][pq])]
        rows += [g["cv_b_pw1"][0].reshape(16, 128), g["cv_w_dw"][0].reshape(248, 128), g["cv_b_dw"][0].reshape(8, 128),
                 g["cv_g_ln"][0].reshape(8, 128), g["cv_b_ln"][0].reshape(8, 128), g["cv_b_pw2"][0].reshape(8, 128)]
        rows += [g["ssd_w_conv"][0].reshape(120, 128), g["ssd_b_conv"][0].reshape(24, 128)]
        v = np.concatenate([np.asarray(r, f32) for r in rows], axis=0)
        vecs = np.zeros((NVEC, 128), f32)
        vecs[:v.shape[0]] = v
        m["vecs"] = vecs
        in_maps.append(m)

    nc, used = _get_nc(str(sorted(cfg.items())), cfg)
    in_maps = [{k: m[k] for k in used} for m in in_maps]
    res = run_bass_kernel_spmd(nc, in_maps, core_ids=list(range(8)))
    R = res.results
    y_prompt = np.stack([R[c]["y"].reshape(4, 256, 1024) for c in range(4)]).reshape(16, 256, 1024)
    y_sample = np.stack([R[4 + b]["y"] for b in range(4)])
    _z = {"ockv": np.zeros((2, 1024, 256), f32), "okpe": np.zeros((2, 1024, 32), f32), "ossm": np.zeros((4, 2, 32, 64, 128), f32)}
    R = [{k: (r[k] if k in r else _z[k]) for k in ("y", "ockv", "okpe", "ossm")} for r in R]
    new_ckv = np.stack([R[c]["ockv"].reshape(2, 4, 256, 256).transpose(1, 0, 2, 3) for c in range(4)]).reshape(16, 2, 256, 256)
    new_kpe = np.stack([R[c]["okpe"].reshape(2, 4, 256, 32).transpose(1, 0, 2, 3) for c in range(4)]).reshape(16, 2, 256, 32)
    new_ssm = np.stack([R[c]["ossm"] for c in range(4)]).reshape(16, 1, 2, 32, 64, 128)
    outs = (y_prompt.astype(f32), y_sample.astype(f32), np.ascontiguousarray(new_ckv, f32),
            np.ascontiguousarray(new_kpe, f32), np.ascontiguousarray(new_ssm, f32))
    return outs
```

```python
import numpy as np
import concourse.bass as bass
import concourse.mybir as mybir
from concourse.bass_utils import run_bass_kernel_spmd
from contextlib import ExitStack

F32 = mybir.dt.float32
BF16 = mybir.dt.bfloat16
AF = mybir.ActivationFunctionType
ALU = mybir.AluOpType
AX = mybir.AxisListType

ENGS = ['pe', 'act', 'dve', 'pool', 'sp']


class Buf:
    def __init__(self, t, name):
        self.t = t
        self.name = name
        self.writer = None
        self.readers = {}
        self.semkey = None
        self.dcount = 0
        self.is_psum = False

    def __getitem__(self, idx):
        return self.t[idx]


class Prog:
    def __init__(self, nc, stack, self_wait=True):
        self.nc = nc
        self.stack = stack
        self.ops = {e: [] for e in ENGS}
        self.count = {e: 0 for e in ENGS}
        self.semh = {}
        self.obs = {e: {} for e in ENGS}
        self.self_wait = self_wait
        for e in ENGS:
            self.semh[e] = stack.enter_context(nc.semaphore("sem_" + e))
        self.nbuf = 0
        self.out_tokens = []
        self.dma_final = {}
        self.used_names = set()
        self.eng = {'pe': nc.tensor, 'act': nc.scalar, 'dve': nc.vector, 'pool': nc.gpsimd, 'sp': nc.sync}

    def _emit(self, eng, waits, fn, inc):
        e = self.eng[eng]
        for (k, v) in waits:
            e.wait_ge(self.semh[k], v)
        if fn is not None:
            ins = fn(e)
            ins.then_inc(self.semh[inc[0]], inc[1])

    def sbuf(self, shape, dtype, name=None, stack=None):
        self.nbuf += 1
        name = name or f"sb{self.nbuf}"
        if name in self.used_names:
            name = f"{name}_u{self.nbuf}"
        self.used_names.add(name)
        t = (stack or self.stack).enter_context(self.nc.sbuf_tensor(name, list(shape), dtype))
        return Buf(t, name)

    def psum(self, shape, dtype=F32, name=None, stack=None):
        self.nbuf += 1
        name = name or f"ps{self.nbuf}"
        t = (stack or self.stack).enter_context(self.nc.psum_tensor(name, list(shape), dtype))
        b = Buf(t, name)
        b.is_psum = True
        return b

    def _dsem(self, b):
        if b.semkey is None:
            b.semkey = "d_" + b.name
            self.semh[b.semkey] = self.stack.enter_context(self.nc.semaphore("dsem_" + b.name))
        return b.semkey

    def _deps(self, eng, reads, writes):
        deps = set()
        for b in reads:
            if b.writer is not None:
                deps.add(b.writer)
            if b.is_psum:
                for rk, rt in b.readers.items():
                    if rk != eng:
                        deps.add(rt)
        for b in writes:
            if b.writer is not None:
                deps.add(b.writer)
            deps.update(b.readers.values())
        waits = []
        for (k, v) in sorted(deps, key=lambda kv: (str(kv[0]), kv[1])):
            if k == eng and (eng == 'pe' or not self.self_wait):
                continue
            if self.obs[eng].get(k, 0) < v:
                waits.append((k, v))
                self.obs[eng][k] = v
        return waits

    def op(self, eng, fn, reads=(), writes=()):
        waits = self._deps(eng, reads, writes)
        self.count[eng] += 1
        tok = (eng, self.count[eng])
        for b in reads:
            b.readers[eng] = tok
        for b in writes:
            b.writer = tok
            b.readers = {}
        self._emit(eng, waits, fn, (eng, 1))

    def dma(self, q, out_ap, in_ap, reads=(), writes=(), sembuf=None, is_output=False, **kw):
        waits = self._deps(q, reads, writes)
        k = self._dsem(sembuf)
        sembuf.dcount += 16
        tok = (k, sembuf.dcount)
        self.dma_final[k] = sembuf.dcount
        for b in reads:
            b.readers['dma_' + k] = tok
        for b in writes:
            b.writer = tok
            b.readers = {}
        if is_output:
            self.out_tokens.append(tok)
        fn = lambda e, o=out_ap, i=in_ap, kw=kw: e.dma_start(out=o, in_=i, **kw)
        self._emit(q, waits, fn, (k, 16))

    def barrier(self):
        for e in ENGS:
            waits = []
            for o in ENGS:
                if o == e or self.count[o] == 0:
                    continue
                if self.obs[e].get(o, 0) < self.count[o]:
                    waits.append((o, self.count[o]))
                    self.obs[e][o] = self.count[o]
            for k, v in self.dma_final.items():
                if self.obs[e].get(k, 0) < v:
                    waits.append((k, v))
                    self.obs[e][k] = v
            if waits:
                self._emit(e, waits, None, None)

    def finish(self):
        waits = []
        seen = {}
        for k, v in self.dma_final.items():
            seen[k] = max(seen.get(k, 0), v)
        for k, v in seen.items():
            waits.append((k, v))
        for o in ENGS:
            if o != 'sp' and self.count[o] > 0:
                waits.append((o, self.count[o]))
        self._emit('sp', waits, None, None)

    def emit(self):
        return
        nc = self.nc
        with nc.Block() as block:
            def run(eng_name, e):
                for (waits, fn, inc) in self.ops[eng_name]:
                    for (k, v) in waits:
                        e.wait_ge(self.semh[k], v)
                    if fn is not None:
                        ins = fn(e)
                        ins.then_inc(self.semh[inc[0]], inc[1])

            @block.tensor
            def _(e):
                run('pe', e)

            @block.scalar
            def _(e):
                run('act', e)

            @block.vector
            def _(e):
                run('dve', e)

            @block.gpsimd
            def _(e):
                run('pool', e)

            @block.sync
            def _(e):
                run('sp', e)


D_MODEL = 1024
NT = 1024
EPS = 1e-6
FFN_H = 2816
NEG = -30000.0

VEC_LAYOUT = []
for _i in range(4):
    VEC_LAYOUT += [(f"g1_{_i}", 8), (f"g2_{_i}", 8), (f"bada_{_i}", 48)]
VEC_LAYOUT += [("cond", 8)]
for _j in range(2):
    VEC_LAYOUT += [(f"gq_{_j}", 3), (f"gkv_{_j}", 2), (f"gqn_{_j}", 1), (f"gqnsw_{_j}", 1), (f"gkn_{_j}", 1), (f"gknsw_{_j}", 1)]
VEC_LAYOUT += [("bpw1", 16), ("wdw", 248), ("bdw", 8), ("gln", 8), ("bln", 8), ("bpw2", 8)]
VEC_LAYOUT += [("wconv", 120), ("bconv", 24)]
VEC_BASE = {}
_o = 0
for _n, _r in VEC_LAYOUT:
    VEC_BASE[_n] = _o
    _o += _r
NVEC = ((_o + 127) // 128) * 128


def build(cfg=None):
    cfg = cfg or {}
    en_mla = cfg.get("mla", True)
    en_conv = cfg.get("conv", True)
    en_ssd = cfg.get("ssd", True)
    en_ffn = cfg.get("ffn", True)
    nlayers = cfg.get("nlayers", 4)

    nc = bass.Bass("TRN2", target_bir_lowering=False)


    IN_SHAPES = {
        "x": [NT, 1024], "vecs": [NVEC, 128], "w_ada": [4, 1024, 6144],
        "ffn_w_in": [4, 1024, 5632], "ffn_w_out": [4, 2816, 1024],
        "mla_w_dq": [2, 1024, 384], "w_uq": [2, 384, 1536], "w_uq_sw": [2, 384, 1536],
        "mla_w_dkv": [2, 1024, 288], "wk": [2, 384, 1536], "wk_sw": [2, 384, 1536],
        "w_ukv_v": [2, 256, 1024], "mla_w_o": [2, 1024, 1024],
        "cache_ckv": [2, 256, 256], "cache_kpe": [2, 256, 32],
        "ropeq": [2, 96, 1024], "ropek": [2, 96, 1280], "maskb": [128, 40], "maskq": [8, 1024], "maskk": [8, 1280],
        "cv_w_pw1": [1024, 2048], "cv_w_pw2": [1024, 1024],
        "ssd_w_in": [1024, 5184], "ssd_w_out": [2048, 1024],
        "h0": [2, 32, 64, 128], "ssd_rows": [4, 64], "bconv_row": [1, 3072], "gnorm_row": [1, 2048],
        "flag": [128, 1], "tri": [3, 128, 128], "negmask": [2, 128, 128],
    }

    class _LazyD(dict):
        def __missing__(self, name):
            if name in OUT_SHAPES:
                ap = nc.dram_tensor(name, list(OUT_SHAPES[name]), F32, kind="ExternalOutput").ap()
                self[name] = ap
                return ap
            ap = nc.dram_tensor(name, list(IN_SHAPES[name]), F32, kind="ExternalInput").ap()
            self[name] = ap
            USED_INPUTS.append(name)
            return ap

    USED_INPUTS = []
    OUT_SHAPES = {"y": [NT, 1024], "ockv": [2, NT, 256], "okpe": [2, NT, 32], "ossm": [4, 2, 32, 64, 128]}
    D = _LazyD()
    with ExitStack() as st:
        P = Prog(nc, st)

        def mm(out, lhsT, rhs, start, stop, R, W):
            P.op('pe', lambda e: e.matmul(out, lhsT=lhsT, rhs=rhs, start=start, stop=stop), R, W)

        def tr(out, in_, ident, R, W):
            P.op('pe', lambda e: e.transpose(out=out, in_=in_, identity=ident), R, W)

        def act(out, in_, func, R, W, bias=None, scale=None, accum=None):
            kw = {}
            if bias is not None:
                kw['bias'] = bias
            if scale is not None:
                kw['scale'] = scale
            if accum is not None:
                kw['accum_out'] = accum
            P.op('act', lambda e: e.activation(out=out, in_=in_, func=func, **kw), R, W)

        def tt(eng, out, in0, in1, op, R, W):
            P.op(eng, lambda e: e.tensor_tensor(out=out, in0=in0, in1=in1, op=op), R, W)

        def ts(eng, out, in0, s1, s2, op0, op1, R, W):
            if op1 is None:
                P.op(eng, lambda e: e.tensor_scalar(out=out, in0=in0, scalar1=s1, scalar2=None, op0=op0), R, W)
            else:
                P.op(eng, lambda e: e.tensor_scalar(out=out, in0=in0, scalar1=s1, scalar2=s2, op0=op0, op1=op1), R, W)

        def stt(out, in0, scalar, in1, op0, op1, R, W):
            P.op('dve', lambda e: e.scalar_tensor_tensor(out=out, in0=in0, scalar=scalar, in1=in1, op0=op0, op1=op1), R, W)

        def cp(eng, out, in_, R, W):
            if eng == 'act':
                P.op('act', lambda e: e.copy(out=out, in_=in_), R, W)
            else:
                P.op(eng, lambda e: e.tensor_copy(out=out, in_=in_), R, W)

        def memset(eng, ap, val, W):
            P.op(eng, lambda e: e.memset(ap, val), (), W)

        _rr = {'n': 0}

        def alt(*engs):
            _rr['n'] += 1
            return engs[_rr['n'] % len(engs)]

        NPB = cfg.get("npsum", 8)
        PS = [P.psum([128, 512], F32, f"psb{i}") for i in range(NPB)]
        _ps = {'n': 0}

        def nps():
            _ps['n'] += 1
            return PS[_ps['n'] % (NPB - 3)]

        _pa = {'n': 0}

        def nacc():
            _pa['n'] += 1
            return PS[NPB - 2 + _pa['n'] % 2]

        xT = [P.sbuf([128, NT], F32, f"xT{i}") for i in range(8)]
        hT = [P.sbuf([128, NT], BF16, f"hT{i}") for i in range(8)]
        ident_f = P.sbuf([128, 128], F32, "ident_f")
        ident_b = P.sbuf([128, 128], BF16, "ident_b")
        ones_f = P.sbuf([128, 128], F32, "ones_f")
        ones_b = P.sbuf([128, 128], BF16, "ones_b")
        vcols = P.sbuf([128, NVEC], F32, "vcols")
        modb = [P.sbuf([128, 64], F32, f"modb{i}") for i in range(4)]
        s_bf = P.sbuf([128, 8], BF16, "s_bf")
        flag = P.sbuf([128, 1], F32, "flag_sb")
        NSLOT = 4
        SLOTW = 4096
        slots = [P.sbuf([128, SLOTW], BF16, f"wslot{i}") for i in range(NSLOT)]
        _sl = {'n': 0}

        def wload(dram_aps):
            _sl['n'] += 1
            s = slots[_sl['n'] % NSLOT]
            for (ap, off, shape) in dram_aps:
                n = 1
                for d in shape[1:]:
                    n *= d
                dst = s[:, off:off + n]
                if len(shape) == 3:
                    dst = dst.rearrange("p (a b) -> p a b", a=shape[1])
                P.dma('pool', dst, ap, writes=[s], sembuf=s)
            return s

        def vc(name, idx=0, n=1):
            b = VEC_BASE[name] + idx
            return vcols[:, b:b + n]

        sq_r = [P.sbuf([128, 512], BF16, f"sq_r{i}") for i in range(4)]
        rstd_t = P.sbuf([128, 512], F32, "rstd_t")
        tmp_r = [P.sbuf([128, 512], F32, f"tmp_r{i}") for i in range(3)]
        _tm = {'n': 0}

        def ntmp():
            _tm['n'] += 1
            return tmp_r[_tm['n'] % 3]

        memset('pool', ident_f[:, :], 1.0, [ident_f])
        P.op('pool', lambda e: e.affine_select(out=ident_f[:, :], in_=ident_f[:, :], pattern=[[-1, 128]],
                                               compare_op=ALU.is_equal, fill=0.0, base=0, channel_multiplier=1),
             [ident_f], [ident_f])
        cp('pool', ident_b[:, :], ident_f[:, :], [ident_f], [ident_b])
        memset('pool', ones_f[:, :], 1.0, [ones_f])
        memset('pool', ones_b[:, :], 1.0, [ones_b])
        P.dma('sp', flag[:, :], D["flag"], writes=[flag], sembuf=flag)

        s0 = ExitStack()
        xstage = [P.sbuf([128, 1024], F32, f"xstage{i}", stack=s0) for i in range(2)]
        for blk in range(NVEC // 128 if cfg.get('stop', 9) > 1 else 0):
            stg = xstage[blk % 2]
            P.dma('sp', stg[:, 0:128], D["vecs"][blk * 128:(blk + 1) * 128, :], writes=[stg], sembuf=stg)
            p = nps()
            tr(p[:, 0:128], stg[:, 0:128], ident_f[:, :], [stg, ident_f], [p])
            cp(alt('dve', 'act'), vcols[:, blk * 128:(blk + 1) * 128], p[:, 0:128], [p], [vcols])
        if cfg.get('stop', 9) > 2:
            act(s_bf[:, :], vc("cond", 0, 8), AF.Silu, [vcols], [s_bf])

        for t in range(cfg.get('nx', 8) if cfg.get('stop', 9) > 3 else 0):
            stg = xstage[t % 2]
            if cfg.get('xsplit', 1) == 1:
                P.dma('sp', stg[:, :], D["x"][t * 128:(t + 1) * 128, :], writes=[stg], sembuf=stg)
            else:
                for q8 in range(8):
                    P.dma('sp', stg[:, q8 * 128:(q8 + 1) * 128], D["x"][t * 128:(t + 1) * 128, q8 * 128:(q8 + 1) * 128], writes=[stg], sembuf=stg)
            for half in range(0 if cfg.get('noxt') else cfg.get('nhalf', 2)):
                p = nps()
                for q in range(cfg.get('nq', 4)):
                    fc = half * 4 + q
                    tr(p[:, q * 128:(q + 1) * 128], stg[:, fc * 128:(fc + 1) * 128], ident_f[:, :], [stg, ident_f], [p])
                for q in range(cfg.get('nq', 4) if not cfg.get('nocp') else 0):
                    fc = half * 4 + q
                    cp(alt('dve', 'act') if not cfg.get('cpdve') else 'dve', xT[fc][:, t * 128:(t + 1) * 128], p[:, q * 128:(q + 1) * 128], [p], [xT[fc]])

        P.barrier()
        s0.close()

        def adaln_gen(i):
            p = PS[NPB - 3]
            for k in range(12):
                s = wload([(D["w_ada"][i, :, k * 512:(k + 1) * 512].rearrange("(kc p) n -> p kc n", p=128), 0, [128, 8, 512])])
                sv = s[:, :].rearrange("p (kc n) -> p kc n", kc=8)
                for o4 in range(4):
                    oc = k * 4 + o4
                    for kc in range(8):
                        mm(p[:, oc:oc + 1], sv[:, kc, o4 * 128:(o4 + 1) * 128], s_bf[:, kc:kc + 1], kc == 0, kc == 7, [s, s_bf], [p])
                yield
            mb = modb[i]
            tt('dve', mb[:, 0:48], p[:, 0:48], vc(f"bada_{i}", 0, 48), ALU.add, [p, vcols], [mb])
            stt(mb[:, 48:56], mb[:, 8:16], 1.0, vc(f"g1_{i}", 0, 8), ALU.add, ALU.mult, [mb, vcols], [mb])
            stt(mb[:, 56:64], mb[:, 32:40], 1.0, vc(f"g2_{i}", 0, 8), ALU.add, ALU.mult, [mb, vcols], [mb])
            ts('dve', mb[:, 48:64], mb[:, 48:64], 32.0, None, ALU.mult, None, [mb], [mb])

        def adaln(i):
            for _ in adaln_gen(i):
                pass

        def norm_mod(i, which):
            mb = modb[i]
            acol = 48 if which == 1 else 56
            bcol = 0 if which == 1 else 24
            for th in range(2):
                cs = slice(th * 512, (th + 1) * 512)
                p = nps()
                for fc in range(8):
                    sq = sq_r[fc % 4]
                    act(sq[:, :], xT[fc][:, cs], AF.Square, [xT[fc]], [sq])
                    mm(p[:, :], ones_b[:, :], sq[:, :], fc == 0, fc == 7, [ones_b, sq], [p])
                act(rstd_t[:, :], p[:, :], AF.Ln, [p], [rstd_t], bias=1024.0 * EPS)
                act(rstd_t[:, :], rstd_t[:, :], AF.Exp, [rstd_t], [rstd_t], scale=-0.5)
                for fc in range(8):
                    tm = ntmp()
                    stt(tm[:, :], xT[fc][:, cs], mb[:, acol + fc:acol + fc + 1], rstd_t[:, :], ALU.mult, ALU.mult,
                        [xT[fc], mb, rstd_t], [tm])
                    act(hT[fc][:, cs], tm[:, :], AF.Identity, [tm, mb], [hT[fc]], bias=mb[:, bcol + fc:bcol + fc + 1])

        def ffn(i, lst, pump=None):
            norm_mod(i, 2)
            gT = [P.sbuf([128, NT], BF16, f"gT{i}_{k}", stack=lst) for k in range(22)]
            sa_r = [P.sbuf([128, 512], F32, f"sa{i}_{k}", stack=lst) for k in range(2)]
            mb = modb[i]
            win = D["ffn_w_in"]
            for hb in range(11):
                s = wload([
                    (win[i, :, hb * 256:(hb + 1) * 256].rearrange("(kc p) n -> p kc n", p=128), 0, [128, 8, 256]),
                    (win[i, :, 2816 + hb * 256:2816 + (hb + 1) * 256].rearrange("(kc p) n -> p kc n", p=128), 2048, [128, 8, 256]),
                ])
                sa_v = s[:, 0:2048].rearrange("p (kc n) -> p kc n", kc=8)
                su_v = s[:, 2048:4096].rearrange("p (kc n) -> p kc n", kc=8)
                for sub in range(2):
                    hc = hb * 2 + sub
                    for th in range(2):
                        cs = slice(th * 512, (th + 1) * 512)
                        pa = nps()
                        for kc in range(8):
                            mm(pa[:, :], sa_v[:, kc, sub * 128:(sub + 1) * 128], hT[kc][:, cs], kc == 0, kc == 7, [s, hT[kc]], [pa])
                        pu = nps()
                        for kc in range(8):
                            mm(pu[:, :], su_v[:, kc, sub * 128:(sub + 1) * 128], hT[kc][:, cs], kc == 0, kc == 7, [s, hT[kc]], [pu])
                        sa = sa_r[(hc * 2 + th) % 2]
                        act(sa[:, :], pa[:, :], AF.Silu, [pa], [sa])
                        tt('dve', gT[hc][:, cs], sa[:, :], pu[:, :], ALU.mult, [sa, pu], [gT[hc]])
                if pump is not None:
                    next(pump, None)
            wout = D["ffn_w_out"]
            for oc in range(8):
                s1 = wload([(wout[i, 0:1408, oc * 128:(oc + 1) * 128].rearrange("(kc p) n -> p kc n", p=128), 0, [128, 11, 128])])
                s2 = wload([(wout[i, 1408:2816, oc * 128:(oc + 1) * 128].rearrange("(kc p) n -> p kc n", p=128), 0, [128, 11, 128])])
                v1 = s1[:, 0:1408].rearrange("p (kc n) -> p kc n", kc=11)
                v2 = s2[:, 0:1408].rearrange("p (kc n) -> p kc n", kc=11)
                for th in range(2):
                    cs = slice(th * 512, (th + 1) * 512)
                    p = nps()
                    for hc in range(22):
                        sv, ss = (v1, s1) if hc < 11 else (v2, s2)
                        mm(p[:, :], sv[:, hc % 11, :], gT[hc][:, cs], hc == 0, hc == 21, [ss, gT[hc]], [p])
                    stt(xT[oc][:, cs], p[:, :], mb[:, 40 + oc:41 + oc], xT[oc][:, cs], ALU.mult, ALU.add, [p, mb, xT[oc]], [xT[oc]])

        MIXERS = {}
        def conv_mixer(i, j, lst):
            mb = modb[i]
            ubuf = [P.sbuf([128, 4 * 286], BF16, f"ubuf{c}", stack=lst) for c in range(8)]
            vbuf = [P.sbuf([128, NT], F32, f"vbuf{c}", stack=lst) for c in range(8)]
            sg_r = [P.sbuf([128, 512], F32, f"sg{k}", stack=lst) for k in range(2)]
            DwE = [P.sbuf([128, 16 * 128], BF16, f"DwE{k}", stack=lst) for k in range(2)]
            DwO = [P.sbuf([128, 15 * 128], BF16, f"DwO{k}", stack=lst) for k in range(2)]
            mean_t = P.sbuf([128, 512], F32, "cv_mean", stack=lst)
            var_t = P.sbuf([128, 512], F32, "cv_var", stack=lst)
            w1 = D["cv_w_pw1"]
            for c in range(8):
                memset('pool', ubuf[c][:, :], 0.0, [ubuf[c]])
            WS = {}

            def tap(cc, w):
                if w % 2 == 0:
                    return DwE[cc % 2], DwE[cc % 2][:, (w // 2) * 128:(w // 2 + 1) * 128]
                return DwO[cc % 2], DwO[cc % 2][:, (w // 2) * 128:(w // 2 + 1) * 128]

            def build_dw(cc):
                for w in range(31):
                    b, ap = tap(cc, w)
                    if w % 2 == 0:
                        ts('dve', ap, ident_f[:, :], vc("wdw", w * 8 + cc), None, ALU.mult, None, [ident_f, vcols], [b])
                    else:
                        act(ap, ident_f[:, :], AF.Identity, [ident_f, vcols], [b], scale=vc("wdw", w * 8 + cc))

            def pw1(cc):
                c2, sub = cc // 2, cc % 2
                if sub == 0:
                    WS[c2] = wload([
                        (w1[:, c2 * 256:(c2 + 1) * 256].rearrange("(kc p) n -> p kc n", p=128), 0, [128, 8, 256]),
                        (w1[:, 1024 + c2 * 256:1024 + (c2 + 1) * 256].rearrange("(kc p) n -> p kc n", p=128), 2048, [128, 8, 256]),
                    ])
                s = WS[c2]
                sa_v = s[:, 0:2048].rearrange("p (kc n) -> p kc n", kc=8)
                sg_v = s[:, 2048:4096].rearrange("p (kc n) -> p kc n", kc=8)
                ub = ubuf[cc][:, :].rearrange("p (s t) -> p s t", s=4)
                for th in range(2):
                    cs = slice(th * 512, (th + 1) * 512)
                    pa = nps()
                    for kc in range(8):
                        mm(pa[:, :], sa_v[:, kc, sub * 128:(sub + 1) * 128], hT[kc][:, cs], kc == 0, kc == 7, [s, hT[kc]], [pa])
                    pg = nps()
                    for kc in range(8):
                        mm(pg[:, :], sg_v[:, kc, sub * 128:(sub + 1) * 128], hT[kc][:, cs], kc == 0, kc == 7, [s, hT[kc]], [pg])
                    sg = sg_r[th]
                    act(sg[:, :], pg[:, :], AF.Sigmoid, [pg, vcols], [sg], bias=vc("bpw1", 8 + cc))
                    stt(ub[:, 2 * th:2 * th + 2, 15:271], pa[:, :].rearrange("p (s t) -> p s t", s=2), vc("bpw1", cc),
                        sg[:, :].rearrange("p (s t) -> p s t", s=2), ALU.add, ALU.mult, [pa, sg, vcols], [ubuf[cc]])
                ts('dve', ub[:, 1:4, 0:15], ub[:, 0:3, 256:271], flag[:, 0:1], None, ALU.mult, None, [ubuf[cc], flag], [ubuf[cc]])
                ts('dve', ub[:, 0:3, 271:286], ub[:, 1:4, 15:30], flag[:, 0:1], None, ALU.mult, None, [ubuf[cc], flag], [ubuf[cc]])

            def dconv(cc):
                ub = ubuf[cc][:, :].rearrange("p (s t) -> p s t", s=4)
                for sp in range(2):
                    p = nps()
                    for w in range(31):
                        b, ap = tap(cc, w)
                        mm(p[:, :], ap, ub[:, 2 * sp:2 * sp + 2, w:w + 256], w == 0, w == 30, [b, ubuf[cc]], [p])
                    act(vbuf[cc][:, sp * 512:(sp + 1) * 512], p[:, :], AF.Identity, [p, vcols], [vbuf[cc]], bias=vc("bdw", cc))

            pw1(0)
            build_dw(0)
            for cc in range(8):
                if cc + 1 < 8:
                    pw1(cc + 1)
                    build_dw(cc + 1)
                dconv(cc)
            for th in range(2):
                cs = slice(th * 512, (th + 1) * 512)
                p1 = nps()
                for cc in range(8):
                    mm(p1[:, :], ones_f[:, :], vbuf[cc][:, cs], cc == 0, cc == 7, [ones_f, vbuf[cc]], [p1])
                p2 = nps()
                for cc in range(8):
                    sq = sq_r[cc % 2]
                    act(sq[:, :], vbuf[cc][:, cs], AF.Square, [vbuf[cc]], [sq])
                    mm(p2[:, :], ones_b[:, :], sq[:, :], cc == 0, cc == 7, [ones_b, sq], [p2])
                act(mean_t[:, :], p1[:, :], AF.Identity, [p1], [mean_t], scale=1.0 / 1024)
                tt('dve', var_t[:, :], mean_t[:, :], mean_t[:, :], ALU.mult, [mean_t], [var_t])
                stt(var_t[:, :], p2[:, :], 1.0 / 1024, var_t[:, :], ALU.mult, ALU.subtract, [p2, var_t], [var_t])
                act(rstd_t[:, :], var_t[:, :], AF.Ln, [var_t], [rstd_t], bias=EPS)
                act(rstd_t[:, :], rstd_t[:, :], AF.Exp, [rstd_t], [rstd_t], scale=-0.5)
                for cc in range(8):
                    tm = ntmp()
                    tt('dve', tm[:, :], vbuf[cc][:, cs], mean_t[:, :], ALU.subtract, [vbuf[cc], mean_t], [tm])
                    tt('pool', tm[:, :], tm[:, :], rstd_t[:, :], ALU.mult, [tm, rstd_t], [tm])
                    act(hT[cc][:, cs], tm[:, :], AF.Silu, [tm, vcols], [hT[cc]], bias=vc("bln", cc), scale=vc("gln", cc))
            w2 = D["cv_w_pw2"]
            for oc2 in range(2):
                s = wload([(w2[:, oc2 * 512:(oc2 + 1) * 512].rearrange("(kc p) n -> p kc n", p=128), 0, [128, 8, 512])])
                sv = s[:, :].rearrange("p (kc n) -> p kc n", kc=8)
                for o4 in range(4):
                    oc = oc2 * 4 + o4
                    for th in range(2):
                        cs = slice(th * 512, (th + 1) * 512)
                        p = nps()
                        for kc in range(8):
                            mm(p[:, :], sv[:, kc, o4 * 128:(o4 + 1) * 128], hT[kc][:, cs], kc == 0, kc == 7, [s, hT[kc]], [p])
                        tm = ntmp()
                        ts('dve', tm[:, :], p[:, :], vc("bpw2", oc), None, ALU.add, None, [p, vcols], [tm])
                        stt(xT[oc][:, cs], tm[:, :], mb[:, 16 + oc:17 + oc], xT[oc][:, cs], ALU.mult, ALU.add, [tm, mb, xT[oc]], [xT[oc]])

        MIXERS['conv'] = conv_mixer

        def mla_mixer(i, j, lst):
            mb = modb[i]
            cqf = [P.sbuf([128, 512], F32, f"cqf{k}", stack=lst) for k in range(3)]
            cqn = [P.sbuf([128, NT], BF16, f"cqn{k}", stack=lst) for k in range(3)]
            kvl = [P.sbuf([128, 1280], BF16, f"kvl{k}", stack=lst) for k in range(3)]
            rq = P.sbuf([96, 2048], F32, "rq", stack=lst)
            rk = P.sbuf([96, 2560], F32, "rk", stack=lst)
            mkb = P.sbuf([128, 40], F32, "mkb", stack=lst)
            cstg = P.sbuf([128, 2 * 288], F32, "cstg", stack=lst)
            ostg = [P.sbuf([128, 288], F32, f"ostg{k}", stack=lst) for k in range(2)]
            qf_r = [P.sbuf([104, NT], BF16, f"qf{k}", stack=lst) for k in range(2)]
            kf_r = [P.sbuf([104, 1280], BF16, f"kf{k}", stack=lst) for k in range(2)]
            vaug = P.sbuf([128, 10 * 4 * 128], BF16, "vaug", stack=lst)
            E_r = [P.sbuf([128, 512], BF16, f"E{k}", stack=lst) for k in range(4)]
            sqh = [P.sbuf([96, 512], BF16, f"sqh{k}", stack=lst) for k in range(2)]
            rs_h = [P.sbuf([96, 512], F32, f"rsh{k}", stack=lst) for k in range(2)]
            rec = [P.sbuf([128, 512], F32, f"rec{k}", stack=lst) for k in range(2)]
            oT = hT
            cnt = {'e': 0, 'h': 0}

            P.dma('sp', rq[:, 0:1024], D["ropeq"][0], writes=[rq], sembuf=rq)
            P.dma('sp', rq[:, 1024:2048], D["ropeq"][1], writes=[rq], sembuf=rq)
            P.dma('sp', rk[:, 0:1280], D["ropek"][0], writes=[rk], sembuf=rk)
            P.dma('sp', rk[:, 1280:2560], D["ropek"][1], writes=[rk], sembuf=rk)
            cv = cstg[:, :].rearrange("p (t f) -> p t f", t=2)
            P.dma('sp', cv[:, :, 0:256], D["cache_ckv"][j].rearrange("(t p) f -> p t f", p=128), writes=[cstg], sembuf=cstg)
            P.dma('sp', cv[:, :, 256:288], D["cache_kpe"][j].rearrange("(t p) f -> p t f", p=128), writes=[cstg], sembuf=cstg)

            def vcp(name, n=96):
                b = VEC_BASE[name]
                return vcols[0:n, b:b + 1]
            ts('dve', rq[:, 0:1024], rq[:, 0:1024], vcp(f"gqn_{j}"), None, ALU.mult, None, [rq, vcols], [rq])
            ts('dve', rq[:, 1024:2048], rq[:, 1024:2048], vcp(f"gqnsw_{j}"), None, ALU.mult, None, [rq, vcols], [rq])
            ts('dve', rk[:, 0:1280], rk[:, 0:1280], vcp(f"gkn_{j}"), None, ALU.mult, None, [rk, vcols], [rk])
            ts('dve', rk[:, 1280:2560], rk[:, 1280:2560], vcp(f"gknsw_{j}"), None, ALU.mult, None, [rk, vcols], [rk])
            memset('pool', vaug[:, :], 1.0, [vaug])
            for k in range(2):
                P.dma('pool', qf_r[k][96:104, :], D["maskq"], writes=[qf_r[k]], sembuf=qf_r[k])
                P.dma('pool', kf_r[k][96:104, :], D["maskk"], writes=[kf_r[k]], sembuf=kf_r[k])

            sdq = wload([(D["mla_w_dq"][j].rearrange("(kc p) n -> p kc n", p=128), 0, [128, 8, 384])])
            dqv = sdq[:, 0:3072].rearrange("p (kc n) -> p kc n", kc=8)
            sdkv = wload([(D["mla_w_dkv"][j].rearrange("(kc p) n -> p kc n", p=128), 0, [128, 8, 288])])
            dkvv = sdkv[:, 0:2304].rearrange("p (kc n) -> p kc n", kc=8)
            for th in range(2):
                cs = slice(th * 512, (th + 1) * 512)
                pss = nacc()
                for oc in range(3):
                    p = nps()
                    for kc in range(8):
                        mm(p[:, :], dqv[:, kc, oc * 128:(oc + 1) * 128], hT[kc][:, cs], kc == 0, kc == 7, [sdq, hT[kc]], [p])
                    cp('dve', cqf[oc][:, :], p[:, :], [p], [cqf[oc]])
                    sq = sq_r[oc % 2]
                    act(sq[:, :], p[:, :], AF.Square, [p], [sq])
                    mm(pss[:, :], ones_b[:, :], sq[:, :], oc == 0, oc == 2, [ones_b, sq], [pss])
                act(rstd_t[:, :], pss[:, :], AF.Ln, [pss], [rstd_t], bias=EPS, scale=1.0 / 384)
                act(rstd_t[:, :], rstd_t[:, :], AF.Exp, [rstd_t], [rstd_t], scale=-0.5)
                for oc in range(3):
                    stt(cqn[oc][:, cs], cqf[oc][:, :], vc(f"gq_{j}", oc), rstd_t[:, :], ALU.mult, ALU.mult, [cqf[oc], vcols, rstd_t], [cqn[oc]])
                pss = nacc()
                ckvf = [ntmp(), ntmp()]
                for oc in range(2):
                    p = nps()
                    for kc in range(8):
                        mm(p[:, :], dkvv[:, kc, oc * 128:(oc + 1) * 128], hT[kc][:, cs], kc == 0, kc == 7, [sdkv, hT[kc]], [p])
                    cp('dve', ckvf[oc][:, :], p[:, :], [p], [ckvf[oc]])
                    sq = sq_r[oc % 2]
                    act(sq[:, :], p[:, :], AF.Square, [p], [sq])
                    mm(pss[:, :], ones_b[:, :], sq[:, :], oc == 0, oc == 1, [ones_b, sq], [pss])
                pk = nps()
                for kc in range(8):
                    mm(pk[0:32, :], dkvv[:, kc, 256:288], hT[kc][:, cs], kc == 0, kc == 7, [sdkv, hT[kc]], [pk])
                kpef = ntmp()
                cp('dve', kpef[0:32, :], pk[0:32, :], [pk], [kpef])
                cp('act', kvl[2][0:32, 256 + th * 512:256 + (th + 1) * 512], pk[0:32, :], [pk], [kvl[2]])
                act(rstd_t[:, :], pss[:, :], AF.Ln, [pss], [rstd_t], bias=EPS, scale=1.0 / 256)
                act(rstd_t[:, :], rstd_t[:, :], AF.Exp, [rstd_t], [rstd_t], scale=-0.5)
                for oc in range(2):
                    stt(ckvf[oc][:, :], ckvf[oc][:, :], vc(f"gkv_{j}", oc), rstd_t[:, :], ALU.mult, ALU.mult, [ckvf[oc], vcols, rstd_t], [ckvf[oc]])
                    cp('act', kvl[oc][:, 256 + th * 512:256 + (th + 1) * 512], ckvf[oc][:, :], [ckvf[oc]], [kvl[oc]])
                for t4 in range(4):
                    t = th * 4 + t4
                    p = nps()
                    for oc in range(2):
                        tr(p[:, oc * 128:(oc + 1) * 128], ckvf[oc][:, t4 * 128:(t4 + 1) * 128], ident_f[:, :], [ckvf[oc], ident_f], [p])
                    tr(p[:, 256:288], kpef[0:32, t4 * 128:(t4 + 1) * 128], ident_f[0:32, 0:32], [kpef, ident_f], [p])
                    og = ostg[t % 2]
                    cp(alt('dve', 'act'), og[:, :], p[:, 0:288], [p], [og])
                    P.dma('sp', D["ockv"][j, t * 128:(t + 1) * 128, :], og[:, 0:256], reads=[og], sembuf=og, is_output=True)
                    P.dma('sp', D["okpe"][j, t * 128:(t + 1) * 128, :], og[:, 256:288], reads=[og], sembuf=og, is_output=True)

            for t in range(2):
                p = nps()
                for fc in range(2):
                    tr(p[:, fc * 128:(fc + 1) * 128], cv[:, t, fc * 128:(fc + 1) * 128], ident_f[:, :], [cstg, ident_f], [p])
                tr(p[0:32, 256:384], cv[:, t, 256:288], ident_f[:, :], [cstg, ident_f], [p])
                for fc in range(2):
                    cp(alt('dve', 'act'), kvl[fc][:, t * 128:(t + 1) * 128], p[:, fc * 128:(fc + 1) * 128], [p], [kvl[fc]])
                cp('dve', kvl[2][0:32, t * 128:(t + 1) * 128], p[0:32, 256:384], [p], [kvl[2]])

            vv = vaug[:, :].rearrange("p (k a b c) -> p k a b c", k=10, a=2, b=2)
            tpe = P.sbuf([96, 1280], F32, "tpe", stack=lst)
            KB = [(0, 512), (512, 512), (1024, 256)]
            ATT = [PS[0], PS[1], PS[2]]
            NRB = [PS[3], PS[4], PS[5]]
            cnt['p'] = 0
            GW = {}

            def load_group(hg):
                sA = wload([
                    (D["w_uq"][j, :, hg * 384:(hg + 1) * 384].rearrange("(kc p) n -> p kc n", p=128), 0, [128, 3, 384]),
                    (D["w_uq_sw"][j, :, hg * 384:(hg + 1) * 384].rearrange("(kc p) n -> p kc n", p=128), 1152, [128, 3, 384]),
                    (D["w_ukv_v"][j, :, hg * 256:(hg + 1) * 256].rearrange("(kc p) n -> p kc n", p=128), 2304, [128, 2, 256]),
                ])
                sB = wload([
                    (D["wk"][j, :, hg * 384:(hg + 1) * 384].rearrange("(kc p) n -> p kc n", p=128), 0, [128, 3, 384]),
                    (D["wk_sw"][j, :, hg * 384:(hg + 1) * 384].rearrange("(kc p) n -> p kc n", p=128), 1152, [128, 3, 384]),
                ])
                GW[hg] = dict(
                    sA=sA, sB=sB,
                    wqv=sA[:, 0:1152].rearrange("p (kc n) -> p kc n", kc=3),
                    wqsv=sA[:, 1152:2304].rearrange("p (kc n) -> p kc n", kc=3),
                    wvv=sA[:, 2304:2816].rearrange("p (kc n) -> p kc n", kc=2),
                    wkv=sB[:, 0:1152].rearrange("p (kc n) -> p kc n", kc=3),
                    wksv=sB[:, 1152:2304].rearrange("p (kc n) -> p kc n", kc=3))

            def compute_v(hg):
                w = GW[hg]
                for kc in range(10):
                    p = ATT[kc % 3]
                    for c2 in range(2):
                        mm(p[:, 0:256], kvl[c2][:, kc * 128:(kc + 1) * 128], w['wvv'][:, c2, :], c2 == 0, c2 == 1, [kvl[c2], w['sA']], [p])
                    pv4 = p[:, 0:256].rearrange("p (a b c) -> p a b c", a=2, b=2)
                    cp('dve', vv[:, kc, :, 0, 0:64], pv4[:, :, 0, :], [p], [vaug])
                    cp('dve', vv[:, kc, :, 1, 64:128], pv4[:, :, 1, :], [p], [vaug])

            def nr_gen(dst, dcols, mma, mmb, n, tab, toff0, toff1, tcols, tpe=None):
                pa, pb, pss = NRB
                for (l, r, st_, sp_, R_) in mma:
                    mm(pa[0:96, 0:n], l, r, st_, sp_, R_, [pa])
                if tpe is None:
                    for (l, r, st_, sp_, R_) in mmb:
                        mm(pb[0:96, 0:n], l, r, st_, sp_, R_, [pb])
                k2 = cnt['h'] % 2
                cnt['h'] += 1
                sq = sqh[k2]
                act(sq[:, 0:n], pa[0:96, 0:n], AF.Square, [pa], [sq])
                yield
                mm(pss[0:96, 0:n], ones_b[0:96, 0:96], sq[:, 0:n], True, True, [ones_b, sq], [pss])
                rs = rs_h[k2]
                act(rs[:, 0:n], pss[0:96, 0:n], AF.Ln, [pss], [rs], bias=EPS, scale=1.0 / 96)
                act(rs[:, 0:n], rs[:, 0:n], AF.Exp, [rs], [rs], scale=-0.5)
                t1 = ntmp()
                if tpe is None:
                    t2 = ntmp()
                    tt('dve', t1[0:96, 0:n], pa[0:96, 0:n], tab[:, toff0 + tcols.start:toff0 + tcols.stop], ALU.mult, [pa, tab], [t1])
                    tt('dve', t2[0:96, 0:n], pb[0:96, 0:n], tab[:, toff1 + tcols.start:toff1 + tcols.stop], ALU.mult, [pb, tab], [t2])
                    tt('pool', t1[0:96, 0:n], t1[0:96, 0:n], t2[0:96, 0:n], ALU.add, [t1, t2], [t1])
                    tt('pool', dst[0:96, dcols], t1[0:96, 0:n], rs[:, 0:n], ALU.mult, [t1, rs], [dst])
                else:
                    tt('dve', t1[0:64, 0:n], pa[0:64, 0:n], tab[0:64, toff0 + tcols.start:toff0 + tcols.stop], ALU.mult, [pa, tab], [t1])
                    tt('pool', dst[0:64, dcols], t1[0:64, 0:n], rs[0:64, 0:n], ALU.mult, [t1, rs], [dst])
                    tt('pool', dst[64:96, dcols], tpe[64:96, tcols], rs[64:96, 0:n], ALU.mult, [tpe, rs], [dst])
                yield


            def head_feeder(h):
                hg, hl = h // 4, h % 4
                w = GW[hg]
                sA, sB = w['sA'], w['sB']
                qf = qf_r[h % 2]
                kf = kf_r[h % 2]
                for th in range(2):
                    cs = slice(th * 512, (th + 1) * 512)
                    mma = [(w['wqv'][:, kc, hl * 96:(hl + 1) * 96], cqn[kc][:, cs], kc == 0, kc == 2, [sA, cqn[kc]]) for kc in range(3)]
                    mmb = [(w['wqsv'][:, kc, hl * 96:(hl + 1) * 96], cqn[kc][:, cs], kc == 0, kc == 2, [sA, cqn[kc]]) for kc in range(3)]
                    yield from nr_gen(qf, cs, mma, mmb, 512, rq, 0, 1024, cs)
                for (k0, n) in KB:
                    ks = slice(k0, k0 + n)
                    mma = []
                    for kc in range(3):
                        ksz = 128 if kc < 2 else 32
                        mma.append((w['wkv'][0:ksz, kc, hl * 96:(hl + 1) * 96], kvl[kc][0:ksz, ks], kc == 0, kc == 2, [sB, kvl[kc]]))
                    yield from nr_gen(kf, ks, mma, [], n, rk, 0, 1280, ks, tpe=tpe)

            def attention(h, feeder):
                hl = h % 4
                qf = qf_r[h % 2]
                kf = kf_r[h % 2]
                a, b = hl // 2, hl % 2
                for th in range(2):
                    cs = slice(th * 512, (th + 1) * 512)
                    po = nacc()
                    psts = {}

                    def issue_qk(kc):
                        pst = ATT[cnt['p'] % 3]
                        cnt['p'] += 1
                        mm(pst[:, :], kf[0:104, kc * 128:(kc + 1) * 128], qf[0:104, cs], True, True, [kf, qf], [pst])
                        psts[kc] = pst
                    issue_qk(0)
                    issue_qk(1)
                    for kc in range(10):
                        pst = psts[kc]
                        E = E_r[cnt['e'] % 4]
                        cnt['e'] += 1
                        act(E[:, :], pst[:, :], AF.Exp, [pst], [E], scale=float(96.0 ** -0.5))
                        if kc + 2 < 10:
                            issue_qk(kc + 2)
                        mm(po[:, :], vv[:, kc, a, b, :], E[:, :], kc == 0, kc == 9, [vaug, E], [po])
                        if feeder is not None:
                            next(feeder, None)
                    rc = rec[th]
                    if b == 0:
                        P.op('dve', lambda e, rc=rc, po=po: e.reciprocal(out=rc[0:64, :], in_=po[64:128, :]), [po], [rc])
                        tt('dve', oT[h // 2][0:64, cs], po[0:64, :], rc[0:64, :], ALU.mult, [po, rc], [oT[h // 2]])
                    else:
                        P.op('dve', lambda e, rc=rc, po=po: e.reciprocal(out=rc[64:128, :], in_=po[0:64, :]), [po], [rc])
                        tt('dve', oT[h // 2][64:128, cs], po[64:128, :], rc[64:128, :], ALU.mult, [po, rc], [oT[h // 2]])

            load_group(0)
            for (k0, n) in KB:
                ks = slice(k0, k0 + n)
                pa, pb, _ = NRB
                w0 = GW[0]
                for kc in range(3):
                    ksz = 128 if kc < 2 else 32
                    mm(pa[0:96, 0:n], w0['wkv'][0:ksz, kc, 0:96], kvl[kc][0:ksz, ks], kc == 0, kc == 2, [w0['sB'], kvl[kc]], [pa])
                for kc in range(3):
                    ksz = 128 if kc < 2 else 32
                    mm(pb[0:96, 0:n], w0['wksv'][0:ksz, kc, 0:96], kvl[kc][0:ksz, ks], kc == 0, kc == 2, [w0['sB'], kvl[kc]], [pb])
                t1 = ntmp()
                t2 = ntmp()
                tt('dve', t1[0:96, 0:n], pa[0:96, 0:n], rk[:, ks], ALU.mult, [pa, rk], [t1])
                tt('dve', t2[0:96, 0:n], pb[0:96, 0:n], rk[:, 1280 + k0:1280 + k0 + n], ALU.mult, [pb, rk], [t2])
                tt('pool', tpe[0:96, ks], t1[0:96, 0:n], t2[0:96, 0:n], ALU.add, [t1, t2], [tpe])
            for _ in head_feeder(0):
                pass
            for h in range(16):
                hg, hl = h // 4, h % 4
                if hl == 0:
                    compute_v(hg)
                if hl == 2 and hg + 1 < 4:
                    load_group(hg + 1)
                nxt = head_feeder(h + 1) if h + 1 < 16 else None
                attention(h, nxt)
                if nxt is not None:
                    for _ in nxt:
                        pass

            wo = D["mla_w_o"]
            for oc2 in range(2):
                s = wload([(wo[j, :, oc2 * 512:(oc2 + 1) * 512].rearrange("(kc p) n -> p kc n", p=128), 0, [128, 8, 512])])
                sv = s[:, :].rearrange("p (kc n) -> p kc n", kc=8)
                for o4 in range(4):
                    oc = oc2 * 4 + o4
                    for th in range(2):
                        cs = slice(th * 512, (th + 1) * 512)
                        p = nps()
                        for kc in range(8):
                            mm(p[:, :], sv[:, kc, o4 * 128:(o4 + 1) * 128], oT[kc][:, cs], kc == 0, kc == 7, [s, oT[kc]], [p])
                        stt(xT[oc][:, cs], p[:, :], mb[:, 16 + oc:17 + oc], xT[oc][:, cs], ALU.mult, ALU.add, [p, mb, xT[oc]], [xT[oc]])

        MIXERS['mla'] = mla_mixer

        def ssd_mixer(i, j, lst):
            mb = modb[i]
            W = D["ssd_w_in"]

            def sb(shape, dt, name):
                return P.sbuf(shape, dt, "ssd_" + name, stack=lst)
            tri_f = sb([128, 128], F32, "tri_f"); tri_b = sb([128, 128], F32, "tri_b")
            negm = sb([128, 256], BF16, "negm")
            rows3 = sb([128, 192], F32, "rows3")
            a_b = sb([128, 64], F32, "a_b"); dsum = sb([128, 32], F32, "dsum")
            oh2 = sb([128, 64], BF16, "oh2")
            sel = sb([128, 16 * 128], BF16, "sel")
            gn_b = sb([128, 512], F32, "gn_b")
            brow = sb([1, 640], BF16, "brow")
            ones1 = sb([1, 128], BF16, "ones1")
            dtc = [sb([128, 64], F32, f"dt{c}") for c in range(8)]
            acum = [sb([128, 64], F32, f"acum{c}") for c in range(8)]
            ea = [sb([128, 64], F32, f"ea{c}") for c in range(8)]
            cd = [sb([128, 64], F32, f"cd{c}") for c in range(8)]
            ddt = [sb([128, 64], F32, f"ddt{c}") for c in range(8)]
            AT2 = [sb([128, 128], BF16, f"AT2{c}") for c in range(8)]
            NA2 = [sb([128, 128], BF16, f"NA2{c}") for c in range(8)]
            a2s = sb([128, 128], F32, "a2s"); n2s = sb([128, 128], F32, "n2s")
            t64 = [sb([128, 64], F32, f"t64{k}") for k in range(3)]
            x_tok = [sb([128, 512], BF16, f"xtok{c}") for c in range(8)]
            Btok = [sb([128, 128], BF16, f"btok{c}") for c in range(8)]
            BT = sb([128, NT], BF16, "BT"); CT = sb([128, NT], BF16, "CT")
            pbuf = [sb([128, 4 * 260], BF16, f"pbuf{k}") for k in range(2)]
            Dw5 = [sb([128, 5 * 128], BF16, f"dw5{k}") for k in range(2)]
            H = [sb([128, 512], F32, f"H{d}") for d in range(2)]
            Hinb = [sb([128, 512], BF16, f"hinb{c}") for c in range(8)]
            Hinf = [sb([128, 512], BF16, f"hinf{c}") for c in range(8)]
            xdd_r = [sb([128, 512], BF16, f"xdd{k}") for k in range(2)]
            cbT = sb([128, 128], F32, "cbT")
            Eb = [sb([128, 512], BF16, f"Eb{k}") for k in range(2)]
            Mt = [[sb([128, 512], BF16, f"Mt{d}{q}") for q in range(2)] for d in range(2)]
            y1_ = [sb([128, 512], F32, f"y1_{k}") for k in range(2)]
            y2_ = [sb([128, 512], F32, f"y2_{k}") for k in range(2)]
            yd_ = [sb([128, 512], F32, f"yd_{k}") for k in range(2)]
            sz2_ = [sb([128, 512], F32, f"sz2_{k}") for k in range(2)]
            yn_ = [sb([128, 512], F32, f"yn_{k}") for k in range(2)]
            ssc_ = [sb([128, 2], F32, f"ssc_{k}") for k in range(2)]
            ynT = sb([128, 4 * NT], BF16, "ynT")
            hst = [sb([128, 512], F32, f"hst{k}") for k in range(2)]
            ynv = ynT[:, :].rearrange("p (k t) -> p k t", k=4)

            P.dma('sp', tri_f[:, :], D["tri"][0], writes=[tri_f], sembuf=tri_f)
            P.dma('sp', tri_b[:, :], D["tri"][1], writes=[tri_b], sembuf=tri_b)
            P.dma('pool', negm[:, :].rearrange("p (d i) -> p d i", d=2), D["negmask"].rearrange("d p i -> p d i"), writes=[negm], sembuf=negm)
            for r in range(3):
                P.dma('sp', rows3[:, r * 64:(r + 1) * 64], D["ssd_rows"][r:r + 1, :].partition_broadcast(128), writes=[rows3], sembuf=rows3)
            act(a_b[:, :], rows3[:, 64:128], AF.Exp, [rows3], [a_b])
            ts('dve', a_b[:, :], a_b[:, :], -1.0, None, ALU.mult, None, [a_b], [a_b])
            tt('dve', dsum[:, :], rows3[:, 128:160], rows3[:, 160:192], ALU.add, [rows3], [dsum])
            tt('dve', oh2[:, :], ident_f[:, 0:64], ident_f[:, 64:128], ALU.add, [ident_f], [oh2])
            memset('pool', ones1[:, :], 1.0, [ones1])
            for k in range(2):
                memset('pool', pbuf[k][:, :], 0.0, [pbuf[k]])
            negv = negm[:, :].rearrange("p (d i) -> p d i", d=2)
            negm4 = sb([128, 2 * 512], BF16, "negm4")
            for d in range(2):
                cp('dve', negm4[:, d * 512:(d + 1) * 512].rearrange("p (h i) -> p h i", h=4), negv[:, d, :].unsqueeze(1).to_broadcast([128, 4, 128]), [negm], [negm4])

            sdt = wload([(W[:, 5120:5184].rearrange("(kc p) n -> p kc n", p=128), 0, [128, 8, 64])])
            dtv = sdt[:, 0:512].rearrange("p (kc n) -> p kc n", kc=8)
            for c in range(8):
                cc = slice(c * 128, (c + 1) * 128)
                pdt = nps()
                for kc in range(8):
                    mm(pdt[:, 0:64], hT[kc][:, cc], dtv[:, kc, :], kc == 0, kc == 7, [hT[kc], sdt], [pdt])
                ta, tl, td = t64
                tt('dve', ta[:, :], pdt[:, 0:64], rows3[:, 0:64], ALU.add, [pdt, rows3], [ta])
                act(ta[:, :], ta[:, :], AF.Exp, [ta], [ta])
                act(dtc[c][:, :], ta[:, :], AF.Ln, [ta], [dtc[c]], bias=1.0)
                act(tl[:, :], dtc[c][:, :], AF.Ln, [dtc[c]], [tl])
                tt('dve', td[:, :], dtc[c][:, :], a_b[:, :], ALU.mult, [dtc[c], a_b], [td])
                pc = nps()
                mm(pc[:, 0:32], tri_f[:, :], td[:, 0:32], True, True, [tri_f, td], [pc])
                mm(pc[:, 32:64], tri_b[:, :], td[:, 32:64], True, True, [tri_b, td], [pc])
                mm(pc[:, 64:128], ones_f[:, :], td[:, 0:64], True, True, [ones_f, td], [pc])
                cp('dve', acum[c][:, :], pc[:, 0:64], [pc], [acum[c]])
                act(ea[c][:, :], pc[:, 0:64], AF.Exp, [pc], [ea[c]])
                act(cd[c][:, :], pc[:, 64:128], AF.Exp, [pc], [cd[c]])
                tt('dve', ta[:, :], pc[:, 64:128], acum[c][:, :], ALU.subtract, [pc, acum[c]], [ta])
                act(ta[:, :], ta[:, :], AF.Exp, [ta], [ta])
                tt('dve', ddt[c][:, :], dtc[c][:, :], ta[:, :], ALU.mult, [dtc[c], ta], [ddt[c]])
                tt('dve', tl[:, :], tl[:, :], acum[c][:, :], ALU.subtract, [tl, acum[c]], [tl])
                for hf in range(2):
                    cp('dve', a2s[:, hf * 64:(hf + 1) * 64], acum[c][:, :], [acum[c]], [a2s])
                    cp('pool', n2s[:, hf * 64:(hf + 1) * 64], tl[:, :], [tl], [n2s])
                pT = nps()
                tr(pT[:, 0:128], a2s[:, :], ident_f[:, :], [a2s, ident_f], [pT])
                tr(pT[:, 128:256], n2s[:, :], ident_f[:, :], [n2s, ident_f], [pT])
                for (dst, off) in ((AT2[c], 0), (NA2[c], 128)):
                    cp('act', dst[:, :], pT[:, off:off + 128], [pT], [dst])
                    tt('dve', dst[64:128, :], pT[64:128, off:off + 128], dst[64:128, :], ALU.subtract, [pT, dst], [dst])

            def state_out(Hd, seq, d, g):
                pt = nps()
                for blk in range(4):
                    tr(pt[:, blk * 128:(blk + 1) * 128], Hd[:, blk * 128:(blk + 1) * 128], ident_f[:, :], [Hd, ident_f], [pt])
                hs = hst[(seq + d) % 2]
                cp(alt('dve', 'act'), hs[:, :], pt[:, :], [pt], [hs])
                P.dma('sp', D["ossm"][seq, d, 8 * g:8 * g + 8].rearrange("(blk hh) p n -> (hh p) blk n", hh=2),
                      hs[:, :].rearrange("p (blk n) -> p blk n", blk=4), reads=[hs], sembuf=hs, is_output=True)

            def state_step(Hd, c, d, g):
                xd = xdd_r[d]
                tt('dve', xd[:, :].rearrange("p (h q) -> p h q", h=8), x_tok[c][:, :].rearrange("p (h q) -> p h q", h=8),
                   ddt[c][:, d * 32 + 8 * g:d * 32 + 8 * g + 8].unsqueeze(2).to_broadcast([128, 8, 64]), ALU.mult, [x_tok[c], ddt[c]], [xd])
                ps = nps()
                mm(ps[:, :], Btok[c][:, :], xd[:, :], True, True, [Btok[c], xd], [ps])
                tt('dve', Hd[:, :].rearrange("p (h q) -> p h q", h=8), Hd[:, :].rearrange("p (h q) -> p h q", h=8),
                   cd[c][:, d * 32 + 8 * g:d * 32 + 8 * g + 8].unsqueeze(2).to_broadcast([128, 8, 64]), ALU.mult, [Hd, cd[c]], [Hd])
                tt('dve', Hd[:, :], Hd[:, :], ps[:, :], ALU.add, [Hd, ps], [Hd])

            for g in range(cfg.get('ssd_groups', 4)):
                P.dma('sp', gn_b[:, :], D["gnorm_row"][0:1, g * 512:(g + 1) * 512].partition_broadcast(128), writes=[gn_b], sembuf=gn_b)
                P.dma('pool', brow[:, 0:512], D["bconv_row"][0:1, g * 512:(g + 1) * 512], writes=[brow], sembuf=brow)
                P.dma('pool', brow[:, 512:640], D["bconv_row"][0:1, 2048 + g * 128:2048 + (g + 1) * 128], writes=[brow], sembuf=brow)
                selv = sel[:, :].rearrange("p (s m) -> p s m", s=16)
                for d in range(2):
                    cp('dve', selv[:, d * 8:(d + 1) * 8, :], oh2[:, d * 32 + 8 * g:d * 32 + 8 * g + 8].unsqueeze(2).to_broadcast([128, 8, 128]), [oh2], [sel])
                sx = wload([(W[:, 2048 + g * 512:2048 + (g + 1) * 512].rearrange("(kc p) n -> p kc n", p=128), 0, [128, 8, 512])])
                sxv = sx[:, :].rearrange("p (kc n) -> p kc n", kc=8)
                sbc = wload([(W[:, 4096 + g * 128:4096 + (g + 1) * 128].rearrange("(kc p) n -> p kc n", p=128), 0, [128, 8, 128]),
                             (W[:, 4608 + g * 128:4608 + (g + 1) * 128].rearrange("(kc p) n -> p kc n", p=128), 1024, [128, 8, 128])])
                sbv = sbc[:, 0:1024].rearrange("p (kc n) -> p kc n", kc=8)
                scv = sbc[:, 1024:2048].rearrange("p (kc n) -> p kc n", kc=8)
                def qinfo(q):
                    if q < 4:
                        return sxv, sx, q * 128, 4 * g + q
                    elif q == 4:
                        return sbv, sbc, 0, 16 + g
                    return scv, sbc, 0, 20 + g

                def inproj(q):
                    wv_, ws_, wc0, ccg = qinfo(q)
                    pb_ = pbuf[q % 2]
                    pb = pb_[:, :].rearrange("p (s t) -> p s t", s=4)
                    for th in range(2):
                        cs = slice(th * 512, (th + 1) * 512)
                        p = nps()
                        for kc in range(8):
                            mm(p[:, :], wv_[:, kc, wc0:wc0 + 128], hT[kc][:, cs], kc == 0, kc == 7, [ws_, hT[kc]], [p])
                        cp('act', pb[:, 2 * th:2 * th + 2, 2:258], p[:, :].rearrange("p (s t) -> p s t", s=2), [p], [pb_])
                    ts('dve', pb[:, 1:4, 0:2], pb[:, 0:3, 256:258], flag[:, 0:1], None, ALU.mult, None, [pb_, flag], [pb_])
                    ts('dve', pb[:, 0:3, 258:260], pb[:, 1:4, 2:4], flag[:, 0:1], None, ALU.mult, None, [pb_, flag], [pb_])
                    dw_ = Dw5[q % 2]
                    dwv = dw_[:, :].rearrange("p (w n) -> p w n", w=5)
                    for w in range(5):
                        ts('dve', dwv[:, w, :], ident_f[:, :], vc("wconv", w * 24 + ccg), None, ALU.mult, None, [ident_f, vcols], [dw_])

                def sconv(q):
                    wv_, ws_, wc0, ccg = qinfo(q)
                    pb_ = pbuf[q % 2]
                    pb = pb_[:, :].rearrange("p (s t) -> p s t", s=4)
                    dw_ = Dw5[q % 2]
                    dwv = dw_[:, :].rearrange("p (w n) -> p w n", w=5)
                    if q < 5:
                        for t in range(8):
                            seg, off = t // 2, (t % 2) * 128
                            p = nps()
                            for w in range(5):
                                mm(p[:, 0:128], pb[:, seg, off + w:off + w + 128], dwv[:, w, :], w == 0, False, [pb_, dw_], [p])
                            bc0 = q * 128 if q < 4 else 512
                            mm(p[:, 0:128], ones1[0:1, :], brow[0:1, bc0:bc0 + 128], False, True, [ones1, brow], [p])
                            if q < 4:
                                act(x_tok[t][:, q * 128:(q + 1) * 128], p[:, 0:128], AF.Silu, [p], [x_tok[t]])
                            else:
                                act(Btok[t][:, :], p[:, 0:128], AF.Silu, [p], [Btok[t]])
                    if q >= 4:
                        dstT = BT if q == 4 else CT
                        for th in range(2):
                            cs = slice(th * 512, (th + 1) * 512)
                            p = nps()
                            for w in range(5):
                                mm(p[:, :], dwv[:, w, :], pb[:, 2 * th:2 * th + 2, w:w + 256], w == 0, w == 4, [dw_, pb_], [p])
                            act(dstT[:, cs], p[:, :], AF.Silu, [p, vcols], [dstT], bias=vc("bconv", ccg))

                SPH = cfg.get('ssd_phase', 9)
                inproj(0)
                for q in range(6):
                    if q + 1 < 6:
                        inproj(q + 1)
                    sconv(q)
                sz_ = wload([(W[:, g * 512:(g + 1) * 512].rearrange("(kc p) n -> p kc n", p=128), 0, [128, 8, 512])])
                szv = sz_[:, :].rearrange("p (kc n) -> p kc n", kc=8)
                for d in range(2):
                    hs = hst[d]
                    P.dma('sp', hs[:, :].rearrange("p (blk n) -> p blk n", blk=4),
                          D["h0"][d, 8 * g:8 * g + 8].rearrange("(blk hh) p n -> (hh p) blk n", hh=2), writes=[hs], sembuf=hs)
                    pt = nps()
                    for blk in range(4):
                        tr(pt[:, blk * 128:(blk + 1) * 128], hs[:, blk * 128:(blk + 1) * 128], ident_f[:, :], [hs, ident_f], [pt])
                    cp('dve', H[d][:, :], pt[:, :], [pt], [H[d]])
                for k8 in range(8 if SPH >= 2 else 0):
                    c = 7 - k8
                    if c in (5, 3, 1):
                        ts('dve', H[1][:, :], H[1][:, :], flag[:, 0:1], None, ALU.mult, None, [H[1], flag], [H[1]])
                    cp('act', Hinb[c][:, :], H[1][:, :], [H[1]], [Hinb[c]])
                    state_step(H[1], c, 1, g)
                    if c in (6, 4, 2, 0):
                        state_out(H[1], c // 2, 1, g)
                    c = k8
                    if c in (2, 4, 6):
                        ts('dve', H[0][:, :], H[0][:, :], flag[:, 0:1], None, ALU.mult, None, [H[0], flag], [H[0]])
                    cp('act', Hinf[c][:, :], H[0][:, :], [H[0]], [Hinf[c]])
                    state_step(H[0], c, 0, g)
                    if c in (1, 3, 5, 7):
                        state_out(H[0], c // 2, 0, g)
                v8 = lambda ap: ap.rearrange("p (h q) -> p h q", h=8)

                def head(c):
                    cc = slice(c * 128, (c + 1) * 128)
                    k = c % 2
                    y1, y2, yd, sz2 = y1_[k], y2_[k], yd_[k], sz2_[k]
                    pcb = nps()
                    mm(pcb[:, 0:128], BT[:, cc], CT[:, cc], True, True, [BT, CT], [pcb])
                    cp('act', cbT[:, :], pcb[:, 0:128], [pcb], [cbT])
                    psegs = {}
                    for d in range(2):
                        for quad in range(2):
                            pseg = nps()
                            psegs[(d, quad)] = pseg
                            si0 = d * 8 + quad * 4
                            mm(pseg[:, :], NA2[c][:, :], sel[:, si0 * 128:(si0 + 4) * 128], True, False, [sel, NA2[c]], [pseg])
                            mm(pseg[:, :], ident_b[:, :], negm4[:, d * 512:(d + 1) * 512], False, False, [ident_b, negm4], [pseg])
                            for hq in range(4):
                                si = si0 + hq
                                o = pseg[:, hq * 128:(hq + 1) * 128]
                                mm(o, selv[:, si, :], AT2[c][:, :], False, hq == 3, [sel, AT2[c]], [pseg])
                            E = Eb[(d * 2 + quad) % len(Eb)]
                            act(E[:, :], pseg[:, :], AF.Exp, [pseg], [E])
                            tt('dve', Mt[d][quad][:, :].rearrange("p (h i) -> p h i", h=4), E[:, :].rearrange("p (h i) -> p h i", h=4),
                               cbT[:, :].unsqueeze(1).to_broadcast([128, 4, 128]), ALU.mult, [E, cbT], [Mt[d][quad]])
                    pyf = nps()
                    mm(pyf[:, :], CT[:, cc], Hinf[c][:, :], True, True, [CT, Hinf[c]], [pyf])
                    pyb = nps()
                    mm(pyb[:, :], CT[:, cc], Hinb[c][:, :], True, True, [CT, Hinb[c]], [pyb])
                    pz = nacc()
                    for kc in range(8):
                        mm(pz[:, :], hT[kc][:, cc], szv[:, kc, :], kc == 0, kc == 7, [hT[kc], sz_], [pz])
                    pyd = nacc()
                    for hl in range(8):
                        for d in range(2):
                            mm(pyd[:, hl * 64:(hl + 1) * 64], Mt[d][hl // 4][:, (hl % 4) * 128:(hl % 4 + 1) * 128], x_tok[c][:, hl * 64:(hl + 1) * 64],
                               d == 0, d == 1, [Mt[d][hl // 4], x_tok[c]], [pyd])
                    tt('dve', v8(y1[:, :]), v8(pyf[:, :]), ea[c][:, 8 * g:8 * g + 8].unsqueeze(2).to_broadcast([128, 8, 64]), ALU.mult, [pyf, ea[c]], [y1])
                    tt('dve', v8(y2[:, :]), v8(pyb[:, :]), ea[c][:, 32 + 8 * g:32 + 8 * g + 8].unsqueeze(2).to_broadcast([128, 8, 64]), ALU.mult, [pyb, ea[c]], [y2])
                    cp('dve', yd[:, :], pyd[:, :], [pyd], [yd])
                    act(sz2[:, :], pz[:, :], AF.Silu, [pz], [sz2])

                def tail(c):
                    cc = slice(c * 128, (c + 1) * 128)
                    k = c % 2
                    y1, y2, yd, sz2, yn, ssc = y1_[k], y2_[k], yd_[k], sz2_[k], yn_[k], ssc_[k]
                    tt('pool', y1[:, :], y1[:, :], y2[:, :], ALU.add, [y1, y2], [y1])
                    tt('dve', v8(y2[:, :]), v8(x_tok[c][:, :]), dsum[:, 8 * g:8 * g + 8].unsqueeze(2).to_broadcast([128, 8, 64]), ALU.mult, [x_tok[c], dsum], [y2])
                    tt('pool', y1[:, :], y1[:, :], y2[:, :], ALU.add, [y1, y2], [y1])
                    tt('pool', y1[:, :], y1[:, :], yd[:, :], ALU.add, [y1, yd], [y1])
                    tt('pool', y1[:, :], y1[:, :], sz2[:, :], ALU.mult, [y1, sz2], [y1])
                    act(y2[:, :], y1[:, :], AF.Square, [y1], [y2, ssc], accum=ssc[:, 0:1])
                    act(ssc[:, 1:2], ssc[:, 0:1], AF.Ln, [ssc], [ssc], bias=EPS, scale=1.0 / 512)
                    act(ssc[:, 1:2], ssc[:, 1:2], AF.Exp, [ssc], [ssc], scale=-0.5)
                    stt(yn[:, :], y1[:, :], ssc[:, 1:2], gn_b[:, :], ALU.mult, ALU.mult, [y1, ssc, gn_b], [yn])
                    pt = nps()
                    for blk in range(4):
                        tr(pt[:, blk * 128:(blk + 1) * 128], yn[:, blk * 128:(blk + 1) * 128], ident_f[:, :], [yn, ident_f], [pt])
                    cp(alt('dve', 'act'), ynv[:, :, cc], pt[:, :].rearrange("p (k t) -> p k t", k=4), [pt], [ynT])

                if SPH >= 3:
                    head(0)
                for c in range(8 if SPH >= 3 else 0):
                    if c + 1 < 8:
                        head(c + 1)
                    tail(c)
                wo = D["ssd_w_out"]
                for oc2 in range(2 if SPH >= 4 else 0):
                    s = wload([(wo[g * 512:(g + 1) * 512, oc2 * 512:(oc2 + 1) * 512].rearrange("(kc p) n -> p kc n", p=128), 0, [128, 4, 512])])
                    sv = s[:, 0:2048].rearrange("p (kc n) -> p kc n", kc=4)
                    for o4 in range(4):
                        oc = oc2 * 4 + o4
                        for th in range(2):
                            cs = slice(th * 512, (th + 1) * 512)
                            p = nps()
                            for kc in range(4):
                                mm(p[:, :], sv[:, kc, o4 * 128:(o4 + 1) * 128], ynv[:, kc, cs], kc == 0, kc == 3, [s, ynT], [p])
                            stt(xT[oc][:, cs], p[:, :], mb[:, 16 + oc:17 + oc], xT[oc][:, cs], ALU.mult, ALU.add, [p, mb, xT[oc]], [xT[oc]])

        MIXERS['ssd'] = ssd_mixer


        if cfg.get('adaln', True):
            adaln(0)
        for i in range(nlayers):
            kind, j = i % 3, i // 3
            with ExitStack() as lst:
                if kind == 0 and en_mla:
                    norm_mod(i, 1)
                    MIXERS['mla'](i, j, lst)
                elif kind == 1 and en_conv:
                    norm_mod(i, 1)
                    MIXERS['conv'](i, j, lst)
                elif kind == 2 and en_ssd:
                    norm_mod(i, 1)
                    MIXERS['ssd'](i, j, lst)
                P.barrier()
            pump = adaln_gen(i + 1) if (i + 1 < nlayers and cfg.get('adaln', True)) else None
            if en_ffn:
                with ExitStack() as lst:
                    ffn(i, lst, pump)
                    if pump is not None:
                        for _ in pump:
                            pass
                    P.barrier()
            elif pump is not None:
                for _ in pump:
                    pass

        P.barrier()
        xstage = [P.sbuf([128, 1024], F32, f"xstageo{i}") for i in range(2)]
        for t in range(8):
            stg = xstage[t % 2]
            for half in range(2):
                p = nps()
                for q in range(4):
                    fc = half * 4 + q
                    tr(p[:, q * 128:(q + 1) * 128], xT[fc][:, t * 128:(t + 1) * 128], ident_f[:, :], [xT[fc], ident_f], [p])
                cp(alt('dve', 'act'), stg[:, half * 512:(half + 1) * 512], p[:, :], [p], [stg])
            P.dma('sp', D["y"][t * 128:(t + 1) * 128, :], stg[:, :], reads=[stg], sembuf=stg, is_output=True)
        P.finish()
        P.emit()
    nc._used_inputs = USED_INPUTS
    return nc, USED_INPUTS


def _partner(d):
    if d < 64:
        return d
    e = d - 64
    blk, r = e // 16, e % 16
    return 64 + blk * 16 + (r + 8) % 16


def _rope_tables(is_sample):
    T = 1024
    cosq = np.ones((96, T), np.float32)
    sinq = np.zeros((96, T), np.float32)
    if is_sample:
        t = np.arange(T)
        row = (t // 64).astype(np.float32)
        col = (t % 64).astype(np.float32)
        inv = (10000.0 ** (-np.arange(0, 16, 2, dtype=np.float32) / 16)).astype(np.float32)
        ang_r = row[None, :] * inv[:, None]
        ang_c = col[None, :] * inv[:, None]
        for blk, ang in ((0, ang_r), (1, ang_c)):
            c = np.cos(ang).astype(np.float32)
            s = np.sin(ang).astype(np.float32)
            b = 64 + blk * 16
            cosq[b:b + 8] = c
            cosq[b + 8:b + 16] = c
            sinq[b:b + 8] = -s
            sinq[b + 8:b + 16] = s
    ropeq = np.stack([cosq, sinq])
    cosk = np.ones((96, 1280), np.float32)
    sink = np.zeros((96, 1280), np.float32)
    cosk[:, 256:] = cosq
    sink[:, 256:] = sinq
    ropek = np.stack([cosk, sink])
    return ropeq, ropek


_NC_CACHE = {}


def _get_nc(cfg_key, cfg):
    if cfg_key not in _NC_CACHE:
        _NC_CACHE[cfg_key] = build(cfg)
    return _NC_CACHE[cfg_key]


def kernel(_cfg=None, **inp):
    f32 = np.float32
    g = {k: np.asarray(v) for k, v in inp.items()}
    cfg = _cfg or {}
    perm = np.array([h * 96 + _partner(d) for h in range(16) for d in range(96)])
    w_uq = np.ascontiguousarray(g["mla_w_uq"], f32)
    w_uq_sw = np.ascontiguousarray(w_uq[:, :, perm])
    wk = np.zeros((2, 384, 1536), f32)
    wk_sw = np.zeros((2, 384, 1536), f32)
    ukv = g["mla_w_ukv"].reshape(2, 256, 16, 128)
    for h in range(16):
        wk[:, 0:256, h * 96:h * 96 + 64] = ukv[:, :, h, 0:64]
        wk_sw[:, 0:256, h * 96:h * 96 + 64] = ukv[:, :, h, 0:64]
        for e in range(32):
            wk[:, 256 + e, h * 96 + 64 + e] = 1.0
            wk_sw[:, 256 + (_partner(64 + e) - 64), h * 96 + 64 + e] = 1.0
    w_ukv_v = np.ascontiguousarray(ukv[:, :, :, 64:128].reshape(2, 256, 1024))
    pq = np.array([_partner(d) for d in range(96)])

    def pad128(v):
        o = np.zeros((1, 128), f32)
        o[0, :v.shape[0]] = v
        return o

    tri = np.zeros((3, 128, 128), f32)
    k_ = np.arange(128)
    tri[0] = (k_[:, None] <= k_[None, :])
    tri[1] = (k_[:, None] >= k_[None, :])
    negmask = np.zeros((2, 128, 128), f32)
    negmask[0] = np.where(k_[None, :] >= k_[:, None], 0.0, NEG)
    negmask[1] = np.where(k_[None, :] <= k_[:, None], 0.0, NEG)
    ssd_rows = np.zeros((4, 64), f32)
    ssd_rows[0] = g["ssd_dt_bias"][0].reshape(64)
    ssd_rows[1] = g["ssd_a_log"][0].reshape(64)
    ssd_rows[2] = g["ssd_d"][0].reshape(64)

    shared = {
        "w_ada": g["w_ada"], "ffn_w_in": g["ffn_w_in"], "ffn_w_out": g["ffn_w_out"],
        "mla_w_dq": g["mla_w_dq"], "w_uq": w_uq, "w_uq_sw": w_uq_sw, "mla_w_dkv": g["mla_w_dkv"],
        "wk": wk, "wk_sw": wk_sw, "w_ukv_v": w_ukv_v, "mla_w_o": g["mla_w_o"],
        "cv_w_pw1": g["cv_w_pw1"][0], "cv_w_pw2": g["cv_w_pw2"][0],
        "ssd_w_in": g["ssd_w_in"][0], "ssd_w_out": g["ssd_w_out"][0],
        "ssd_rows": ssd_rows, "bconv_row": g["ssd_b_conv"][0].reshape(1, 3072), "gnorm_row": g["ssd_g_norm"][0].reshape(1, 2048),
        "tri": tri, "negmask": negmask,
    }
    shared = {k: np.ascontiguousarray(v, f32) for k, v in shared.items()}
    in_maps = []
    for c in range(8):
        is_sample = c >= 4
        m = dict(shared)
        if is_sample:
            b = c - 4
            m["x"] = np.ascontiguousarray(g["x_sample"][b], f32)
            cond = g["c"][b]
            m["cache_ckv"] = np.ascontiguousarray(g["cache_ckv"][b], f32)
            m["cache_kpe"] = np.ascontiguousarray(g["cache_kpe"][b], f32)
            m["h0"] = np.ascontiguousarray(g["state_ssm"][b, 0], f32)
            maskb = np.zeros((128, 40), f32)
            m["flag"] = np.ones((128, 1), f32)
        else:
            m["x"] = np.ascontiguousarray(g["x_prompt"][4 * c:4 * c + 4].reshape(1024, 1024), f32)
            cond = g["c_ctx"]
            m["cache_ckv"] = np.zeros((2, 256, 256), f32)
            m["cache_kpe"] = np.zeros((2, 256, 32), f32)
            m["h0"] = np.zeros((2, 32, 64, 128), f32)
            maskb = np.full((128, 40), NEG, f32)
            for kc in range(2, 10):
                maskb[:, kc * 4 + (kc - 2) // 2] = 0.0
            m["flag"] = np.zeros((128, 1), f32)
        m["maskb"] = maskb
        mq = np.zeros((8, 1024), f32)
        mk = np.zeros((8, 1280), f32)
        if not is_sample:
            for jj in range(4):
                mq[jj, jj * 256:(jj + 1) * 256] = 1.0
                mk[jj, :] = NEG
                mk[jj, 256 + jj * 256:256 + (jj + 1) * 256] = 0.0
        m["maskq"], m["maskk"] = mq, mk
        rq, rk = _rope_tables(is_sample)
        m["ropeq"], m["ropek"] = rq, rk
        rows = []
        for i in range(4):
            rows += [g["g_norm1"][i].reshape(8, 128), g["g_norm2"][i].reshape(8, 128), g["b_ada"][i].reshape(48, 128)]
        rows += [np.asarray(cond).reshape(8, 128)]
        for j in range(2):
            rows += [g["mla_g_q"][j].reshape(3, 128), g["mla_g_kv"][j].reshape(2, 128),
                     pad128(g["mla_g_qn"][j]), pad128(g["mla_g_qn"][j][pq]),
                     pad128(g["mla_g_kn"][j]), pad128(g["mla_g_kn"][j][pq])]
        rows += [g["cv_b_pw1"][0].reshape(16, 128), g["cv_w_dw"][0].reshape(248, 128), g["cv_b_dw"][0].reshape(8, 128),
                 g["cv_g_ln"][0].reshape(8, 128), g["cv_b_ln"][0].reshape(8, 128), g["cv_b_pw2"][0].reshape(8, 128)]
        rows += [g["ssd_w_conv"][0].reshape(120, 128), g["ssd_b_conv"][0].reshape(24, 128)]
        v = np.concatenate([np.asarray(r, f32) for r in rows], axis=0)
        vecs = np.zeros((NVEC, 128), f32)
        vecs[:v.shape[0]] = v
        m["vecs"] = vecs
        in_maps.append(m)

    nc, used = _get_nc(str(sorted(cfg.items())), cfg)
    in_maps = [{k: m[k] for k in used} for m in in_maps]
    res = run_bass_kernel_spmd(nc, in_maps, core_ids=list(range(8)))
    R = res.results
    y_prompt = np.stack([R[c]["y"].reshape(4, 256, 1024) for c in range(4)]).reshape(16, 256, 1024)
    y_sample = np.stack([R[4 + b]["y"] for b in range(4)])
    _z = {"ockv": np.zeros((2, 1024, 256), f32), "okpe": np.zeros((2, 1024, 32), f32), "ossm": np.zeros((4, 2, 32, 64, 128), f32)}
    R = [{k: (r[k] if k in r else _z[k]) for k in ("y", "ockv", "okpe", "ossm")} for r in R]
    new_ckv = np.stack([R[c]["ockv"].reshape(2, 4, 256, 256).transpose(1, 0, 2, 3) for c in range(4)]).reshape(16, 2, 256, 256)
    new_kpe = np.stack([R[c]["okpe"].reshape(2, 4, 256, 32).transpose(1, 0, 2, 3) for c in range(4)]).reshape(16, 2, 256, 32)
    new_ssm = np.stack([R[c]["ossm"] for c in range(4)]).reshape(16, 1, 2, 32, 64, 128)
    outs = (y_prompt.astype(f32), y_sample.astype(f32), np.ascontiguousarray(new_ckv, f32),
            np.ascontiguousarray(new_kpe, f32), np.ascontiguousarray(new_ssm, f32))
    return outs
```

```python
import numpy as np
import concourse.bass as bass
import concourse.mybir as mybir
from concourse.bass_utils import run_bass_kernel_spmd
from contextlib import ExitStack

F32 = mybir.dt.float32
BF16 = mybir.dt.bfloat16
AF = mybir.ActivationFunctionType
ALU = mybir.AluOpType
AX = mybir.AxisListType

ENGS = ['pe', 'act', 'dve', 'pool', 'sp']


class Buf:
    def __init__(self, t, name):
        self.t = t
        self.name = name
        self.writer = None
        self.readers = {}
        self.semkey = None
        self.dcount = 0
        self.is_psum = False

    def __getitem__(self, idx):
        return self.t[idx]


class Prog:
    def __init__(self, nc, stack, self_wait=True):
        self.nc = nc
        self.stack = stack
        self.ops = {e: [] for e in ENGS}
        self.count = {e: 0 for e in ENGS}
        self.semh = {}
        self.obs = {e: {} for e in ENGS}
        self.self_wait = self_wait
        for e in ENGS:
            self.semh[e] = stack.enter_context(nc.semaphore("sem_" + e))
        self.nbuf = 0
        self.out_tokens = []
        self.dma_final = {}
        self.used_names = set()
        self.eng = {'pe': nc.tensor, 'act': nc.scalar, 'dve': nc.vector, 'pool': nc.gpsimd, 'sp': nc.sync}

    def _emit(self, eng, waits, fn, inc):
        e = self.eng[eng]
        for (k, v) in waits:
            e.wait_ge(self.semh[k], v)
        if fn is not None:
            ins = fn(e)
            ins.then_inc(self.semh[inc[0]], inc[1])

    def sbuf(self, shape, dtype, name=None, stack=None):
        self.nbuf += 1
        name = name or f"sb{self.nbuf}"
        if name in self.used_names:
            name = f"{name}_u{self.nbuf}"
        self.used_names.add(name)
        t = (stack or self.stack).enter_context(self.nc.sbuf_tensor(name, list(shape), dtype))
        return Buf(t, name)

    def psum(self, shape, dtype=F32, name=None, stack=None):
        self.nbuf += 1
        name = name or f"ps{self.nbuf}"
        t = (stack or self.stack).enter_context(self.nc.psum_tensor(name, list(shape), dtype))
        b = Buf(t, name)
        b.is_psum = True
        return b

    def _dsem(self, b):
        if b.semkey is None:
            b.semkey = "d_" + b.name
            self.semh[b.semkey] = self.stack.enter_context(self.nc.semaphore("dsem_" + b.name))
        return b.semkey

    def _deps(self, eng, reads, writes):
        deps = set()
        for b in reads:
            if b.writer is not None:
                deps.add(b.writer)
            if b.is_psum:
                for rk, rt in b.readers.items():
                    if rk != eng:
                        deps.add(rt)
        for b in writes:
            if b.writer is not None:
                deps.add(b.writer)
            deps.update(b.readers.values())
        waits = []
        for (k, v) in sorted(deps, key=lambda kv: (str(kv[0]), kv[1])):
            if k == eng and (eng == 'pe' or not self.self_wait):
                continue
            if self.obs[eng].get(k, 0) < v:
                waits.append((k, v))
                self.obs[eng][k] = v
        return waits

    def op(self, eng, fn, reads=(), writes=()):
        waits = self._deps(eng, reads, writes)
        self.count[eng] += 1
        tok = (eng, self.count[eng])
        for b in reads:
            b.readers[eng] = tok
        for b in writes:
            b.writer = tok
            b.readers = {}
        self._emit(eng, waits, fn, (eng, 1))

    def dma(self, q, out_ap, in_ap, reads=(), writes=(), sembuf=None, is_output=False, **kw):
        waits = self._deps(q, reads, writes)
        k = self._dsem(sembuf)
        sembuf.dcount += 16
        tok = (k, sembuf.dcount)
        self.dma_final[k] = sembuf.dcount
        for b in reads:
            b.readers['dma_' + k] = tok
        for b in writes:
            b.writer = tok
            b.readers = {}
        if is_output:
            self.out_tokens.append(tok)
        fn = lambda e, o=out_ap, i=in_ap, kw=kw: e.dma_start(out=o, in_=i, **kw)
        self._emit(q, waits, fn, (k, 16))

    def barrier(self):
        for e in ENGS:
            waits = []
            for o in ENGS:
                if o == e or self.count[o] == 0:
                    continue
                if self.obs[e].get(o, 0) < self.count[o]:
                    waits.append((o, self.count[o]))
                    self.obs[e][o] = self.count[o]
            for k, v in self.dma_final.items():
                if self.obs[e].get(k, 0) < v:
                    waits.append((k, v))
                    self.obs[e][k] = v
            if waits:
                self._emit(e, waits, None, None)

    def finish(self):
        waits = []
        seen = {}
        for k, v in self.dma_final.items():
            seen[k] = max(seen.get(k, 0), v)
        for k, v in seen.items():
            waits.append((k, v))
        for o in ENGS:
            if o != 'sp' and self.count[o] > 0:
                waits.append((o, self.count[o]))
        self._emit('sp', waits, None, None)

    def emit(self):
        return
        nc = self.nc
        with nc.Block() as block:
            def run(eng_name, e):
                for (waits, fn, inc) in self.ops[eng_name]:
                    for (k, v) in waits:
                        e.wait_ge(self.semh[k], v)
                    if fn is not None:
                        ins = fn(e)
                        ins.then_inc(self.semh[inc[0]], inc[1])

            @block.tensor
            def _(e):
                run('pe', e)

            @block.scalar
            def _(e):
                run('act', e)

            @block.vector
            def _(e):
                run('dve', e)

            @block.gpsimd
            def _(e):
                run('pool', e)

            @block.sync
            def _(e):
                run('sp', e)


D_MODEL = 1024
NT = 1024
EPS = 1e-6
FFN_H = 2816
NEG = -30000.0

VEC_LAYOUT = []
for _i in range(4):
    VEC_LAYOUT += [(f"g1_{_i}", 8), (f"g2_{_i}", 8), (f"bada_{_i}", 48)]
VEC_LAYOUT += [("cond", 8)]
for _j in range(2):
    VEC_LAYOUT += [(f"gq_{_j}", 3), (f"gkv_{_j}", 2), (f"gqn_{_j}", 1), (f"gqnsw_{_j}", 1), (f"gkn_{_j}", 1), (f"gknsw_{_j}", 1)]
VEC_LAYOUT += [("bpw1", 16), ("wdw", 248), ("bdw", 8), ("gln", 8), ("bln", 8), ("bpw2", 8)]
VEC_LAYOUT += [("wconv", 120), ("bconv", 24)]
VEC_BASE = {}
_o = 0
for _n, _r in VEC_LAYOUT:
    VEC_BASE[_n] = _o
    _o += _r
NVEC = ((_o + 127) // 128) * 128


def build(cfg=None):
    cfg = cfg or {}
    en_mla = cfg.get("mla", True)
    en_conv = cfg.get("conv", True)
    en_ssd = cfg.get("ssd", True)
    en_ffn = cfg.get("ffn", True)
    nlayers = cfg.get("nlayers", 4)

    nc = bass.Bass("TRN2", target_bir_lowering=False)


    IN_SHAPES = {
        "x": [NT, 1024], "vecs": [NVEC, 128], "w_ada": [4, 1024, 6144],
        "ffn_w_in": [4, 1024, 5632], "ffn_w_out": [4, 2816, 1024],
        "mla_w_dq": [2, 1024, 384], "w_uq": [2, 384, 1536], "w_uq_sw": [2, 384, 1536],
        "mla_w_dkv": [2, 1024, 288], "wk": [2, 384, 1536], "wk_sw": [2, 384, 1536],
        "w_ukv_v": [2, 256, 1024], "mla_w_o": [2, 1024, 1024],
        "cache_ckv": [2, 256, 256], "cache_kpe": [2, 256, 32],
        "ropeq": [2, 96, 1024], "ropek": [2, 96, 1280], "maskb": [128, 40], "maskq": [8, 1024], "maskk": [8, 1280],
        "cv_w_pw1": [1024, 2048], "cv_w_pw2": [1024, 1024],
        "ssd_w_in": [1024, 5184], "ssd_w_out": [2048, 1024],
        "h0": [2, 32, 64, 128], "ssd_rows": [4, 64], "bconv_row": [1, 3072], "gnorm_row": [1, 2048],
        "flag": [128, 1], "tri": [3, 128, 128], "negmask": [2, 128, 128],
    }

    class _LazyD(dict):
        def __missing__(self, name):
            if name in OUT_SHAPES:
                ap = nc.dram_tensor(name, list(OUT_SHAPES[name]), F32, kind="ExternalOutput").ap()
                self[name] = ap
                return ap
            ap = nc.dram_tensor(name, list(IN_SHAPES[name]), F32, kind="ExternalInput").ap()
            self[name] = ap
            USED_INPUTS.append(name)
            return ap

    USED_INPUTS = []
    OUT_SHAPES = {"y": [NT, 1024], "ockv": [2, NT, 256], "okpe": [2, NT, 32], "ossm": [4, 2, 32, 64, 128]}
    D = _LazyD()
    with ExitStack() as st:
        P = Prog(nc, st)

        def mm(out, lhsT, rhs, start, stop, R, W):
            P.op('pe', lambda e: e.matmul(out, lhsT=lhsT, rhs=rhs, start=start, stop=stop), R, W)

        def tr(out, in_, ident, R, W):
            P.op('pe', lambda e: e.transpose(out=out, in_=in_, identity=ident), R, W)

        def act(out, in_, func, R, W, bias=None, scale=None, accum=None):
            kw = {}
            if bias is not None:
                kw['bias'] = bias
            if scale is not None:
                kw['scale'] = scale
            if accum is not None:
                kw['accum_out'] = accum
            P.op('act', lambda e: e.activation(out=out, in_=in_, func=func, **kw), R, W)

        def tt(eng, out, in0, in1, op, R, W):
            P.op(eng, lambda e: e.tensor_tensor(out=out, in0=in0, in1=in1, op=op), R, W)

        def ts(eng, out, in0, s1, s2, op0, op1, R, W):
            if op1 is None:
                P.op(eng, lambda e: e.tensor_scalar(out=out, in0=in0, scalar1=s1, scalar2=None, op0=op0), R, W)
            else:
                P.op(eng, lambda e: e.tensor_scalar(out=out, in0=in0, scalar1=s1, scalar2=s2, op0=op0, op1=op1), R, W)

        def stt(out, in0, scalar, in1, op0, op1, R, W):
            P.op('dve', lambda e: e.scalar_tensor_tensor(out=out, in0=in0, scalar=scalar, in1=in1, op0=op0, op1=op1), R, W)

        def cp(eng, out, in_, R, W):
            if eng == 'act':
                P.op('act', lambda e: e.copy(out=out, in_=in_), R, W)
            else:
                P.op(eng, lambda e: e.tensor_copy(out=out, in_=in_), R, W)

        def memset(eng, ap, val, W):
            P.op(eng, lambda e: e.memset(ap, val), (), W)

        _rr = {'n': 0}

        def alt(*engs):
            _rr['n'] += 1
            return engs[_rr['n'] % len(engs)]

        NPB = cfg.get("npsum", 8)
        PS = [P.psum([128, 512], F32, f"psb{i}") for i in range(NPB)]
        _ps = {'n': 0}

        def nps():
            _ps['n'] += 1
            return PS[_ps['n'] % (NPB - 3)]

        _pa = {'n': 0}

        def nacc():
            _pa['n'] += 1
            return PS[NPB - 2 + _pa['n'] % 2]

        xT = [P.sbuf([128, NT], F32, f"xT{i}") for i in range(8)]
        hT = [P.sbuf([128, NT], BF16, f"hT{i}") for i in range(8)]
        ident_f = P.sbuf([128, 128], F32, "ident_f")
        ident_b = P.sbuf([128, 128], BF16, "ident_b")
        ones_f = P.sbuf([128, 128], F32, "ones_f")
        ones_b = P.sbuf([128, 128], BF16, "ones_b")
        vcols = P.sbuf([128, NVEC], F32, "vcols")
        modb = [P.sbuf([128, 64], F32, f"modb{i}") for i in range(4)]
        s_bf = P.sbuf([128, 8], BF16, "s_bf")
        flag = P.sbuf([128, 1], F32, "flag_sb")
        NSLOT = 4
        SLOTW = 4096
        slots = [P.sbuf([128, SLOTW], BF16, f"wslot{i}") for i in range(NSLOT)]
        _sl = {'n': 0}

        def wload(dram_aps):
            _sl['n'] += 1
            s = slots[_sl['n'] % NSLOT]
            for (ap, off, shape) in dram_aps:
                n = 1
                for d in shape[1:]:
                    n *= d
                dst = s[:, off:off + n]
                if len(shape) == 3:
                    dst = dst.rearrange("p (a b) -> p a b", a=shape[1])
                P.dma('pool', dst, ap, writes=[s], sembuf=s)
            return s

        def vc(name, idx=0, n=1):
            b = VEC_BASE[name] + idx
            return vcols[:, b:b + n]

        sq_r = [P.sbuf([128, 512], BF16, f"sq_r{i}") for i in range(4)]
        rstd_t = P.sbuf([128, 512], F32, "rstd_t")
        tmp_r = [P.sbuf([128, 512], F32, f"tmp_r{i}") for i in range(3)]
        _tm = {'n': 0}

        def ntmp():
            _tm['n'] += 1
            return tmp_r[_tm['n'] % 3]

        memset('pool', ident_f[:, :], 1.0, [ident_f])
        P.op('pool', lambda e: e.affine_select(out=ident_f[:, :], in_=ident_f[:, :], pattern=[[-1, 128]],
                                               compare_op=ALU.is_equal, fill=0.0, base=0, channel_multiplier=1),
             [ident_f], [ident_f])
        cp('pool', ident_b[:, :], ident_f[:, :], [ident_f], [ident_b])
        memset('pool', ones_f[:, :], 1.0, [ones_f])
        memset('pool', ones_b[:, :], 1.0, [ones_b])
        P.dma('sp', flag[:, :], D["flag"], writes=[flag], sembuf=flag)

        s0 = ExitStack()
        xstage = [P.sbuf([128, 1024], F32, f"xstage{i}", stack=s0) for i in range(2)]
        for blk in range(NVEC // 128 if cfg.get('stop', 9) > 1 else 0):
            stg = xstage[blk % 2]
            P.dma('sp', stg[:, 0:128], D["vecs"][blk * 128:(blk + 1) * 128, :], writes=[stg], sembuf=stg)
            p = nps()
            tr(p[:, 0:128], stg[:, 0:128], ident_f[:, :], [stg, ident_f], [p])
            cp(alt('dve', 'act'), vcols[:, blk * 128:(blk + 1) * 128], p[:, 0:128], [p], [vcols])
        if cfg.get('stop', 9) > 2:
            act(s_bf[:, :], vc("cond", 0, 8), AF.Silu, [vcols], [s_bf])

        for t in range(cfg.get('nx', 8) if cfg.get('stop', 9) > 3 else 0):
            stg = xstage[t % 2]
            if cfg.get('xsplit', 1) == 1:
                P.dma('sp', stg[:, :], D["x"][t * 128:(t + 1) * 128, :], writes=[stg], sembuf=stg)
            else:
                for q8 in range(8):
                    P.dma('sp', stg[:, q8 * 128:(q8 + 1) * 128], D["x"][t * 128:(t + 1) * 128, q8 * 128:(q8 + 1) * 128], writes=[stg], sembuf=stg)
            for half in range(0 if cfg.get('noxt') else cfg.get('nhalf', 2)):
                p = nps()
                for q in range(cfg.get('nq', 4)):
                    fc = half * 4 + q
                    tr(p[:, q * 128:(q + 1) * 128], stg[:, fc * 128:(fc + 1) * 128], ident_f[:, :], [stg, ident_f], [p])
                for q in range(cfg.get('nq', 4) if not cfg.get('nocp') else 0):
                    fc = half * 4 + q
                    cp(alt('dve', 'act') if not cfg.get('cpdve') else 'dve', xT[fc][:, t * 128:(t + 1) * 128], p[:, q * 128:(q + 1) * 128], [p], [xT[fc]])

        P.barrier()
        s0.close()

        def adaln_gen(i):
            p = PS[NPB - 3]
            for k in range(12):
                s = wload([(D["w_ada"][i, :, k * 512:(k + 1) * 512].rearrange("(kc p) n -> p kc n", p=128), 0, [128, 8, 512])])
                sv = s[:, :].rearrange("p (kc n) -> p kc n", kc=8)
                for o4 in range(4):
                    oc = k * 4 + o4
                    for kc in range(8):
                        mm(p[:, oc:oc + 1], sv[:, kc, o4 * 128:(o4 + 1) * 128], s_bf[:, kc:kc + 1], kc == 0, kc == 7, [s, s_bf], [p])
                yield
            mb = modb[i]
            tt('dve', mb[:, 0:48], p[:, 0:48], vc(f"bada_{i}", 0, 48), ALU.add, [p, vcols], [mb])
            stt(mb[:, 48:56], mb[:, 8:16], 1.0, vc(f"g1_{i}", 0, 8), ALU.add, ALU.mult, [mb, vcols], [mb])
            stt(mb[:, 56:64], mb[:, 32:40], 1.0, vc(f"g2_{i}", 0, 8), ALU.add, ALU.mult, [mb, vcols], [mb])
            ts('dve', mb[:, 48:64], mb[:, 48:64], 32.0, None, ALU.mult, None, [mb], [mb])

        def adaln(i):
            for _ in adaln_gen(i):
                pass

        def norm_mod(i, which):
            mb = modb[i]
            acol = 48 if which == 1 else 56
            bcol = 0 if which == 1 else 24
            for th in range(2):
                cs = slice(th * 512, (th + 1) * 512)
                p = nps()
                for fc in range(8):
                    sq = sq_r[fc % 4]
                    act(sq[:, :], xT[fc][:, cs], AF.Square, [xT[fc]], [sq])
                    mm(p[:, :], ones_b[:, :], sq[:, :], fc == 0, fc == 7, [ones_b, sq], [p])
                act(rstd_t[:, :], p[:, :], AF.Ln, [p], [rstd_t], bias=1024.0 * EPS)
                act(rstd_t[:, :], rstd_t[:, :], AF.Exp, [rstd_t], [rstd_t], scale=-0.5)
                for fc in range(8):
                    tm = ntmp()
                    stt(tm[:, :], xT[fc][:, cs], mb[:, acol + fc:acol + fc + 1], rstd_t[:, :], ALU.mult, ALU.mult,
                        [xT[fc], mb, rstd_t], [tm])
                    act(hT[fc][:, cs], tm[:, :], AF.Identity, [tm, mb], [hT[fc]], bias=mb[:, bcol + fc:bcol + fc + 1])

        def ffn(i, lst, pump=None):
            norm_mod(i, 2)
            gT = [P.sbuf([128, NT], BF16, f"gT{i}_{k}", stack=lst) for k in range(22)]
            sa_r = [P.sbuf([128, 512], F32, f"sa{i}_{k}", stack=lst) for k in range(2)]
            mb = modb[i]
            win = D["ffn_w_in"]
            for hb in range(11):
                s = wload([
                    (win[i, :, hb * 256:(hb + 1) * 256].rearrange("(kc p) n -> p kc n", p=128), 0, [128, 8, 256]),
                    (win[i, :, 2816 + hb * 256:2816 + (hb + 1) * 256].rearrange("(kc p) n -> p kc n", p=128), 2048, [128, 8, 256]),
                ])
                sa_v = s[:, 0:2048].rearrange("p (kc n) -> p kc n", kc=8)
                su_v = s[:, 2048:4096].rearrange("p (kc n) -> p kc n", kc=8)
                for sub in range(2):
                    hc = hb * 2 + sub
                    for th in range(2):
                        cs = slice(th * 512, (th + 1) * 512)
                        pa = nps()
                        for kc in range(8):
                            mm(pa[:, :], sa_v[:, kc, sub * 128:(sub + 1) * 128], hT[kc][:, cs], kc == 0, kc == 7, [s, hT[kc]], [pa])
                        pu = nps()
                        for kc in range(8):
                            mm(pu[:, :], su_v[:, kc, sub * 128:(sub + 1) * 128], hT[kc][:, cs], kc == 0, kc == 7, [s, hT[kc]], [pu])
                        sa = sa_r[(hc * 2 + th) % 2]
                        act(sa[:, :], pa[:, :], AF.Silu, [pa], [sa])
                        tt('dve', gT[hc][:, cs], sa[:, :], pu[:, :], ALU.mult, [sa, pu], [gT[hc]])
                if pump is not None:
                    next(pump, None)
            wout = D["ffn_w_out"]
            for oc in range(8):
                s1 = wload([(wout[i, 0:1408, oc * 128:(oc + 1) * 128].rearrange("(kc p) n -> p kc n", p=128), 0, [128, 11, 128])])
                s2 = wload([(wout[i, 1408:2816, oc * 128:(oc + 1) * 128].rearrange("(kc p) n -> p kc n", p=128), 0, [128, 11, 128])])
                v1 = s1[:, 0:1408].rearrange("p (kc n) -> p kc n", kc=11)
                v2 = s2[:, 0:1408].rearrange("p (kc n) -> p kc n", kc=11)
                for th in range(2):
                    cs = slice(th * 512, (th + 1) * 512)
                    p = nps()
                    for hc in range(22):
                        sv, ss = (v1, s1) if hc < 11 else (v2, s2)
                        mm(p[:, :], sv[:, hc % 11, :], gT[hc][:, cs], hc == 0, hc == 21, [ss, gT[hc]], [p])
                    stt(xT[oc][:, cs], p[:, :], mb[:, 40 + oc:41 + oc], xT[oc][:, cs], ALU.mult, ALU.add, [p, mb, xT[oc]], [xT[oc]])

        MIXERS = {}
        def conv_mixer(i, j, lst):
            mb = modb[i]
            ubuf = [P.sbuf([128, 4 * 286], BF16, f"ubuf{c}", stack=lst) for c in range(8)]
            vbuf = [P.sbuf([128, NT], F32, f"vbuf{c}", stack=lst) for c in range(8)]
            sg_r = [P.sbuf([128, 512], F32, f"sg{k}", stack=lst) for k in range(2)]
            DwE = [P.sbuf([128, 16 * 128], BF16, f"DwE{k}", stack=lst) for k in range(2)]
            DwO = [P.sbuf([128, 15 * 128], BF16, f"DwO{k}", stack=lst) for k in range(2)]
            mean_t = P.sbuf([128, 512], F32, "cv_mean", stack=lst)
            var_t = P.sbuf([128, 512], F32, "cv_var", stack=lst)
            w1 = D["cv_w_pw1"]
            for c in range(8):
                memset('pool', ubuf[c][:, :], 0.0, [ubuf[c]])
            WS = {}

            def tap(cc, w):
                if w % 2 == 0:
                    return DwE[cc % 2], DwE[cc % 2][:, (w // 2) * 128:(w // 2 + 1) * 128]
                return DwO[cc % 2], DwO[cc % 2][:, (w // 2) * 128:(w // 2 + 1) * 128]

            def build_dw(cc):
                for w in range(31):
                    b, ap = tap(cc, w)
                    if w % 2 == 0:
                        ts('dve', ap, ident_f[:, :], vc("wdw", w * 8 + cc), None, ALU.mult, None, [ident_f, vcols], [b])
                    else:
                        act(ap, ident_f[:, :], AF.Identity, [ident_f, vcols], [b], scale=vc("wdw", w * 8 + cc))

            def pw1(cc):
                c2, sub = cc // 2, cc % 2
                if sub == 0:
                    WS[c2] = wload([
                        (w1[:, c2 * 256:(c2 + 1) * 256].rearrange("(kc p) n -> p kc n", p=128), 0, [128, 8, 256]),
                        (w1[:, 1024 + c2 * 256:1024 + (c2 + 1) * 256].rearrange("(kc p) n -> p kc n", p=128), 2048, [128, 8, 256]),
                    ])
                s = WS[c2]
                sa_v = s[:, 0:2048].rearrange("p (kc n) -> p kc n", kc=8)
                sg_v = s[:, 2048:4096].rearrange("p (kc n) -> p kc n", kc=8)
                ub = ubuf[cc][:, :].rearrange("p (s t) -> p s t", s=4)
                for th in range(2):
                    cs = slice(th * 512, (th + 1) * 512)
                    pa = nps()
                    for kc in range(8):
                        mm(pa[:, :], sa_v[:, kc, sub * 128:(sub + 1) * 128], hT[kc][:, cs], kc == 0, kc == 7, [s, hT[kc]], [pa])
                    pg = nps()
                    for kc in range(8):
                        mm(pg[:, :], sg_v[:, kc, sub * 128:(sub + 1) * 128], hT[kc][:, cs], kc == 0, kc == 7, [s, hT[kc]], [pg])
                    sg = sg_r[th]
                    act(sg[:, :], pg[:, :], AF.Sigmoid, [pg, vcols], [sg], bias=vc("bpw1", 8 + cc))
                    stt(ub[:, 2 * th:2 * th + 2, 15:271], pa[:, :].rearrange("p (s t) -> p s t", s=2), vc("bpw1", cc),
                        sg[:, :].rearrange("p (s t) -> p s t", s=2), ALU.add, ALU.mult, [pa, sg, vcols], [ubuf[cc]])
                ts('dve', ub[:, 1:4, 0:15], ub[:, 0:3, 256:271], flag[:, 0:1], None, ALU.mult, None, [ubuf[cc], flag], [ubuf[cc]])
                ts('dve', ub[:, 0:3, 271:286], ub[:, 1:4, 15:30], flag[:, 0:1], None, ALU.mult, None, [ubuf[cc], flag], [ubuf[cc]])

            def dconv(cc):
                ub = ubuf[cc][:, :].rearrange("p (s t) -> p s t", s=4)
                for sp in range(2):
                    p = nps()
                    for w in range(31):
                        b, ap = tap(cc, w)
                        mm(p[:, :], ap, ub[:, 2 * sp:2 * sp + 2, w:w + 256], w == 0, w == 30, [b, ubuf[cc]], [p])
                    act(vbuf[cc][:, sp * 512:(sp + 1) * 512], p[:, :], AF.Identity, [p, vcols], [vbuf[cc]], bias=vc("bdw", cc))

            pw1(0)
            build_dw(0)
            for cc in range(8):
                if cc + 1 < 8:
                    pw1(cc + 1)
                    build_dw(cc + 1)
                dconv(cc)
            for th in range(2):
                cs = slice(th * 512, (th + 1) * 512)
                p1 = nps()
                for cc in range(8):
                    mm(p1[:, :], ones_f[:, :], vbuf[cc][:, cs], cc == 0, cc == 7, [ones_f, vbuf[cc]], [p1])
                p2 = nps()
                for cc in range(8):
                    sq = sq_r[cc % 2]
                    act(sq[:, :], vbuf[cc][:, cs], AF.Square, [vbuf[cc]], [sq])
                    mm(p2[:, :], ones_b[:, :], sq[:, :], cc == 0, cc == 7, [ones_b, sq], [p2])
                act(mean_t[:, :], p1[:, :], AF.Identity, [p1], [mean_t], scale=1.0 / 1024)
                tt('dve', var_t[:, :], mean_t[:, :], mean_t[:, :], ALU.mult, [mean_t], [var_t])
                stt(var_t[:, :], p2[:, :], 1.0 / 1024, var_t[:, :], ALU.mult, ALU.subtract, [p2, var_t], [var_t])
                act(rstd_t[:, :], var_t[:, :], AF.Ln, [var_t], [rstd_t], bias=EPS)
                act(rstd_t[:, :], rstd_t[:, :], AF.Exp, [rstd_t], [rstd_t], scale=-0.5)
                for cc in range(8):
                    tm = ntmp()
                    tt('dve', tm[:, :], vbuf[cc][:, cs], mean_t[:, :], ALU.subtract, [vbuf[cc], mean_t], [tm])
                    tt('pool', tm[:, :], tm[:, :], rstd_t[:, :], ALU.mult, [tm, rstd_t], [tm])
                    act(hT[cc][:, cs], tm[:, :], AF.Silu, [tm, vcols], [hT[cc]], bias=vc("bln", cc), scale=vc("gln", cc))
            w2 = D["cv_w_pw2"]
            for oc2 in range(2):
                s = wload([(w2[:, oc2 * 512:(oc2 + 1) * 512].rearrange("(kc p) n -> p kc n", p=128), 0, [128, 8, 512])])
                sv = s[:, :].rearrange("p (kc n) -> p kc n", kc=8)
                for o4 in range(4):
                    oc = oc2 * 4 + o4
                    for th in range(2):
                        cs = slice(th * 512, (th + 1) * 512)
                        p = nps()
                        for kc in range(8):
                            mm(p[:, :], sv[:, kc, o4 * 128:(o4 + 1) * 128], hT[kc][:, cs], kc == 0, kc == 7, [s, hT[kc]], [p])
                        tm = ntmp()
                        ts('dve', tm[:, :], p[:, :], vc("bpw2", oc), None, ALU.add, None, [p, vcols], [tm])
                        stt(xT[oc][:, cs], tm[:, :], mb[:, 16 + oc:17 + oc], xT[oc][:, cs], ALU.mult, ALU.add, [tm, mb, xT[oc]], [xT[oc]])

        MIXERS['conv'] = conv_mixer

        def mla_mixer(i, j, lst):
            mb = modb[i]
            cqf = [P.sbuf([128, 512], F32, f"cqf{k}", stack=lst) for k in range(3)]
            cqn = [P.sbuf([128, NT], BF16, f"cqn{k}", stack=lst) for k in range(3)]
            kvl = [P.sbuf([128, 1280], BF16, f"kvl{k}", stack=lst) for k in range(3)]
            rq = P.sbuf([96, 2048], F32, "rq", stack=lst)
            rk = P.sbuf([96, 2560], F32, "rk", stack=lst)
            mkb = P.sbuf([128, 40], F32, "mkb", stack=lst)
            cstg = P.sbuf([128, 2 * 288], F32, "cstg", stack=lst)
            ostg = [P.sbuf([128, 288], F32, f"ostg{k}", stack=lst) for k in range(2)]
            qf_r = [P.sbuf([104, NT], BF16, f"qf{k}", stack=lst) for k in range(2)]
            kf_r = [P.sbuf([104, 1280], BF16, f"kf{k}", stack=lst) for k in range(2)]
            vaug = P.sbuf([128, 10 * 4 * 128], BF16, "vaug", stack=lst)
            E_r = [P.sbuf([128, 512], BF16, f"E{k}", stack=lst) for k in range(4)]
            sqh = [P.sbuf([96, 512], BF16, f"sqh{k}", stack=lst) for k in range(2)]
            rs_h = [P.sbuf([96, 512], F32, f"rsh{k}", stack=lst) for k in range(2)]
            rec = [P.sbuf([128, 512], F32, f"rec{k}", stack=lst) for k in range(2)]
            oT = hT
            cnt = {'e': 0, 'h': 0}

            P.dma('sp', rq[:, 0:1024], D["ropeq"][0], writes=[rq], sembuf=rq)
            P.dma('sp', rq[:, 1024:2048], D["ropeq"][1], writes=[rq], sembuf=rq)
            P.dma('sp', rk[:, 0:1280], D["ropek"][0], writes=[rk], sembuf=rk)
            P.dma('sp', rk[:, 1280:2560], D["ropek"][1], writes=[rk], sembuf=rk)
            cv = cstg[:, :].rearrange("p (t f) -> p t f", t=2)
            P.dma('sp', cv[:, :, 0:256], D["cache_ckv"][j].rearrange("(t p) f -> p t f", p=128), writes=[cstg], sembuf=cstg)
            P.dma('sp', cv[:, :, 256:288], D["cache_kpe"][j].rearrange("(t p) f -> p t f", p=128), writes=[cstg], sembuf=cstg)

            def vcp(name, n=96):
                b = VEC_BASE[name]
                return vcols[0:n, b:b + 1]
            ts('dve', rq[:, 0:1024], rq[:, 0:1024], vcp(f"gqn_{j}"), None, ALU.mult, None, [rq, vcols], [rq])
            ts('dve', rq[:, 1024:2048], rq[:, 1024:2048], vcp(f"gqnsw_{j}"), None, ALU.mult, None, [rq, vcols], [rq])
            ts('dve', rk[:, 0:1280], rk[:, 0:1280], vcp(f"gkn_{j}"), None, ALU.mult, None, [rk, vcols], [rk])
            ts('dve', rk[:, 1280:2560], rk[:, 1280:2560], vcp(f"gknsw_{j}"), None, ALU.mult, None, [rk, vcols], [rk])
            memset('pool', vaug[:, :], 1.0, [vaug])
            for k in range(2):
                P.dma('pool', qf_r[k][96:104, :], D["maskq"], writes=[qf_r[k]], sembuf=qf_r[k])
                P.dma('pool', kf_r[k][96:104, :], D["maskk"], writes=[kf_r[k]], sembuf=kf_r[k])

            sdq = wload([(D["mla_w_dq"][j].rearrange("(kc p) n -> p kc n", p=128), 0, [128, 8, 384])])
            dqv = sdq[:, 0:3072].rearrange("p (kc n) -> p kc n", kc=8)
            sdkv = wload([(D["mla_w_dkv"][j].rearrange("(kc p) n -> p kc n", p=128), 0, [128, 8, 288])])
            dkvv = sdkv[:, 0:2304].rearrange("p (kc n) -> p kc n", kc=8)
            for th in range(2):
                cs = slice(th * 512, (th + 1) * 512)
                pss = nacc()
                for oc in range(3):
                    p = nps()
                    for kc in range(8):
                        mm(p[:, :], dqv[:, kc, oc * 128:(oc + 1) * 128], hT[kc][:, cs], kc == 0, kc == 7, [sdq, hT[kc]], [p])
                    cp('dve', cqf[oc][:, :], p[:, :], [p], [cqf[oc]])
                    sq = sq_r[oc % 2]
                    act(sq[:, :], p[:, :], AF.Square, [p], [sq])
                    mm(pss[:, :], ones_b[:, :], sq[:, :], oc == 0, oc == 2, [ones_b, sq], [pss])
                act(rstd_t[:, :], pss[:, :], AF.Ln, [pss], [rstd_t], bias=EPS, scale=1.0 / 384)
                act(rstd_t[:, :], rstd_t[:, :], AF.Exp, [rstd_t], [rstd_t], scale=-0.5)
                for oc in range(3):
                    stt(cqn[oc][:, cs], cqf[oc][:, :], vc(f"gq_{j}", oc), rstd_t[:, :], ALU.mult, ALU.mult, [cqf[oc], vcols, rstd_t], [cqn[oc]])
                pss = nacc()
                ckvf = [ntmp(), ntmp()]
                for oc in range(2):
                    p = nps()
                    for kc in range(8):
                        mm(p[:, :], dkvv[:, kc, oc * 128:(oc + 1) * 128], hT[kc][:, cs], kc == 0, kc == 7, [sdkv, hT[kc]], [p])
                    cp('dve', ckvf[oc][:, :], p[:, :], [p], [ckvf[oc]])
                    sq = sq_r[oc % 2]
                    act(sq[:, :], p[:, :], AF.Square, [p], [sq])
                    mm(pss[:, :], ones_b[:, :], sq[:, :], oc == 0, oc == 1, [ones_b, sq], [pss])
                pk = nps()
                for kc in range(8):
                    mm(pk[0:32, :], dkvv[:, kc, 256:288], hT[kc][:, cs], kc == 0, kc == 7, [sdkv, hT[kc]], [pk])
                kpef = ntmp()
                cp('dve', kpef[0:32, :], pk[0:32, :], [pk], [kpef])
                cp('act', kvl[2][0:32, 256 + th * 512:256 + (th + 1) * 512], pk[0:32, :], [pk], [kvl[2]])
                act(rstd_t[:, :], pss[:, :], AF.Ln, [pss], [rstd_t], bias=EPS, scale=1.0 / 256)
                act(rstd_t[:, :], rstd_t[:, :], AF.Exp, [rstd_t], [rstd_t], scale=-0.5)
                for oc in range(2):
                    stt(ckvf[oc][:, :], ckvf[oc][:, :], vc(f"gkv_{j}", oc), rstd_t[:, :], ALU.mult, ALU.mult, [ckvf[oc], vcols, rstd_t], [ckvf[oc]])
                    cp('act', kvl[oc][:, 256 + th * 512:256 + (th + 1) * 512], ckvf[oc][:, :], [ckvf[oc]], [kvl[oc]])
                for t4 in range(4):
                    t = th * 4 + t4
                    p = nps()
                    for oc in range(2):
                        tr(p[:, oc * 128:(oc + 1) * 128], ckvf[oc][:, t4 * 128:(t4 + 1) * 128], ident_f[:, :], [ckvf[oc], ident_f], [p])
                    tr(p[:, 256:288], kpef[0:32, t4 * 128:(t4 + 1) * 128], ident_f[0:32, 0:32], [kpef, ident_f], [p])
                    og = ostg[t % 2]
                    cp(alt('dve', 'act'), og[:, :], p[:, 0:288], [p], [og])
                    P.dma('sp', D["ockv"][j, t * 128:(t + 1) * 128, :], og[:, 0:256], reads=[og], sembuf=og, is_output=True)
                    P.dma('sp', D["okpe"][j, t * 128:(t + 1) * 128, :], og[:, 256:288], reads=[og], sembuf=og, is_output=True)

            for t in range(2):
                p = nps()
                for fc in range(2):
                    tr(p[:, fc * 128:(fc + 1) * 128], cv[:, t, fc * 128:(fc + 1) * 128], ident_f[:, :], [cstg, ident_f], [p])
                tr(p[0:32, 256:384], cv[:, t, 256:288], ident_f[:, :], [cstg, ident_f], [p])
                for fc in range(2):
                    cp(alt('dve', 'act'), kvl[fc][:, t * 128:(t + 1) * 128], p[:, fc * 128:(fc + 1) * 128], [p], [kvl[fc]])
                cp('dve', kvl[2][0:32, t * 128:(t + 1) * 128], p[0:32, 256:384], [p], [kvl[2]])

            vv = vaug[:, :].rearrange("p (k a b c) -> p k a b c", k=10, a=2, b=2)
            tpe = P.sbuf([96, 1280], F32, "tpe", stack=lst)
            KB = [(0, 512), (512, 512), (1024, 256)]
            for hg in range(4):
                sA = wload([
                    (D["w_uq"][j, :, hg * 384:(hg + 1) * 384].rearrange("(kc p) n -> p kc n", p=128), 0, [128, 3, 384]),
                    (D["w_uq_sw"][j, :, hg * 384:(hg + 1) * 384].rearrange("(kc p) n -> p kc n", p=128), 1152, [128, 3, 384]),
                    (D["w_ukv_v"][j, :, hg * 256:(hg + 1) * 256].rearrange("(kc p) n -> p kc n", p=128), 2304, [128, 2, 256]),
                ])
                wqv = sA[:, 0:1152].rearrange("p (kc n) -> p kc n", kc=3)
                wqsv = sA[:, 1152:2304].rearrange("p (kc n) -> p kc n", kc=3)
                wvv = sA[:, 2304:2816].rearrange("p (kc n) -> p kc n", kc=2)
                sB = wload([
                    (D["wk"][j, :, hg * 384:(hg + 1) * 384].rearrange("(kc p) n -> p kc n", p=128), 0, [128, 3, 384]),
                    (D["wk_sw"][j, :, hg * 384:(hg + 1) * 384].rearrange("(kc p) n -> p kc n", p=128), 1152, [128, 3, 384]),
                ])
                wkv = sB[:, 0:1152].rearrange("p (kc n) -> p kc n", kc=3)
                wksv = sB[:, 1152:2304].rearrange("p (kc n) -> p kc n", kc=3)
                for kc in range(10):
                    p = nps()
                    for c2 in range(2):
                        mm(p[:, 0:256], kvl[c2][:, kc * 128:(kc + 1) * 128], wvv[:, c2, :], c2 == 0, c2 == 1, [kvl[c2], sA], [p])
                    pv4 = p[:, 0:256].rearrange("p (a b c) -> p a b c", a=2, b=2)
                    cp('dve', vv[:, kc, :, 0, 0:64], pv4[:, :, 0, :], [p], [vaug])
                    cp('act', vv[:, kc, :, 1, 64:128], pv4[:, :, 1, :], [p], [vaug])
                ATT = [PS[0], PS[1], PS[2]]
                NRB = [PS[3], PS[4], PS[5]]

                def nr_gen(dst, dcols, mma, mmb, n, tab, toff0, toff1, tcols, tpe=None):
                    pa, pb, pss = NRB
                    for (l, r, st_, sp_, R_) in mma:
                        mm(pa[0:96, 0:n], l, r, st_, sp_, R_, [pa])
                    if tpe is None:
                        for (l, r, st_, sp_, R_) in mmb:
                            mm(pb[0:96, 0:n], l, r, st_, sp_, R_, [pb])
                    k2 = cnt['h'] % 2
                    cnt['h'] += 1
                    sq = sqh[k2]
                    act(sq[:, 0:n], pa[0:96, 0:n], AF.Square, [pa], [sq])
                    yield
                    mm(pss[0:96, 0:n], ones_b[0:96, 0:96], sq[:, 0:n], True, True, [ones_b, sq], [pss])
                    rs = rs_h[k2]
                    act(rs[:, 0:n], pss[0:96, 0:n], AF.Ln, [pss], [rs], bias=EPS, scale=1.0 / 96)
                    act(rs[:, 0:n], rs[:, 0:n], AF.Exp, [rs], [rs], scale=-0.5)
                    t1 = ntmp()
                    if tpe is None:
                        t2 = ntmp()
                        tt('dve', t1[0:96, 0:n], pa[0:96, 0:n], tab[:, toff0 + tcols.start:toff0 + tcols.stop], ALU.mult, [pa, tab], [t1])
                        tt('dve', t2[0:96, 0:n], pb[0:96, 0:n], tab[:, toff1 + tcols.start:toff1 + tcols.stop], ALU.mult, [pb, tab], [t2])
                        tt('pool', t1[0:96, 0:n], t1[0:96, 0:n], t2[0:96, 0:n], ALU.add, [t1, t2], [t1])
                        tt('pool', dst[0:96, dcols], t1[0:96, 0:n], rs[:, 0:n], ALU.mult, [t1, rs], [dst])
                    else:
                        tt('dve', t1[0:64, 0:n], pa[0:64, 0:n], tab[0:64, toff0 + tcols.start:toff0 + tcols.stop], ALU.mult, [pa, tab], [t1])
                        tt('pool', dst[0:64, dcols], t1[0:64, 0:n], rs[0:64, 0:n], ALU.mult, [t1, rs], [dst])
                        tt('pool', dst[64:96, dcols], tpe[64:96, tcols], rs[64:96, 0:n], ALU.mult, [tpe, rs], [dst])
                    yield

                def head_feeder(hl):
                    h = hg * 4 + hl
                    qf = qf_r[h % 2]
                    kf = kf_r[h % 2]
                    for th in range(2):
                        cs = slice(th * 512, (th + 1) * 512)
                        mma = [(wqv[:, kc, hl * 96:(hl + 1) * 96], cqn[kc][:, cs], kc == 0, kc == 2, [sA, cqn[kc]]) for kc in range(3)]
                        mmb = [(wqsv[:, kc, hl * 96:(hl + 1) * 96], cqn[kc][:, cs], kc == 0, kc == 2, [sA, cqn[kc]]) for kc in range(3)]
                        yield from nr_gen(qf, cs, mma, mmb, 512, rq, 0, 1024, cs)
                    for (k0, n) in KB:
                        ks = slice(k0, k0 + n)
                        mma = []
                        mmb = []
                        for kc in range(3):
                            ksz = 128 if kc < 2 else 32
                            mma.append((wkv[0:ksz, kc, hl * 96:(hl + 1) * 96], kvl[kc][0:ksz, ks], kc == 0, kc == 2, [sB, kvl[kc]]))
                            mmb.append((wksv[0:ksz, kc, hl * 96:(hl + 1) * 96], kvl[kc][0:ksz, ks], kc == 0, kc == 2, [sB, kvl[kc]]))
                        yield from nr_gen(kf, ks, mma, mmb, n, rk, 0, 1280, ks)

                def attention(hl, feeder):
                    h = hg * 4 + hl
                    qf = qf_r[h % 2]
                    kf = kf_r[h % 2]
                    a, b = hl // 2, hl % 2
                    for th in range(2):
                        cs = slice(th * 512, (th + 1) * 512)
                        po = nacc()
                        psts = {}

                        def issue_qk(kc):
                            pst = ATT[cnt['p'] % 3]
                            cnt['p'] += 1
                            mm(pst[:, :], kf[0:104, kc * 128:(kc + 1) * 128], qf[0:104, cs], True, True, [kf, qf], [pst])
                            psts[kc] = pst
                        issue_qk(0)
                        issue_qk(1)
                        for kc in range(10):
                            pst = psts[kc]
                            E = E_r[cnt['e'] % 4]
                            cnt['e'] += 1
                            act(E[:, :], pst[:, :], AF.Exp, [pst], [E], scale=float(96.0 ** -0.5))
                            if kc + 2 < 10:
                                issue_qk(kc + 2)
                            mm(po[:, :], vv[:, kc, a, b, :], E[:, :], kc == 0, kc == 9, [vaug, E], [po])
                            if feeder is not None:
                                next(feeder, None)
                        rc = rec[th]
                        if b == 0:
                            P.op('dve', lambda e, rc=rc, po=po: e.reciprocal(out=rc[0:64, :], in_=po[64:128, :]), [po], [rc])
                            tt('dve', oT[h // 2][0:64, cs], po[0:64, :], rc[0:64, :], ALU.mult, [po, rc], [oT[h // 2]])
                        else:
                            P.op('dve', lambda e, rc=rc, po=po: e.reciprocal(out=rc[64:128, :], in_=po[0:64, :]), [po], [rc])
                            tt('dve', oT[h // 2][64:128, cs], po[64:128, :], rc[64:128, :], ALU.mult, [po, rc], [oT[h // 2]])

                if hg == 0:
                    for (k0, n) in KB:
                        ks = slice(k0, k0 + n)
                        pa, pb, _ = NRB
                        for kc in range(3):
                            ksz = 128 if kc < 2 else 32
                            mm(pa[0:96, 0:n], wkv[0:ksz, kc, 0:96], kvl[kc][0:ksz, ks], kc == 0, kc == 2, [sB, kvl[kc]], [pa])
                        for kc in range(3):
                            ksz = 128 if kc < 2 else 32
                            mm(pb[0:96, 0:n], wksv[0:ksz, kc, 0:96], kvl[kc][0:ksz, ks], kc == 0, kc == 2, [sB, kvl[kc]], [pb])
                        t1 = ntmp()
                        t2 = ntmp()
                        tt('dve', t1[0:96, 0:n], pa[0:96, 0:n], rk[:, ks], ALU.mult, [pa, rk], [t1])
                        tt('dve', t2[0:96, 0:n], pb[0:96, 0:n], rk[:, 1280 + k0:1280 + k0 + n], ALU.mult, [pb, rk], [t2])
                        tt('pool', tpe[0:96, ks], t1[0:96, 0:n], t2[0:96, 0:n], ALU.add, [t1, t2], [tpe])
                cnt.setdefault('p', 0)
                for _ in head_feeder(0):
                    pass
                for hl in range(4):
                    nxt = head_feeder(hl + 1) if hl < 3 else None
                    attention(hl, nxt)
                    if nxt is not None:
                        for _ in nxt:
                            pass

            wo = D["mla_w_o"]
            for oc2 in range(2):
                s = wload([(wo[j, :, oc2 * 512:(oc2 + 1) * 512].rearrange("(kc p) n -> p kc n", p=128), 0, [128, 8, 512])])
                sv = s[:, :].rearrange("p (kc n) -> p kc n", kc=8)
                for o4 in range(4):
                    oc = oc2 * 4 + o4
                    for th in range(2):
                        cs = slice(th * 512, (th + 1) * 512)
                        p = nps()
                        for kc in range(8):
                            mm(p[:, :], sv[:, kc, o4 * 128:(o4 + 1) * 128], oT[kc][:, cs], kc == 0, kc == 7, [s, oT[kc]], [p])
                        stt(xT[oc][:, cs], p[:, :], mb[:, 16 + oc:17 + oc], xT[oc][:, cs], ALU.mult, ALU.add, [p, mb, xT[oc]], [xT[oc]])

        MIXERS['mla'] = mla_mixer

        def ssd_mixer(i, j, lst):
            mb = modb[i]
            W = D["ssd_w_in"]

            def sb(shape, dt, name):
                return P.sbuf(shape, dt, "ssd_" + name, stack=lst)
            tri_f = sb([128, 128], F32, "tri_f"); tri_b = sb([128, 128], F32, "tri_b")
            negm = sb([128, 256], BF16, "negm")
            rows3 = sb([128, 192], F32, "rows3")
            a_b = sb([128, 64], F32, "a_b"); dsum = sb([128, 32], F32, "dsum")
            oh2 = sb([128, 64], BF16, "oh2")
            sel = sb([128, 16 * 128], BF16, "sel")
            gn_b = sb([128, 512], F32, "gn_b")
            brow = sb([1, 640], BF16, "brow")
            ones1 = sb([1, 128], BF16, "ones1")
            dtc = [sb([128, 64], F32, f"dt{c}") for c in range(8)]
            acum = [sb([128, 64], F32, f"acum{c}") for c in range(8)]
            ea = [sb([128, 64], F32, f"ea{c}") for c in range(8)]
            cd = [sb([128, 64], F32, f"cd{c}") for c in range(8)]
            ddt = [sb([128, 64], F32, f"ddt{c}") for c in range(8)]
            AT2 = [sb([128, 128], BF16, f"AT2{c}") for c in range(8)]
            NA2 = [sb([128, 128], BF16, f"NA2{c}") for c in range(8)]
            a2s = sb([128, 128], F32, "a2s"); n2s = sb([128, 128], F32, "n2s")
            t64 = [sb([128, 64], F32, f"t64{k}") for k in range(3)]
            x_tok = [sb([128, 512], BF16, f"xtok{c}") for c in range(8)]
            Btok = [sb([128, 128], BF16, f"btok{c}") for c in range(8)]
            BT = sb([128, NT], BF16, "BT"); CT = sb([128, NT], BF16, "CT")
            pbuf = [sb([128, 4 * 260], BF16, f"pbuf{k}") for k in range(2)]
            Dw5 = [sb([128, 5 * 128], BF16, f"dw5{k}") for k in range(2)]
            H = [sb([128, 512], F32, f"H{d}") for d in range(2)]
            Hinb = [sb([128, 512], BF16, f"hinb{c}") for c in range(8)]
            Hinf = [sb([128, 512], BF16, f"hinf{c}") for c in range(8)]
            xdd_r = [sb([128, 512], BF16, f"xdd{k}") for k in range(2)]
            cbT = sb([128, 128], F32, "cbT")
            Eb = [sb([128, 512], BF16, f"Eb{k}") for k in range(2)]
            Mt = [[sb([128, 512], BF16, f"Mt{d}{q}") for q in range(2)] for d in range(2)]
            y1_ = [sb([128, 512], F32, f"y1_{k}") for k in range(2)]
            y2_ = [sb([128, 512], F32, f"y2_{k}") for k in range(2)]
            yd_ = [sb([128, 512], F32, f"yd_{k}") for k in range(2)]
            sz2_ = [sb([128, 512], F32, f"sz2_{k}") for k in range(2)]
            yn_ = [sb([128, 512], F32, f"yn_{k}") for k in range(2)]
            ssc_ = [sb([128, 2], F32, f"ssc_{k}") for k in range(2)]
            ynT = sb([128, 4 * NT], BF16, "ynT")
            hst = [sb([128, 512], F32, f"hst{k}") for k in range(2)]
            ynv = ynT[:, :].rearrange("p (k t) -> p k t", k=4)

            P.dma('sp', tri_f[:, :], D["tri"][0], writes=[tri_f], sembuf=tri_f)
            P.dma('sp', tri_b[:, :], D["tri"][1], writes=[tri_b], sembuf=tri_b)
            P.dma('pool', negm[:, :].rearrange("p (d i) -> p d i", d=2), D["negmask"].rearrange("d p i -> p d i"), writes=[negm], sembuf=negm)
            for r in range(3):
                P.dma('sp', rows3[:, r * 64:(r + 1) * 64], D["ssd_rows"][r:r + 1, :].partition_broadcast(128), writes=[rows3], sembuf=rows3)
            act(a_b[:, :], rows3[:, 64:128], AF.Exp, [rows3], [a_b])
            ts('dve', a_b[:, :], a_b[:, :], -1.0, None, ALU.mult, None, [a_b], [a_b])
            tt('dve', dsum[:, :], rows3[:, 128:160], rows3[:, 160:192], ALU.add, [rows3], [dsum])
            tt('dve', oh2[:, :], ident_f[:, 0:64], ident_f[:, 64:128], ALU.add, [ident_f], [oh2])
            memset('pool', ones1[:, :], 1.0, [ones1])
            for k in range(2):
                memset('pool', pbuf[k][:, :], 0.0, [pbuf[k]])
            negv = negm[:, :].rearrange("p (d i) -> p d i", d=2)
            negm4 = sb([128, 2 * 512], BF16, "negm4")
            for d in range(2):
                cp('dve', negm4[:, d * 512:(d + 1) * 512].rearrange("p (h i) -> p h i", h=4), negv[:, d, :].unsqueeze(1).to_broadcast([128, 4, 128]), [negm], [negm4])

            sdt = wload([(W[:, 5120:5184].rearrange("(kc p) n -> p kc n", p=128), 0, [128, 8, 64])])
            dtv = sdt[:, 0:512].rearrange("p (kc n) -> p kc n", kc=8)
            for c in range(8):
                cc = slice(c * 128, (c + 1) * 128)
                pdt = nps()
                for kc in range(8):
                    mm(pdt[:, 0:64], hT[kc][:, cc], dtv[:, kc, :], kc == 0, kc == 7, [hT[kc], sdt], [pdt])
                ta, tl, td = t64
                tt('dve', ta[:, :], pdt[:, 0:64], rows3[:, 0:64], ALU.add, [pdt, rows3], [ta])
                act(ta[:, :], ta[:, :], AF.Exp, [ta], [ta])
                act(dtc[c][:, :], ta[:, :], AF.Ln, [ta], [dtc[c]], bias=1.0)
                act(tl[:, :], dtc[c][:, :], AF.Ln, [dtc[c]], [tl])
                tt('dve', td[:, :], dtc[c][:, :], a_b[:, :], ALU.mult, [dtc[c], a_b], [td])
                pc = nps()
                mm(pc[:, 0:32], tri_f[:, :], td[:, 0:32], True, True, [tri_f, td], [pc])
                mm(pc[:, 32:64], tri_b[:, :], td[:, 32:64], True, True, [tri_b, td], [pc])
                mm(pc[:, 64:128], ones_f[:, :], td[:, 0:64], True, True, [ones_f, td], [pc])
                cp('dve', acum[c][:, :], pc[:, 0:64], [pc], [acum[c]])
                act(ea[c][:, :], pc[:, 0:64], AF.Exp, [pc], [ea[c]])
                act(cd[c][:, :], pc[:, 64:128], AF.Exp, [pc], [cd[c]])
                tt('dve', ta[:, :], pc[:, 64:128], acum[c][:, :], ALU.subtract, [pc, acum[c]], [ta])
                act(ta[:, :], ta[:, :], AF.Exp, [ta], [ta])
                tt('dve', ddt[c][:, :], dtc[c][:, :], ta[:, :], ALU.mult, [dtc[c], ta], [ddt[c]])
                tt('dve', tl[:, :], tl[:, :], acum[c][:, :], ALU.subtract, [tl, acum[c]], [tl])
                for hf in range(2):
                    cp('dve', a2s[:, hf * 64:(hf + 1) * 64], acum[c][:, :], [acum[c]], [a2s])
                    cp('pool', n2s[:, hf * 64:(hf + 1) * 64], tl[:, :], [tl], [n2s])
                pT = nps()
                tr(pT[:, 0:128], a2s[:, :], ident_f[:, :], [a2s, ident_f], [pT])
                tr(pT[:, 128:256], n2s[:, :], ident_f[:, :], [n2s, ident_f], [pT])
                for (dst, off) in ((AT2[c], 0), (NA2[c], 128)):
                    cp('act', dst[:, :], pT[:, off:off + 128], [pT], [dst])
                    tt('dve', dst[64:128, :], pT[64:128, off:off + 128], dst[64:128, :], ALU.subtract, [pT, dst], [dst])

            def state_out(Hd, seq, d, g):
                pt = nps()
                for blk in range(4):
                    tr(pt[:, blk * 128:(blk + 1) * 128], Hd[:, blk * 128:(blk + 1) * 128], ident_f[:, :], [Hd, ident_f], [pt])
                hs = hst[(seq + d) % 2]
                cp(alt('dve', 'act'), hs[:, :], pt[:, :], [pt], [hs])
                P.dma('sp', D["ossm"][seq, d, 8 * g:8 * g + 8].rearrange("(blk hh) p n -> (hh p) blk n", hh=2),
                      hs[:, :].rearrange("p (blk n) -> p blk n", blk=4), reads=[hs], sembuf=hs, is_output=True)

            def state_step(Hd, c, d, g):
                xd = xdd_r[d]
                tt('dve', xd[:, :].rearrange("p (h q) -> p h q", h=8), x_tok[c][:, :].rearrange("p (h q) -> p h q", h=8),
                   ddt[c][:, d * 32 + 8 * g:d * 32 + 8 * g + 8].unsqueeze(2).to_broadcast([128, 8, 64]), ALU.mult, [x_tok[c], ddt[c]], [xd])
                ps = nps()
                mm(ps[:, :], Btok[c][:, :], xd[:, :], True, True, [Btok[c], xd], [ps])
                tt('dve', Hd[:, :].rearrange("p (h q) -> p h q", h=8), Hd[:, :].rearrange("p (h q) -> p h q", h=8),
                   cd[c][:, d * 32 + 8 * g:d * 32 + 8 * g + 8].unsqueeze(2).to_broadcast([128, 8, 64]), ALU.mult, [Hd, cd[c]], [Hd])
                tt('dve', Hd[:, :], Hd[:, :], ps[:, :], ALU.add, [Hd, ps], [Hd])

            for g in range(cfg.get('ssd_groups', 4)):
                P.dma('sp', gn_b[:, :], D["gnorm_row"][0:1, g * 512:(g + 1) * 512].partition_broadcast(128), writes=[gn_b], sembuf=gn_b)
                P.dma('pool', brow[:, 0:512], D["bconv_row"][0:1, g * 512:(g + 1) * 512], writes=[brow], sembuf=brow)
                P.dma('pool', brow[:, 512:640], D["bconv_row"][0:1, 2048 + g * 128:2048 + (g + 1) * 128], writes=[brow], sembuf=brow)
                selv = sel[:, :].rearrange("p (s m) -> p s m", s=16)
                for d in range(2):
                    cp('dve', selv[:, d * 8:(d + 1) * 8, :], oh2[:, d * 32 + 8 * g:d * 32 + 8 * g + 8].unsqueeze(2).to_broadcast([128, 8, 128]), [oh2], [sel])
                sx = wload([(W[:, 2048 + g * 512:2048 + (g + 1) * 512].rearrange("(kc p) n -> p kc n", p=128), 0, [128, 8, 512])])
                sxv = sx[:, :].rearrange("p (kc n) -> p kc n", kc=8)
                sbc = wload([(W[:, 4096 + g * 128:4096 + (g + 1) * 128].rearrange("(kc p) n -> p kc n", p=128), 0, [128, 8, 128]),
                             (W[:, 4608 + g * 128:4608 + (g + 1) * 128].rearrange("(kc p) n -> p kc n", p=128), 1024, [128, 8, 128])])
                sbv = sbc[:, 0:1024].rearrange("p (kc n) -> p kc n", kc=8)
                scv = sbc[:, 1024:2048].rearrange("p (kc n) -> p kc n", kc=8)
                def qinfo(q):
                    if q < 4:
                        return sxv, sx, q * 128, 4 * g + q
                    elif q == 4:
                        return sbv, sbc, 0, 16 + g
                    return scv, sbc, 0, 20 + g

                def inproj(q):
                    wv_, ws_, wc0, ccg = qinfo(q)
                    pb_ = pbuf[q % 2]
                    pb = pb_[:, :].rearrange("p (s t) -> p s t", s=4)
                    for th in range(2):
                        cs = slice(th * 512, (th + 1) * 512)
                        p = nps()
                        for kc in range(8):
                            mm(p[:, :], wv_[:, kc, wc0:wc0 + 128], hT[kc][:, cs], kc == 0, kc == 7, [ws_, hT[kc]], [p])
                        cp('act', pb[:, 2 * th:2 * th + 2, 2:258], p[:, :].rearrange("p (s t) -> p s t", s=2), [p], [pb_])
                    ts('dve', pb[:, 1:4, 0:2], pb[:, 0:3, 256:258], flag[:, 0:1], None, ALU.mult, None, [pb_, flag], [pb_])
                    ts('dve', pb[:, 0:3, 258:260], pb[:, 1:4, 2:4], flag[:, 0:1], None, ALU.mult, None, [pb_, flag], [pb_])
                    dw_ = Dw5[q % 2]
                    dwv = dw_[:, :].rearrange("p (w n) -> p w n", w=5)
                    for w in range(5):
                        ts('dve', dwv[:, w, :], ident_f[:, :], vc("wconv", w * 24 + ccg), None, ALU.mult, None, [ident_f, vcols], [dw_])

                def sconv(q):
                    wv_, ws_, wc0, ccg = qinfo(q)
                    pb_ = pbuf[q % 2]
                    pb = pb_[:, :].rearrange("p (s t) -> p s t", s=4)
                    dw_ = Dw5[q % 2]
                    dwv = dw_[:, :].rearrange("p (w n) -> p w n", w=5)
                    if q < 5:
                        for t in range(8):
                            seg, off = t // 2, (t % 2) * 128
                            p = nps()
                            for w in range(5):
                                mm(p[:, 0:128], pb[:, seg, off + w:off + w + 128], dwv[:, w, :], w == 0, False, [pb_, dw_], [p])
                            bc0 = q * 128 if q < 4 else 512
                            mm(p[:, 0:128], ones1[0:1, :], brow[0:1, bc0:bc0 + 128], False, True, [ones1, brow], [p])
                            if q < 4:
                                act(x_tok[t][:, q * 128:(q + 1) * 128], p[:, 0:128], AF.Silu, [p], [x_tok[t]])
                            else:
                                act(Btok[t][:, :], p[:, 0:128], AF.Silu, [p], [Btok[t]])
                    if q >= 4:
                        dstT = BT if q == 4 else CT
                        for th in range(2):
                            cs = slice(th * 512, (th + 1) * 512)
                            p = nps()
                            for w in range(5):
                                mm(p[:, :], dwv[:, w, :], pb[:, 2 * th:2 * th + 2, w:w + 256], w == 0, w == 4, [dw_, pb_], [p])
                            act(dstT[:, cs], p[:, :], AF.Silu, [p, vcols], [dstT], bias=vc("bconv", ccg))

                SPH = cfg.get('ssd_phase', 9)
                inproj(0)
                for q in range(6):
                    if q + 1 < 6:
                        inproj(q + 1)
                    sconv(q)
                sz_ = wload([(W[:, g * 512:(g + 1) * 512].rearrange("(kc p) n -> p kc n", p=128), 0, [128, 8, 512])])
                szv = sz_[:, :].rearrange("p (kc n) -> p kc n", kc=8)
                for d in range(2):
                    hs = hst[d]
                    P.dma('sp', hs[:, :].rearrange("p (blk n) -> p blk n", blk=4),
                          D["h0"][d, 8 * g:8 * g + 8].rearrange("(blk hh) p n -> (hh p) blk n", hh=2), writes=[hs], sembuf=hs)
                    pt = nps()
                    for blk in range(4):
                        tr(pt[:, blk * 128:(blk + 1) * 128], hs[:, blk * 128:(blk + 1) * 128], ident_f[:, :], [hs, ident_f], [pt])
                    cp('dve', H[d][:, :], pt[:, :], [pt], [H[d]])
                for k8 in range(8 if SPH >= 2 else 0):
                    c = 7 - k8
                    if c in (5, 3, 1):
                        ts('dve', H[1][:, :], H[1][:, :], flag[:, 0:1], None, ALU.mult, None, [H[1], flag], [H[1]])
                    cp('act', Hinb[c][:, :], H[1][:, :], [H[1]], [Hinb[c]])
                    state_step(H[1], c, 1, g)
                    if c in (6, 4, 2, 0):
                        state_out(H[1], c // 2, 1, g)
                    c = k8
                    if c in (2, 4, 6):
                        ts('dve', H[0][:, :], H[0][:, :], flag[:, 0:1], None, ALU.mult, None, [H[0], flag], [H[0]])
                    cp('act', Hinf[c][:, :], H[0][:, :], [H[0]], [Hinf[c]])
                    state_step(H[0], c, 0, g)
                    if c in (1, 3, 5, 7):
                        state_out(H[0], c // 2, 0, g)
                v8 = lambda ap: ap.rearrange("p (h q) -> p h q", h=8)

                def head(c):
                    cc = slice(c * 128, (c + 1) * 128)
                    k = c % 2
                    y1, y2, yd, sz2 = y1_[k], y2_[k], yd_[k], sz2_[k]
                    pcb = nps()
                    mm(pcb[:, 0:128], BT[:, cc], CT[:, cc], True, True, [BT, CT], [pcb])
                    cp('act', cbT[:, :], pcb[:, 0:128], [pcb], [cbT])
                    psegs = {}
                    for d in range(2):
                        for quad in range(2):
                            pseg = nps()
                            psegs[(d, quad)] = pseg
                            si0 = d * 8 + quad * 4
                            mm(pseg[:, :], NA2[c][:, :], sel[:, si0 * 128:(si0 + 4) * 128], True, False, [sel, NA2[c]], [pseg])
                            mm(pseg[:, :], ident_b[:, :], negm4[:, d * 512:(d + 1) * 512], False, False, [ident_b, negm4], [pseg])
                            for hq in range(4):
                                si = si0 + hq
                                o = pseg[:, hq * 128:(hq + 1) * 128]
                                mm(o, selv[:, si, :], AT2[c][:, :], False, hq == 3, [sel, AT2[c]], [pseg])
                            E = Eb[(d * 2 + quad) % len(Eb)]
                            act(E[:, :], pseg[:, :], AF.Exp, [pseg], [E])
                            tt('dve', Mt[d][quad][:, :].rearrange("p (h i) -> p h i", h=4), E[:, :].rearrange("p (h i) -> p h i", h=4),
                               cbT[:, :].unsqueeze(1).to_broadcast([128, 4, 128]), ALU.mult, [E, cbT], [Mt[d][quad]])
                    pyf = nps()
                    mm(pyf[:, :], CT[:, cc], Hinf[c][:, :], True, True, [CT, Hinf[c]], [pyf])
                    pyb = nps()
                    mm(pyb[:, :], CT[:, cc], Hinb[c][:, :], True, True, [CT, Hinb[c]], [pyb])
                    pz = nacc()
                    for kc in range(8):
                        mm(pz[:, :], hT[kc][:, cc], szv[:, kc, :], kc == 0, kc == 7, [hT[kc], sz_], [pz])
                    pyd = nacc()
                    for hl in range(8):
                        for d in range(2):
                            mm(pyd[:, hl * 64:(hl + 1) * 64], Mt[d][hl // 4][:, (hl % 4) * 128:(hl % 4 + 1) * 128], x_tok[c][:, hl * 64:(hl + 1) * 64],
                               d == 0, d == 1, [Mt[d][hl // 4], x_tok[c]], [pyd])
                    tt('dve', v8(y1[:, :]), v8(pyf[:, :]), ea[c][:, 8 * g:8 * g + 8].unsqueeze(2).to_broadcast([128, 8, 64]), ALU.mult, [pyf, ea[c]], [y1])
                    tt('dve', v8(y2[:, :]), v8(pyb[:, :]), ea[c][:, 32 + 8 * g:32 + 8 * g + 8].unsqueeze(2).to_broadcast([128, 8, 64]), ALU.mult, [pyb, ea[c]], [y2])
                    cp('dve', yd[:, :], pyd[:, :], [pyd], [yd])
                    act(sz2[:, :], pz[:, :], AF.Silu, [pz], [sz2])

                def tail(c):
                    cc = slice(c * 128, (c + 1) * 128)
                    k = c % 2
                    y1, y2, yd, sz2, yn, ssc = y1_[k], y2_[k], yd_[k], sz2_[k], yn_[k], ssc_[k]
                    tt('pool', y1[:, :], y1[:, :], y2[:, :], ALU.add, [y1, y2], [y1])
                    tt('pool', v8(y2[:, :]), v8(x_tok[c][:, :]), dsum[:, 8 * g:8 * g + 8].unsqueeze(2).to_broadcast([128, 8, 64]), ALU.mult, [x_tok[c], dsum], [y2])
                    tt('pool', y1[:, :], y1[:, :], y2[:, :], ALU.add, [y1, y2], [y1])
                    tt('pool', y1[:, :], y1[:, :], yd[:, :], ALU.add, [y1, yd], [y1])
                    tt('pool', y1[:, :], y1[:, :], sz2[:, :], ALU.mult, [y1, sz2], [y1])
                    act(y2[:, :], y1[:, :], AF.Square, [y1], [y2, ssc], accum=ssc[:, 0:1])
                    act(ssc[:, 1:2], ssc[:, 0:1], AF.Ln, [ssc], [ssc], bias=EPS, scale=1.0 / 512)
                    act(ssc[:, 1:2], ssc[:, 1:2], AF.Exp, [ssc], [ssc], scale=-0.5)
                    stt(yn[:, :], y1[:, :], ssc[:, 1:2], gn_b[:, :], ALU.mult, ALU.mult, [y1, ssc, gn_b], [yn])
                    pt = nps()
                    for blk in range(4):
                        tr(pt[:, blk * 128:(blk + 1) * 128], yn[:, blk * 128:(blk + 1) * 128], ident_f[:, :], [yn, ident_f], [pt])
                    cp(alt('dve', 'act'), ynv[:, :, cc], pt[:, :].rearrange("p (k t) -> p k t", k=4), [pt], [ynT])

                if SPH >= 3:
                    head(0)
                for c in range(8 if SPH >= 3 else 0):
                    if c + 1 < 8:
                        head(c + 1)
                    tail(c)
                wo = D["ssd_w_out"]
                for oc2 in range(2 if SPH >= 4 else 0):
                    s = wload([(wo[g * 512:(g + 1) * 512, oc2 * 512:(oc2 + 1) * 512].rearrange("(kc p) n -> p kc n", p=128), 0, [128, 4, 512])])
                    sv = s[:, 0:2048].rearrange("p (kc n) -> p kc n", kc=4)
                    for o4 in range(4):
                        oc = oc2 * 4 + o4
                        for th in range(2):
                            cs = slice(th * 512, (th + 1) * 512)
                            p = nps()
                            for kc in range(4):
                                mm(p[:, :], sv[:, kc, o4 * 128:(o4 + 1) * 128], ynv[:, kc, cs], kc == 0, kc == 3, [s, ynT], [p])
                            stt(xT[oc][:, cs], p[:, :], mb[:, 16 + oc:17 + oc], xT[oc][:, cs], ALU.mult, ALU.add, [p, mb, xT[oc]], [xT[oc]])

        MIXERS['ssd'] = ssd_mixer


        if cfg.get('adaln', True):
            adaln(0)
        for i in range(nlayers):
            kind, j = i % 3, i // 3
            with ExitStack() as lst:
                if kind == 0 and en_mla:
                    norm_mod(i, 1)
                    MIXERS['mla'](i, j, lst)
                elif kind == 1 and en_conv:
                    norm_mod(i, 1)
                    MIXERS['conv'](i, j, lst)
                elif kind == 2 and en_ssd:
                    norm_mod(i, 1)
                    MIXERS['ssd'](i, j, lst)
                P.barrier()
            pump = adaln_gen(i + 1) if (i + 1 < nlayers and cfg.get('adaln', True)) else None
            if en_ffn:
                with ExitStack() as lst:
                    ffn(i, lst, pump)
                    if pump is not None:
                        for _ in pump:
                            pass
                    P.barrier()
            elif pump is not None:
                for _ in pump:
                    pass

        P.barrier()
        xstage = [P.sbuf([128, 1024], F32, f"xstageo{i}") for i in range(2)]
        for t in range(8):
            stg = xstage[t % 2]
            for half in range(2):
                p = nps()
                for q in range(4):
                    fc = half * 4 + q
                    tr(p[:, q * 128:(q + 1) * 128], xT[fc][:, t * 128:(t + 1) * 128], ident_f[:, :], [xT[fc], ident_f], [p])
                cp(alt('dve', 'act'), stg[:, half * 512:(half + 1) * 512], p[:, :], [p], [stg])
            P.dma('sp', D["y"][t * 128:(t + 1) * 128, :], stg[:, :], reads=[stg], sembuf=stg, is_output=True)
        P.finish()
        P.emit()
    nc._used_inputs = USED_INPUTS
    return nc, USED_INPUTS


def _partner(d):
    if d < 64:
        return d
    e = d - 64
    blk, r = e // 16, e % 16
    return 64 + blk * 16 + (r + 8) % 16


def _rope_tables(is_sample):
    T = 1024
    cosq = np.ones((96, T), np.float32)
    sinq = np.zeros((96, T), np.float32)
    if is_sample:
        t = np.arange(T)
        row = (t // 64).astype(np.float32)
        col = (t % 64).astype(np.float32)
        inv = (10000.0 ** (-np.arange(0, 16, 2, dtype=np.float32) / 16)).astype(np.float32)
        ang_r = row[None, :] * inv[:, None]
        ang_c = col[None, :] * inv[:, None]
        for blk, ang in ((0, ang_r), (1, ang_c)):
            c = np.cos(ang).astype(np.float32)
            s = np.sin(ang).astype(np.float32)
            b = 64 + blk * 16
            cosq[b:b + 8] = c
            cosq[b + 8:b + 16] = c
            sinq[b:b + 8] = -s
            sinq[b + 8:b + 16] = s
    ropeq = np.stack([cosq, sinq])
    cosk = np.ones((96, 1280), np.float32)
    sink = np.zeros((96, 1280), np.float32)
    cosk[:, 256:] = cosq
    sink[:, 256:] = sinq
    ropek = np.stack([cosk, sink])
    return ropeq, ropek


_NC_CACHE = {}


def _get_nc(cfg_key, cfg):
    if cfg_key not in _NC_CACHE:
        _NC_CACHE[cfg_key] = build(cfg)
    return _NC_CACHE[cfg_key]


def kernel(_cfg=None, **inp):
    f32 = np.float32
    g = {k: np.asarray(v) for k, v in inp.items()}
    cfg = _cfg or {}
    perm = np.array([h * 96 + _partner(d) for h in range(16) for d in range(96)])
    w_uq = np.ascontiguousarray(g["mla_w_uq"], f32)
    w_uq_sw = np.ascontiguousarray(w_uq[:, :, perm])
    wk = np.zeros((2, 384, 1536), f32)
    wk_sw = np.zeros((2, 384, 1536), f32)
    ukv = g["mla_w_ukv"].reshape(2, 256, 16, 128)
    for h in range(16):
        wk[:, 0:256, h * 96:h * 96 + 64] = ukv[:, :, h, 0:64]
        wk_sw[:, 0:256, h * 96:h * 96 + 64] = ukv[:, :, h, 0:64]
        for e in range(32):
            wk[:, 256 + e, h * 96 + 64 + e] = 1.0
            wk_sw[:, 256 + (_partner(64 + e) - 64), h * 96 + 64 + e] = 1.0
    w_ukv_v = np.ascontiguousarray(ukv[:, :, :, 64:128].reshape(2, 256, 1024))
    pq = np.array([_partner(d) for d in range(96)])

    def pad128(v):
        o = np.zeros((1, 128), f32)
        o[0, :v.shape[0]] = v
        return o

    tri = np.zeros((3, 128, 128), f32)
    k_ = np.arange(128)
    tri[0] = (k_[:, None] <= k_[None, :])
    tri[1] = (k_[:, None] >= k_[None, :])
    negmask = np.zeros((2, 128, 128), f32)
    negmask[0] = np.where(k_[None, :] >= k_[:, None], 0.0, NEG)
    negmask[1] = np.where(k_[None, :] <= k_[:, None], 0.0, NEG)
    ssd_rows = np.zeros((4, 64), f32)
    ssd_rows[0] = g["ssd_dt_bias"][0].reshape(64)
    ssd_rows[1] = g["ssd_a_log"][0].reshape(64)
    ssd_rows[2] = g["ssd_d"][0].reshape(64)

    shared = {
        "w_ada": g["w_ada"], "ffn_w_in": g["ffn_w_in"], "ffn_w_out": g["ffn_w_out"],
        "mla_w_dq": g["mla_w_dq"], "w_uq": w_uq, "w_uq_sw": w_uq_sw, "mla_w_dkv": g["mla_w_dkv"],
        "wk": wk, "wk_sw": wk_sw, "w_ukv_v": w_ukv_v, "mla_w_o": g["mla_w_o"],
        "cv_w_pw1": g["cv_w_pw1"][0], "cv_w_pw2": g["cv_w_pw2"][0],
        "ssd_w_in": g["ssd_w_in"][0], "ssd_w_out": g["ssd_w_out"][0],
        "ssd_rows": ssd_rows, "bconv_row": g["ssd_b_conv"][0].reshape(1, 3072), "gnorm_row": g["ssd_g_norm"][0].reshape(1, 2048),
        "tri": tri, "negmask": negmask,
    }
    shared = {k: np.ascontiguousarray(v, f32) for k, v in shared.items()}
    in_maps = []
    for c in range(8):
        is_sample = c >= 4
        m = dict(shared)
        if is_sample:
            b = c - 4
            m["x"] = np.ascontiguousarray(g["x_sample"][b], f32)
            cond = g["c"][b]
            m["cache_ckv"] = np.ascontiguousarray(g["cache_ckv"][b], f32)
            m["cache_kpe"] = np.ascontiguousarray(g["cache_kpe"][b], f32)
            m["h0"] = np.ascontiguousarray(g["state_ssm"][b, 0], f32)
            maskb = np.zeros((128, 40), f32)
            m["flag"] = np.ones((128, 1), f32)
        else:
            m["x"] = np.ascontiguousarray(g["x_prompt"][4 * c:4 * c + 4].reshape(1024, 1024), f32)
            cond = g["c_ctx"]
            m["cache_ckv"] = np.zeros((2, 256, 256), f32)
            m["cache_kpe"] = np.zeros((2, 256, 32), f32)
            m["h0"] = np.zeros((2, 32, 64, 128), f32)
            maskb = np.full((128, 40), NEG, f32)
            for kc in range(2, 10):
                maskb[:, kc * 4 + (kc - 2) // 2] = 0.0
            m["flag"] = np.zeros((128, 1), f32)
        m["maskb"] = maskb
        mq = np.zeros((8, 1024), f32)
        mk = np.zeros((8, 1280), f32)
        if not is_sample:
            for jj in range(4):
                mq[jj, jj * 256:(jj + 1) * 256] = 1.0
                mk[jj, :] = NEG
                mk[jj, 256 + jj * 256:256 + (jj + 1) * 256] = 0.0
        m["maskq"], m["maskk"] = mq, mk
        rq, rk = _rope_tables(is_sample)
        m["ropeq"], m["ropek"] = rq, rk
        rows = []
        for i in range(4):
            rows += [g["g_norm1"][i].reshape(8, 128), g["g_norm2"][i].reshape(8, 128), g["b_ada"][i].reshape(48, 128)]
        rows += [np.asarray(cond).reshape(8, 128)]
        for j in range(2):
            rows += [g["mla_g_q"][j].reshape(3, 128), g["mla_g_kv"][j].reshape(2, 128),
                     pad128(g["mla_g_qn"][j]), pad128(g["mla_g_qn"][j][pq]),
                     pad128(g["mla_g_kn"][j]), pad128(g["mla_g_kn"][j][pq])]
        rows += [g["cv_b_pw1"][0].reshape(16, 128), g["cv_w_dw"][0].reshape(248, 128), g["cv_b_dw"][0].reshape(8, 128),
                 g["cv_g_ln"][0].reshape(8, 128), g["cv_b_ln"][0].reshape(8, 128), g["cv_b_pw2"][0].reshape(8, 128)]
        rows += [g["ssd_w_conv"][0].reshape(120, 128), g["ssd_b_conv"][0].reshape(24, 128)]
        v = np.concatenate([np.asarray(r, f32) for r in rows], axis=0)
        vecs = np.zeros((NVEC, 128), f32)
        vecs[:v.shape[0]] = v
        m["vecs"] = vecs
        in_maps.append(m)

    nc, used = _get_nc(str(sorted(cfg.items())), cfg)
    in_maps = [{k: m[k] for k in used} for m in in_maps]
    res = run_bass_kernel_spmd(nc, in_maps, core_ids=list(range(8)))
    R = res.results
    y_prompt = np.stack([R[c]["y"].reshape(4, 256, 1024) for c in range(4)]).reshape(16, 256, 1024)
    y_sample = np.stack([R[4 + b]["y"] for b in range(4)])
    _z = {"ockv": np.zeros((2, 1024, 256), f32), "okpe": np.zeros((2, 1024, 32), f32), "ossm": np.zeros((4, 2, 32, 64, 128), f32)}
    R = [{k: (r[k] if k in r else _z[k]) for k in ("y", "ockv", "okpe", "ossm")} for r in R]
    new_ckv = np.stack([R[c]["ockv"].reshape(2, 4, 256, 256).transpose(1, 0, 2, 3) for c in range(4)]).reshape(16, 2, 256, 256)
    new_kpe = np.stack([R[c]["okpe"].reshape(2, 4, 256, 32).transpose(1, 0, 2, 3) for c in range(4)]).reshape(16, 2, 256, 32)
    new_ssm = np.stack([R[c]["ossm"] for c in range(4)]).reshape(16, 1, 2, 32, 64, 128)
    outs = (y_prompt.astype(f32), y_sample.astype(f32), np.ascontiguousarray(new_ckv, f32),
            np.ascontiguousarray(new_kpe, f32), np.ascontiguousarray(new_ssm, f32))
    return outs
```
